# Optimizing a Trainium2 kernel written in Bass

```python
import jax, jax.numpy as jnp
from jax import lax
import numpy as np

D_MODEL = 1024
BATCH = 4
SEQ = 4096
DEPTH = 4
DEC_BATCH = 128
DEC_SEQ = 4
PAST_LEN = 8192
PAGE_SIZE = 128

D_MIX = D_MODEL
D_CONV = D_MIX // 4
D_POOL = D_MIX // 4
D_ATTN = D_MIX - D_CONV - D_POOL
HEAD_DIM = 64
N_HEADS = D_ATTN // HEAD_DIM
N_KV_HEADS = 2
GROUP = N_HEADS // N_KV_HEADS
D_KV = N_KV_HEADS * HEAD_DIM
WINDOW = 128
BLOCK = 128
CONV_WIDTH = 31
CONV_BUF = CONV_WIDTH - 1
POOL_WINDOWS = (2, 4, 8, 16)
N_POOL_GROUPS = len(POOL_WINDOWS)
POOL_GROUP_DIM = D_POOL // N_POOL_GROUPS
POOL_BUF = max(POOL_WINDOWS) - 1
ROPE_THETA = 10000.0
LN_EPS = 1e-5
ALPHA = (2.0 * DEPTH) ** 0.25
BETA = (8.0 * DEPTH) ** -0.25
MASK_VALUE = -1e30
IN_SIZES = (D_CONV, D_CONV, D_CONV, D_POOL, D_POOL, D_ATTN, D_KV, D_KV, D_ATTN)
D_IN = sum(IN_SIZES)
IN_SPLITS = tuple(int(s) for s in np.cumsum(IN_SIZES)[:-1])

kernel_name = "hymba_conv_pool_swa_deepnorm_step"


def _layernorm(x, g=None, b=None):
    xf = x.astype(jnp.float32)
    mu = jnp.mean(xf, axis=-1, keepdims=True)
    var = jnp.mean(jnp.square(xf - mu), axis=-1, keepdims=True)
    y = (xf - mu) * lax.rsqrt(var + LN_EPS)
    if g is not None:
        y = y * g.astype(jnp.float32) + b.astype(jnp.float32)
    return y.astype(x.dtype)


def _rope(x, pos):
    half = HEAD_DIM // 2
    inv_freq = ROPE_THETA ** (-jnp.arange(half, dtype=jnp.float32) * (2.0 / HEAD_DIM))
    ang = pos.astype(jnp.float32)[:, None] * inv_freq[None, :]
    cos = jnp.concatenate([jnp.cos(ang), jnp.cos(ang)], -1)[None, :, None, :]
    sin = jnp.concatenate([jnp.sin(ang), jnp.sin(ang)], -1)[None, :, None, :]
    xf = x.astype(jnp.float32)
    rot = jnp.concatenate([-xf[..., half:], xf[..., :half]], -1)
    return (xf * cos + rot * sin).astype(x.dtype)


def _sink_attention(q, k, v, q_pos, k_pos, sinks):
    s = jnp.einsum('...qhgd,...khd->...hgqk', q, k).astype(jnp.float32) * (HEAD_DIM ** -0.5)
    kp = k_pos[..., None, :]
    qp = q_pos[..., :, None]
    mask = (kp <= qp) & (kp > qp - WINDOW) & (kp >= 0)
    s = jnp.where(mask[..., None, None, :, :], s, MASK_VALUE)
    sink = sinks.astype(jnp.float32).reshape(N_KV_HEADS, GROUP, 1)
    m = jnp.maximum(jnp.max(s, axis=-1), sink)
    p = jnp.exp(s - m[..., None])
    denom = jnp.sum(p, axis=-1) + jnp.exp(sink - m)
    probs = (p / denom[..., None]).astype(v.dtype)
    return jnp.einsum('...hgqk,...khd->...qhgd', probs, v)


def _causal_dwconv(buf, a, w, b):
    xcat = jnp.concatenate([buf, a], axis=1)
    out = lax.conv_general_dilated(
        xcat, w.astype(xcat.dtype)[:, None, :], window_strides=(1,), padding='VALID',
        dimension_numbers=('NWC', 'WIO', 'NWC'), feature_group_count=D_CONV)
    return out + b.astype(out.dtype), xcat[:, -CONV_BUF:]


def _pool_mix(buf, v, pos0, pool_w, pool_scale):
    B, T, _ = v.shape
    xcat = jnp.concatenate([buf, v], axis=1)
    cs = jnp.cumsum(xcat.astype(jnp.float32), axis=1)
    cs0 = jnp.concatenate([jnp.zeros((B, 1, D_POOL), jnp.float32), cs], axis=1)
    pos = (pos0 + jnp.arange(T)).astype(jnp.float32)
    end = cs0[:, POOL_BUF + 1:POOL_BUF + 1 + T]
    means = []
    for gi, w in enumerate(POOL_WINDOWS):
        sl = slice(gi * POOL_GROUP_DIM, (gi + 1) * POOL_GROUP_DIM)
        start = cs0[:, POOL_BUF + 1 - w:POOL_BUF + 1 - w + T, sl]
        cnt = jnp.minimum(jnp.float32(w), pos + 1.0)[None, :, None]
        means.append((end[..., sl] - start) / cnt)
    pooled = (jnp.concatenate(means, -1) - v.astype(jnp.float32)).astype(v.dtype)
    y = jnp.einsum('btgc,gcd->btgd', pooled.reshape(B, T, N_POOL_GROUPS, POOL_GROUP_DIM), pool_w)
    return y.reshape(B, T, D_POOL) * pool_scale, xcat[:, -POOL_BUF:]


def _layer(x, c, conv_buf, pool_buf, k_buf, v_buf, pos0,
           w_in, w_out, conv_w, conv_b, cnorm_g, cnorm_b, pool_w, pool_scale,
           sinks, w_mod, b_mod, ln_g, ln_b):
    B, T, _ = x.shape
    mod = jax.nn.silu(c) @ w_mod + b_mod
    shift, scale, gate = jnp.split(mod, 3, axis=-1)
    h = _layernorm(x) * (1.0 + scale[:, None]) + shift[:, None]
    proj = h @ w_in
    u, g, zc, pv, zp, q, k, v, za = jnp.split(proj, IN_SPLITS, axis=-1)

    a = u * jax.nn.sigmoid(g)
    conv, new_conv = _causal_dwconv(conv_buf, a, conv_w, conv_b)
    ya = jax.nn.silu(_layernorm(conv, cnorm_g, cnorm_b)) * jax.nn.silu(zc)

    pooled, new_pool = _pool_mix(pool_buf, pv, pos0, pool_w, pool_scale)
    yb = pooled * jax.nn.silu(zp)

    pos = pos0 + jnp.arange(T, dtype=jnp.int32)
    q = _rope(q.reshape(B, T, N_HEADS, HEAD_DIM), pos).reshape(B, T, N_KV_HEADS, GROUP, HEAD_DIM)
    k = _rope(k.reshape(B, T, N_KV_HEADS, HEAD_DIM), pos)
    v = v.reshape(B, T, N_KV_HEADS, HEAD_DIM)
    if k_buf is None:
        nb = T // BLOCK
        qb = q.reshape(B, nb, BLOCK, N_KV_HEADS, GROUP, HEAD_DIM)
        kb = k.reshape(B, nb, BLOCK, N_KV_HEADS, HEAD_DIM)
        vb = v.reshape(B, nb, BLOCK, N_KV_HEADS, HEAD_DIM)
        kk = jnp.concatenate([jnp.concatenate([jnp.zeros_like(kb[:, :1]), kb[:, :-1]], 1), kb], 2)
        vv = jnp.concatenate([jnp.concatenate([jnp.zeros_like(vb[:, :1]), vb[:, :-1]], 1), vb], 2)
        pos_b = pos.reshape(nb, BLOCK)
        kpos = jnp.concatenate([pos_b - BLOCK, pos_b], axis=1)
        o = _sink_attention(qb, kk, vv, pos_b, kpos, sinks)
        new_k, new_v = k[:, -WINDOW:], v[:, -WINDOW:]
    else:
        wb = k_buf.shape[1]
        kk = jnp.concatenate([k_buf, k], axis=1)
        vv = jnp.concatenate([v_buf, v], axis=1)
        kpos = pos0 - wb + jnp.arange(wb + T, dtype=jnp.int32)
        o = _sink_attention(q, kk, vv, pos, kpos, sinks)
        new_k, new_v = kk[:, -wb:], vv[:, -wb:]
    yc = o.reshape(B, T, D_ATTN) * jax.nn.silu(za)

    mix = jnp.concatenate([ya, yb, yc], axis=-1) @ w_out
    x = _layernorm(ALPHA * x + (1.0 + gate[:, None]) * mix, ln_g, ln_b)
    return x, new_conv, new_pool, new_k, new_v


def setup_inputs(seed: int = 0) -> dict:
    key = jax.random.key(seed)
    ks = jax.random.split(key, 24)
    f = jnp.float32
    wb = min(WINDOW, PAST_LEN)
    n = lambda i, shape, s=1.0: s * jax.random.normal(ks[i], shape, f)
    return {
        'x_prompt': n(0, (BATCH, SEQ, D_MODEL)),
        'x_sample': n(1, (DEC_BATCH, DEC_SEQ, D_MODEL)),
        'cache_conv': n(2, (DEPTH, DEC_BATCH, CONV_BUF, D_CONV), 0.5),
        'cache_pool': n(3, (DEPTH, DEC_BATCH, POOL_BUF, D_POOL)),
        'cache_k': n(4, (DEPTH, DEC_BATCH, wb, N_KV_HEADS, HEAD_DIM)),
        'cache_v': n(5, (DEPTH, DEC_BATCH, wb, N_KV_HEADS, HEAD_DIM)),
        'c_prompt': n(6, (BATCH, D_MODEL)),
        'c_sample': n(7, (DEC_BATCH, D_MODEL)),
        'w_in': n(8, (DEPTH, D_MODEL, D_IN), D_MODEL ** -0.5),
        'w_out': n(9, (DEPTH, D_MIX, D_MODEL), BETA * D_MIX ** -0.5),
        'conv_w': n(10, (DEPTH, CONV_WIDTH, D_CONV), CONV_WIDTH ** -0.5),
        'conv_b': n(11, (DEPTH, D_CONV), 0.02),
        'cnorm_g': 1.0 + n(12, (DEPTH, D_CONV), 0.02),
        'cnorm_b': n(13, (DEPTH, D_CONV), 0.02),
        'pool_w': n(14, (DEPTH, N_POOL_GROUPS, POOL_GROUP_DIM, POOL_GROUP_DIM), POOL_GROUP_DIM ** -0.5),
        'pool_scale': 1.0 + n(15, (DEPTH, D_POOL), 0.1),
        'sinks': n(16, (DEPTH, N_HEADS), 0.5),
        'w_mod': n(17, (DEPTH, D_MODEL, 3 * D_MODEL), 0.5 * D_MODEL ** -0.5),
        'b_mod': n(18, (DEPTH, 3 * D_MODEL), 0.02),
        'ln_g': 1.0 + n(19, (DEPTH, D_MODEL), 0.02),
        'ln_b': n(20, (DEPTH, D_MODEL), 0.02),
    }


def reference(x_prompt, x_sample, cache_conv, cache_pool, cache_k, cache_v, c_prompt, c_sample,
              w_in, w_out, conv_w, conv_b, cnorm_g, cnorm_b, pool_w, pool_scale, sinks,
              w_mod, b_mod, ln_g, ln_b):
    xp, xs = x_prompt, x_sample
    B = xp.shape[0]
    conv_p, pool_p, k_p, v_p = [], [], [], []
    conv_s, pool_s, k_s, v_s = [], [], [], []
    for l in range(DEPTH):
        params = (w_in[l], w_out[l], conv_w[l], conv_b[l], cnorm_g[l], cnorm_b[l],
                  pool_w[l], pool_scale[l], sinks[l], w_mod[l], b_mod[l], ln_g[l], ln_b[l])
        zc = jnp.zeros((B, CONV_BUF, D_CONV), xp.dtype)
        zp = jnp.zeros((B, POOL_BUF, D_POOL), xp.dtype)
        xp, cp, pp, kp, vp = _layer(xp, c_prompt, zc, zp, None, None, 0, *params)
        xs, cs, ps, ksn, vsn = _layer(xs, c_sample, cache_conv[l], cache_pool[l],
                                      cache_k[l], cache_v[l], PAST_LEN, *params)
        conv_p.append(cp); pool_p.append(pp); k_p.append(kp); v_p.append(vp)
        conv_s.append(cs); pool_s.append(ps); k_s.append(ksn); v_s.append(vsn)
    return (xp, xs,
            jnp.stack(conv_p), jnp.stack(pool_p), jnp.stack(k_p), jnp.stack(v_p),
            jnp.stack(conv_s), jnp.stack(pool_s), jnp.stack(k_s), jnp.stack(v_s))
```

```python
import contextlib
import re
import numpy as np
import concourse.bass as bass
import concourse.mybir as mybir
from concourse.bass_utils import run_bass_kernel_spmd

F32 = mybir.dt.float32
BF16 = mybir.dt.bfloat16
AF = mybir.ActivationFunctionType
ALU = mybir.AluOpType

D = 1024
NL = 4
NCORE = 8
OWN_BLOCKS = 16
HALO_BLOCKS = 4
NSB = 16
NS = 64
PAST_LEN = 8192
LN_EPS = 1e-5
ALPHA = (2.0 * NL) ** 0.25
EPS2 = LN_EPS / (ALPHA * ALPHA)
UOFF, GOFF, ZCOFF, PVOFF, ZPOFF = 0, 256, 512, 768, 1024
QOFF, KOFF, VOFF, ZAOFF = 1280, 1792, 1920, 2048
FW = 304
GB = 2
SAME_DIST = 0


class Sched:
    NDS = 24

    def __init__(self, nc, es, same_engine_sync=True):
        self.nc = nc
        self.eng = {'pe': nc.tensor, 'act': nc.scalar, 'dve': nc.vector, 'pool': nc.gpsimd, 'sp': nc.sync}
        self.sem = {e: es.enter_context(nc.semaphore('s_' + e)) for e in self.eng}
        self.cnt = {e: 0 for e in self.eng}
        self.waited = {e: {} for e in self.eng}
        self.dsem = [es.enter_context(nc.semaphore('d%d' % i)) for i in range(self.NDS)]
        self.dtarget = [0] * self.NDS
        self.dnext = 0
        self.lastw = {}
        self.readers = {}
        self.children = {}
        self.same = same_engine_sync
        self.nwaits = 0
        self.nops = 0
        self.nsame = 0
        self.nskip = 0
        self.same_dist = SAME_DIST

    def _wait(self, e, ev):
        if ev[0] == 'e':
            _, f, k = ev
            if f == e and (e == 'pe' or not self.same):
                return
            if f == e:
                self.nsame += 1
                if self.same_dist and self.cnt[e] - k >= self.same_dist:
                    self.nskip += 1
                    return
            key = f
            sem = self.sem[f]
        else:
            _, idx, k = ev
            key = ('d', idx)
            sem = self.dsem[idx]
        if self.waited[e].get(key, 0) >= k:
            return
        self.eng[e].wait_ge(sem, k)
        self.waited[e][key] = k
        self.nwaits += 1

    _BASE = re.compile(r'^(fr|br|tr|sm)\d+')

    def _base(self, x):
        m = self._BASE.match(x)
        return m.group(0) if m else x

    def _deps(self, e, r, w):
        extra = []
        for x in w:
            if self._base(x) == x and x in self.children:
                extra.extend(self.children[x])
        if extra:
            w = list(w) + extra
        best = {}
        def add(ev):
            if ev is None:
                return
            key = ev[1] if ev[0] == 'e' else ('d', ev[1])
            if key not in best or best[key][2] < ev[2]:
                best[key] = ev
        for x in r:
            add(self.lastw.get(x))
        for x in w:
            add(self.lastw.get(x))
            for ev in self.readers.get(x, {}).values():
                add(ev)
        for ev in best.values():
            self._wait(e, ev)

    def _record(self, ev, r, w):
        for x in list(r) + list(w):
            b = self._base(x)
            if b != x:
                self.children.setdefault(b, set()).add(x)
        key = ev[1] if ev[0] == 'e' else ('d', ev[1])
        for x in w:
            self.lastw[x] = ev
            self.readers[x] = {}
        for x in r:
            d = self.readers.setdefault(x, {})
            if key not in d or d[key][2] < ev[2]:
                d[key] = ev

    def op(self, e, fn, r=(), w=()):
        px = [x for x in r if x.startswith('pb') or x.startswith('pstat') or x.startswith('ptp')]
        if px:
            w = list(w) + px
        self._deps(e, r, w)
        inst = fn(self.eng[e])
        self.cnt[e] += 1
        inst.then_inc(self.sem[e], 1)
        ev = ('e', e, self.cnt[e])
        self._record(ev, r, w)
        self.nops += 1
        return ev

    def dma(self, q, out, in_, r=(), w=(), **kw):
        self._deps(q, r, w)
        idx = self.dnext
        self.dnext = (self.dnext + 1) % self.NDS
        if self.dtarget[idx] > 0:
            self._wait(q, ('d', idx, self.dtarget[idx]))
        pairs = list(zip(out, in_)) if isinstance(out, (list, tuple)) else [(out, in_)]
        for o, i in pairs:
            self.eng[q].dma_start(out=o, in_=i, **kw).then_inc(self.dsem[idx], 16)
            self.dtarget[idx] += 16
        ev = ('d', idx, self.dtarget[idx])
        self._record(ev, r, w)
        return ev

    def finish(self, evs):
        for ev in evs:
            self._wait('sp', ev)


class _Stop(Exception):
    pass


def build_program(passes=('H', 0, 1, 2, 3), nl=NL, same_engine_sync=True, stop_at=None):
    nc = bass.Bass("TRN2", target_bir_lowering=False)
    seen = []

    def ckpt(name):
        seen.append(name)
        if stop_at is not None and name == stop_at:
            raise _Stop()

    def din(name, shape):
        return nc.dram_tensor(name, list(shape), F32, kind="ExternalInput").ap()

    def dout(name, shape):
        return nc.dram_tensor(name, list(shape), F32, kind="ExternalOutput").ap()

    xin_d = din("xin", (2560, D))
    xs_d = din("xsin", (NS, D))
    w_in_d = din("w_in", (NL, D, 2560))
    w_out_d = din("w_out", (NL, D, D))
    w_mod_d = din("w_mod", (NL, D, 3 * D))
    cT_d = din("cT", (128, 8, 17))
    bmodT_d = din("bmodT", (128, NL, 24))
    pvec_d = din("pvec", (128, NL, 70))
    poolbd_d = din("poolbd", (128, NL, 2, 128))
    lngb_d = din("lngb", (NL, 2, D))
    sinks_d = din("sinks", (1, NL * 8))
    ident_d = din("ident", (128, 128))
    prot_d = din("prot", (128, 128))
    masks_d = din("masks", (128, 3, 512))
    smask_d = din("smask", (128, 2, 512))
    ropeC_d = din("ropeC", (128, 2560))
    ropeS_d = din("ropeS", (128, 2560))
    ropeCs_d = din("ropeCs", (128, NS))
    ropeSs_d = din("ropeSs", (128, NS))
    misc_d = din("misc", (128, 36))
    cconvT_d = din("cconvT", (NL, 128, 2, NSB, 30))
    cpoolT_d = din("cpoolT", (NL, 128, 2, NSB, 15))
    ckT_d = din("ckT", (NL, 128, NSB, 128))
    cv_d = din("cv", (NL, 128, NSB, 128))
    cconv_n = din("cconv_n", (NL, NSB, 30, 256))
    cpool_n = din("cpool_n", (NL, NSB, 15, 256))
    ck_n = din("ck_n", (NL, NSB, 128, 128))
    cv_n = din("cv_n", (NL, NSB, 128, 128))

    yp_d = dout("y_p", (OWN_BLOCKS * 128, D))
    ys_d = dout("y_s", (NS, D))
    convp_d = dout("convp", (NL, 128, 2, 30))
    poolp_d = dout("poolp", (NL, 128, 2, 15))
    kp_d = dout("kp", (NL, 128, 128))
    vp_d = dout("vp", (NL, 128, 128))
    convs_old = dout("convs_old", (NL, NSB, 26, 256))
    pools_old = dout("pools_old", (NL, NSB, 11, 256))
    ks_old = dout("ks_old", (NL, NSB, 124, 128))
    vs_old = dout("vs_old", (NL, NSB, 124, 128))
    convs_new = dout("convs_new", (NL, 128, 2, NSB, 4))
    pools_new = dout("pools_new", (NL, 128, 2, NSB, 4))
    ks_new = dout("ks_new", (NL, 128, NS))
    vs_new = dout("vs_new", (NL, NS, 128))

    out_evs = []
    with contextlib.ExitStack() as es:
        S = Sched(nc, es, same_engine_sync)

        def sb(name, shape, dt=F32):
            return es.enter_context(nc.sbuf_tensor("sb_" + name, list(shape), dt))

        def ps(name, shape, dt=F32):
            return es.enter_context(nc.psum_tensor("ps_" + name, list(shape), dt))

        xo = [sb("xo%d" % i, (128, D)) for i in range(4)]
        xsm = sb("xsm", (128, D))
        wbf = sb("wbf", (128, 8, 2560), BF16)
        wobf = sb("wobf", (128, 8, D), BF16)
        hT = sb("hT", (128, 8, 256), BF16)
        mixT = sb("mixT", (128, 8, 256), BF16)
        xn_all = sb("xn_all", (128, GB, D), BF16)
        qrT = sb("qrT", (128, 4, 256), BF16)
        krT = sb("krT", (128, 256), BF16)
        vext = sb("vext", (128, GB, 2, 65), BF16)
        diag = sb("diag", (128, 2, 31, 128), BF16)
        cf = sb("cf", (128, 2, 256))
        aT = sb("aT", (128, 2, 30 + 256), BF16)
        pvT = sb("pvT", (128, 2, 15 + 256))
        ropet = sb("ropet", (128, 2, 256))
        lng = sb("lng", (128, 2, D))
        g1a = sb("g1a", (128, D))
        modT = sb("modT", (128, NL, 24, 17))
        masks = sb("masks", (128, 3, 512), BF16)
        smask = sb("smask", (128, 2, 512), BF16)
        ident32 = sb("ident32", (128, 128))
        ones32 = sb("ones32", (128, 128))
        identb = sb("identb", (128, 128), BF16)
        protb = sb("protb", (128, 128), BF16)
        onesb = sb("onesb", (128, 128), BF16)
        oneb1 = sb("oneb1", (128, 128), BF16)
        pvec = sb("pvec", (128, NL, 70))
        dvh = sb("dvh", (128, NL, 70))
        dvq = sb("dvq", (128, NL, 70))
        poolbd = sb("poolbd", (128, NL, 2, 128), BF16)
        bmodT = sb("bmodT", (128, NL, 24))
        esink = sb("esink", (128, NL * 8))
        esink2 = sb("esink2", (128, NL * 8))
        misc = sb("misc", (128, 36))
        mhalf = sb("mhalf", (128, 4))
        cT = sb("cT", (128, 8, 17))
        siluT = sb("siluT", (128, 8, 17), BF16)
        kst = sb("kst", (128, NL, 128), BF16)
        vst = sb("vst", (128, NL, 2, 65), BF16)
        cst = sb("cst", (128, NL, 2, 30), BF16)
        pst = sb("pst", (128, NL, 2, 15))
        small = sb("small", (128, 16, 16))
        hTs = sb("hTs", (128, 8, NS), BF16)
        mixTs = sb("mixTs", (128, 8, NS), BF16)
        qrTs = sb("qrTs", (128, 4, NS), BF16)
        krTs = sb("krTs", (128, NS), BF16)
        vnew = sb("vnew", (128, 128), BF16)
        ckTb = sb("ckTb", (128, NSB, 128), BF16)
        cvb = sb("cvb", (128, NSB, 128), BF16)
        aTs = sb("aTs", (128, 2, NSB, 34), BF16)
        pvTs = sb("pvTs", (128, 2, NSB, 19))
        ropes = sb("ropes", (128, 2, NS))
        esinkS = sb("esinkS", (128, 512))

        NFR = 16
        fring = [sb("fr%d" % i, (128, FW)) for i in range(NFR)]
        NBR = 11
        bring = [sb("br%d" % i, (128, 512), BF16) for i in range(NBR)]
        tring = [sb("tr%d" % i, (128, D)) for i in range(2)]
        NPB = 4
        pbank = [ps("pb%d" % i, (128, 512)) for i in range(NPB)]
        pstatA = ps("pstatA", (128, 512))
        pstatB = ps("pstatB", (128, 512))
        ptp = [ps("ptp%d" % i, (128, 1024), BF16) for i in range(2)]
        ctr = {'f': 0, 'b': 0, 't': 0, 'p': 0, 'tp': 0, 's': 0,
               'fX': 0, 'fY': 0, 'bX': 0, 'bY': 0, 'pX': 0, 'pY': 0, 'pZ0': 0, 'pZ1': 0}
        ctx = {'chain': None}
        FR = {'X': list(range(0, 8)), 'Y': list(range(8, 16))}
        BR = {'X': list(range(0, 8)), 'Y': list(range(8, 11))}
        PB = {'X': [0, 1], 'Y': [2, 3], 'Z0': [0, 1], 'Z1': [2, 3]}

        def tf():
            ch = ctx['chain']
            if ch is None:
                i = ctr['f']; ctr['f'] = (i + 1) % NFR
            else:
                j = ctr['f' + ch]; ctr['f' + ch] = (j + 1) % len(FR[ch]); i = FR[ch][j]
            return fring[i], 'fr%d' % i

        def tb():
            ch = ctx['chain']
            if ch is None:
                i = ctr['b']; ctr['b'] = (i + 1) % NBR
            else:
                j = ctr['b' + ch]; ctr['b' + ch] = (j + 1) % len(BR[ch]); i = BR[ch][j]
            return bring[i], 'br%d' % i

        def tt():
            i = ctr['t']; ctr['t'] = (i + 1) % 2
            return tring[i], 'tr%d' % i

        def bank():
            ch = ctx['chain']
            if ch is None:
                i = ctr['p']; ctr['p'] = (i + 1) % NPB
            else:
                j = ctr['p' + ch]; ctr['p' + ch] = (j + 1) % len(PB[ch]); i = PB[ch][j]
            return pbank[i], 'pb%d' % i

        def run_interleaved(chains):
            live = list(chains)
            while live:
                for item in list(live):
                    ctx['chain'] = item[0]
                    try:
                        for _ in range(2 if item[0] == 'X' else 1):
                            next(item[1])
                    except StopIteration:
                        live.remove(item)
            ctx['chain'] = None

        def tph():
            i = ctr['tp']; ctr['tp'] = (i + 1) % 2
            return ptp[i][:, 0:512], 'ptp%d' % i

        def sm():
            i = ctr['s']; ctr['s'] = (i + 1) % 16
            return small[:, i, :], 'sm%d' % i

        def mm(out, lhsT, rhs, start, stop, r, w):
            S.op('pe', lambda e: e.matmul(out, lhsT=lhsT, rhs=rhs, start=start, stop=stop, skip_group_check=True), r=r, w=w)

        def tr(out, in_, r, w):
            S.op('pe', lambda e: e.transpose(out, in_, identb[:in_.shape[0], :in_.shape[0]]), r=list(r) + ['identb'], w=w)

        def act(out, in_, func, r, w, scale=None, bias=None):
            kw = {}
            if scale is not None:
                kw['scale'] = scale
            if bias is not None:
                kw['bias'] = bias
            S.op('act', lambda e: e.activation(out=out, in_=in_, func=func, **kw), r=r, w=w)

        def ts(eng, out, in0, s1, s2, op0, op1, r, w):
            if s2 is None:
                S.op(eng, lambda e: e.tensor_scalar(out=out, in0=in0, scalar1=s1, scalar2=None, op0=op0), r=r, w=w)
            else:
                S.op(eng, lambda e: e.tensor_scalar(out=out, in0=in0, scalar1=s1, scalar2=s2, op0=op0, op1=op1), r=r, w=w)

        def stt(out, in0, scalar, in1, op0, op1, r, w):
            S.op('dve', lambda e: e.scalar_tensor_tensor(out=out, in0=in0, scalar=scalar, in1=in1, op0=op0, op1=op1), r=r, w=w)

        def tten(eng, out, in0, in1, op, r, w):
            S.op(eng, lambda e: e.tensor_tensor(out=out, in0=in0, in1=in1, op=op), r=r, w=w)

        def cpy(eng, out, in_, r, w):
            S.op(eng, lambda e: e.tensor_copy(out=out, in_=in_), r=r, w=w)

        def mset(eng, ap, val, w):
            S.op(eng, lambda e: e.memset(ap, val), w=w)

        try:
            S.dma('sp', ident32[:], ident_d, w=['ident32'])
            S.dma('pool', identb[:], ident_d, w=['identb'])
            S.dma('pool', protb[:], prot_d, w=['protb'])
            S.dma('pool', masks[:], masks_d, w=['masks'])
            S.dma('pool', smask[:], smask_d, w=['smask'])
            S.dma('sp', pvec[:], pvec_d, w=['pvec'])
            S.dma('pool', poolbd[:], poolbd_d, w=['poolbd'])
            S.dma('sp', bmodT[:], bmodT_d, w=['bmodT'])
            S.dma('sp', misc[:], misc_d, w=['misc'])
            S.dma('sp', cT[:], cT_d, w=['cT'])
            S.dma('sp', esink[:], sinks_d.partition_broadcast(128), w=['esink'])
            mset('pool', ones32[:], 1.0, ['ones32'])
            mset('pool', onesb[:], 1.0 / 256.0, ['onesb'])
            mset('pool', oneb1[:], 1.0, ['oneb1'])
            mset('pool', mhalf[:], -0.5, ['mhalf'])
            mset('pool', kst[:], 0.0, ['kst%d' % l for l in range(NL)])
            mset('pool', vst[:], 0.0, ['vst%d' % l for l in range(NL)])
            mset('pool', cst[:], 0.0, ['cst%d' % l for l in range(NL)])
            mset('pool', pst[:], 0.0, ['pst%d' % l for l in range(NL)])
            mset('pool', vext[:], 1.0, ['vext%d' % b for b in range(GB)])
            mset('pool', xsm[:], 0.0, ['xsm'])
            act(esink[:], esink[:], AF.Exp, r=['esink'], w=['esink'])
            ts('dve', esink2[:], esink[:], 2.0, None, ALU.mult, None, r=['esink'], w=['esink2'])
            ts('dve', dvh[:], pvec[:], 0.5, None, ALU.mult, None, r=['pvec'], w=['dvh'])
            ts('dve', dvq[:], pvec[:], 0.25, None, ALU.mult, None, r=['pvec'], w=['dvq'])

            ckpt('consts')
            if 'H' in passes:
                for l in range(nl):
                    out_evs.append(S.dma('sp', convs_old[l], cconv_n[l, :, 4:30, :]))
                    out_evs.append(S.dma('sp', pools_old[l], cpool_n[l, :, 4:15, :]))
                    out_evs.append(S.dma('sp', ks_old[l], ck_n[l, :, 4:128, :]))
                    out_evs.append(S.dma('sp', vs_old[l], cv_n[l, :, 4:128, :]))

            ckpt('oldcopy')
            t1, k1 = tf()
            act(t1[:, 0:136], cT[:].rearrange("p a b -> p (a b)"), AF.Tanh, r=['cT'], w=[k1], scale=0.5)
            t2, k2 = tf()
            stt(t2[:, 0:136], t1[:, 0:136], 1.0, cT[:].rearrange("p a b -> p (a b)"), ALU.add, ALU.mult, r=[k1, 'cT'], w=[k2])
            ts('dve', siluT[:].rearrange("p a b -> p (a b)"), t2[:, 0:136], 0.5, None, ALU.mult, None, r=[k2], w=['siluT'])
            pieces = [(l, j) for l in range(nl) for j in range(3)]
            for idx, (l, j) in enumerate(pieces):
                half = idx % 2
                key = 'winA' if half == 0 else 'winB'
                dst = wbf[:, :, half * 1280: half * 1280 + 1024]
                src = w_mod_d[l].rearrange("(kc p) n -> p kc n", p=128)[:, :, j * 1024:(j + 1) * 1024]
                S.dma('pool', [dst[:, 0:4, :], dst[:, 4:8, :]], [src[:, 0:4, :], src[:, 4:8, :]], w=[key])
                for oc in range(8):
                    bk, bkey = bank()
                    for kc in range(8):
                        mm(bk[:, 0:17], dst[:, kc, oc * 128:(oc + 1) * 128], siluT[:, kc, :], kc == 0, kc == 7,
                           r=[key, 'siluT'], w=[bkey])
                    ch = j * 8 + oc
                    if j == 0:
                        ts('dve', modT[:, l, ch, :], bk[:, 0:17], bmodT[:, l, ch:ch + 1], None, ALU.add, None,
                           r=[bkey, 'bmodT'], w=['modT'])
                    else:
                        ts('dve', modT[:, l, ch, :], bk[:, 0:17], bmodT[:, l, ch:ch + 1], 1.0, ALU.add, ALU.add,
                           r=[bkey, 'bmodT'], w=['modT'])
            for l in range(nl):
                ts('dve', modT[:, l, 16:24, :], modT[:, l, 16:24, :], 1.0 / ALPHA, None, ALU.mult, None, r=['modT'], w=['modT'])

            ckpt('prologue')
            def load_w_in(l, half):
                key = 'winA' if half == 0 else 'winB'
                src = w_in_d[l].rearrange("(kc p) n -> p kc n", p=128)[:, :, half * 1280:(half + 1) * 1280]
                dst = wbf[:, :, half * 1280:(half + 1) * 1280]
                S.dma('pool', [dst[:, 0:4, :], dst[:, 4:8, :]], [src[:, 0:4, :], src[:, 4:8, :]], w=[key])

            def load_w_out(l):
                src = w_out_d[l].rearrange("(kc p) n -> p kc n", p=128)
                S.dma('pool', [wobf[:, 0:4, :], wobf[:, 4:8, :]], [src[:, 0:4, :], src[:, 4:8, :]], w=['wout'])

            def diag_gen(l, gate):
                while gate is not None and not gate[0]:
                    yield
                k = 0
                for c in range(2):
                    for j in range(31):
                        if k % 2 == 0:
                            act(diag[:, c, j, :], ident32[:], AF.Identity, r=['ident32', 'dvh'], w=['diagA'],
                                scale=dvh[:, l, c * 31 + j: c * 31 + j + 1])
                        else:
                            ts('pool', diag[:, c, j, :], ident32[:], dvh[:, l, c * 31 + j: c * 31 + j + 1], 1.0, ALU.mult, ALU.mult,
                               r=['ident32', 'dvh'], w=['diagP'])
                        k += 1
                        if k % 4 == 0:
                            yield

            def load_lng(l):
                S.dma('sp', [lng[:, 0, :], lng[:, 1, :]],
                      [lngb_d[l, 0:1, :].partition_broadcast(128), lngb_d[l, 1:2, :].partition_broadcast(128)], w=['lng'])

            def layer_setup(l, with_sample, build_diag=True, load_lng_now=True):
                if load_lng_now:
                    load_lng(l)
                for hf in range(2):
                    bk, bkey = bank()
                    for c4 in range(4):
                        ch = hf * 4 + c4
                        dg, dk = tf()
                        ts('dve', dg[:, 0:128], ident32[:], modT[:, l, 16 + ch, 0:1], None, ALU.mult, None,
                           r=['ident32', 'modT'], w=[dk])
                        mm(bk[:, c4 * 128:(c4 + 1) * 128], ones32[:], dg[:, 0:128], True, True, r=['ones32', dk], w=[bkey])
                    act(g1a[:, hf * 512:(hf + 1) * 512], bk[:], AF.Copy, r=[bkey], w=['g1a'])
                if build_diag:
                    for _ in diag_gen(l, None):
                        pass
                if with_sample:
                    S.dma('pool', ckTb[:], ckT_d[l], w=['ckTb'])
                    S.dma('pool', cvb[:], cv_d[l], w=['cvb'])
                    S.dma('pool', aTs[:, :, :, 0:30], cconvT_d[l], w=['aTs'])
                    S.dma('sp', pvTs[:, :, :, 0:15], cpoolT_d[l], w=['pvTs'])
                    ts('pool', aTs[:, :, :, 0:30], aTs[:, :, :, 0:30], 2.0, 1.0, ALU.mult, ALU.mult, r=['aTs'], w=['aTs'])
                    cpy('dve', esinkS[:].rearrange("p (h r) -> p h r", h=8),
                        esink[:, l * 8:(l + 1) * 8].unsqueeze(2).to_broadcast([128, 8, 64]), r=['esink'], w=['esinkS'])

            def ln_stats(src, nrows, eps, rkeys):
                st, sk = sm()
                S.op('dve', lambda e: e.bn_stats(st[:nrows, 0:6], src[:nrows, 0:512]), r=rkeys, w=[sk + 'a'])
                S.op('dve', lambda e: e.bn_stats(st[:nrows, 6:12], src[:nrows, 512:1024]), r=rkeys, w=[sk + 'b'])
                mv, mk = sm()
                S.op('dve', lambda e: e.bn_aggr(mv[:nrows, 0:2], st[:nrows, 0:12].rearrange("p (a b) -> p a b", a=2)),
                     r=[sk + 'a', sk + 'b'], w=[mk])
                ts('pool', mv[:nrows, 2:3], mv[:nrows, 1:2], eps, 1.0, ALU.add, ALU.mult, r=[mk], w=[mk + 'v'])
                tten('pool', mv[:nrows, 3:4], mv[:nrows, 2:3], mhalf[:nrows, 0:1], ALU.pow, r=[mk + 'v', 'mhalf'], w=[mk + 'r'])
                stt(mv[:nrows, 4:5], mv[:nrows, 0:1], -1.0, mv[:nrows, 3:4], ALU.mult, ALU.mult, r=[mk, mk + 'r'], w=[mk + 'n'])
                return mv[:nrows, 3:4], mv[:nrows, 4:5], [mk + 'r', mk + 'n']

            def make_s1(l, blocks):
                nb = len(blocks)
                N = nb * 128
                for bi, (xb, xk) in enumerate(blocks):
                    rstd, nmr, lk = ln_stats(xb, 128, LN_EPS, list(xk))
                    act(xn_all[:, bi, :], xb[:], AF.Identity, r=list(xk) + lk, w=['xn%d' % bi], scale=rstd, bias=nmr)
                    yield
                for kc in range(8):
                    tp, tk = tph()
                    for bi in range(nb):
                        tr(tp[:, bi * 128:(bi + 1) * 128], xn_all[:, bi, kc * 128:(kc + 1) * 128], r=['xn%d' % bi], w=[tk])
                    if kc % 2 == 0:
                        ts('dve', hT[:, kc, :N], tp[:, :N], modT[:, l, 8 + kc, 0:1], modT[:, l, kc, 0:1], ALU.mult, ALU.add,
                           r=[tk, 'modT'], w=['hT'])
                    else:
                        act(hT[:, kc, :N], tp[:, :N], AF.Identity, r=[tk, 'modT'], w=['hT'],
                            scale=modT[:, l, 8 + kc, 0:1], bias=modT[:, l, kc, 0:1])
                    yield

            def prompt_group(l, blocks, tcol, first_seq, last_seq, hooks=None, s1_done=None, next_s1=None, diag_next=None, pending=None, defer=False, after_xy=None, partial=False):
                conv_done = [False]
                hooks = hooks or {}
                nb = len(blocks)
                N = nb * 128
                wA, wB = 'winA', 'winB'
                if s1_done is None:
                    for _ in make_s1(l, blocks):
                        pass

                def proj(off):
                    bk, bkey = bank()
                    for kc in range(8):
                        mm(bk[:, :N], wbf[:, kc, off:off + 128], hT[:, kc, :N], kc == 0, kc == 7,
                           r=[wA if off < 1280 else wB, 'hT'], w=[bkey])
                    return bk, bkey

                def chainX():
                    S.dma('sp', [ropet[:, 0, :N], ropet[:, 1, :N]], [ropeC_d[:, tcol:tcol + N], ropeS_d[:, tcol:tcol + N]], w=['ropet'])
                    ckpt('p1')
                    for c in ([4] if partial else range(5)):
                        bk, bkey = proj(QOFF + c * 128)
                        A_, ak = tf()
                        tten('dve', A_[:, :N], bk[:, :N], ropet[:, 0, :N], ALU.mult, r=[bkey, 'ropet'], w=[ak])
                        B_, bkk = tb()
                        tten('dve', B_[:, :N], bk[:, :N], ropet[:, 1, :N], ALU.mult, r=[bkey, 'ropet'], w=[bkk])
                        yield
                        pr, prk = bank()
                        mm(pr[:, :N], protb[:], B_[:, :N], True, True, r=['protb', bkk], w=[prk])
                        if c < 4:
                            tten('dve', qrT[:, c, :N], pr[:, :N], A_[:, :N], ALU.add, r=[prk, ak], w=['qrT'])
                            yield
                        else:
                            tten('dve', krT[:, :N], pr[:, :N], A_[:, :N], ALU.add, r=[prk, ak], w=['krT'])
                            if last_seq:
                                ko, kok = tf()
                                tten('dve', ko[:, 0:128], pr[:, N - 128:N], A_[:, N - 128:N], ALU.add, r=[prk, ak], w=[kok])
                                out_evs.append(S.dma('sp', kp_d[l], ko[:, 0:128], r=[kok]))
                    ckpt('p2')
                    gz = []
                    for bi in range(nb):
                        bk, bkey = bank()
                        for kc in range(8):
                            mm(bk[:, 0:128], hT[:, kc, bi * 128:(bi + 1) * 128], wbf[:, kc, VOFF:VOFF + 128], kc == 0, kc == 7,
                               r=[wB, 'hT'], w=[bkey])
                        act(vext[:, bi, :, 0:64], bk[:, 0:128].rearrange("p (h d) -> p h d", h=2), AF.Copy, r=[bkey], w=['vext%d' % bi])
                        yield
                        if last_seq and bi == nb - 1:
                            vo, vok = tf()
                            cpy('dve', vo[:, 0:128], bk[:, 0:128], r=[bkey], w=[vok])
                            out_evs.append(S.dma('sp', vp_d[l], vo[:, 0:128], r=[vok]))
                        if partial:
                            continue
                        bz, bzk = bank()
                        for kc in range(8):
                            mm(bz[:, :], hT[:, kc, bi * 128:(bi + 1) * 128], wbf[:, kc, ZAOFF:ZAOFF + 512], kc == 0, kc == 7,
                               r=[wB, 'hT'], w=[bzk])
                        gpair = []
                        for hh in range(2):
                            g_, gk_ = tf()
                            act(g_[:, 0:256], bz[:, hh * 256:(hh + 1) * 256], AF.Tanh, r=[bzk], w=[gk_], scale=0.5)
                            stt(g_[:, 0:256], g_[:, 0:256], 1.0, bz[:, hh * 256:(hh + 1) * 256], ALU.add, ALU.mult, r=[gk_, bzk], w=[gk_])
                            gpair.append((g_, gk_))
                        gz.append(tuple(gpair))
                        yield
                    if 3 in hooks:
                        hooks[3]()
                    ckpt('p3')
                    for bi in ([] if partial else range(nb)):
                        if bi == 0:
                            kprev = lambda hk: kst[hk * 64:(hk + 1) * 64, l, :]
                            kprev_key = 'kst%d' % l
                            vprev = lambda hk: vst[:, l, hk, :]
                            vprev_key = 'vst%d' % l
                        else:
                            kprev = lambda hk, b_=bi: krT[hk * 64:(hk + 1) * 64, (b_ - 1) * 128:b_ * 128]
                            kprev_key = 'krT'
                            vprev = lambda hk, b_=bi: vext[:, b_ - 1, hk, :]
                            vprev_key = 'vext%d' % (bi - 1)
                        mprev = masks[:, 2, :] if (first_seq and bi == 0) else masks[:, 1, :]
                        E = {}
                        for hk in range(2):
                            for kb in range(2):
                                st_, stk = bank()
                                lhs = kprev(hk) if kb == 0 else krT[hk * 64:(hk + 1) * 64, bi * 128:(bi + 1) * 128]
                                lk_ = kprev_key if kb == 0 else 'krT'
                                mm(st_[:, :], lhs, qrT[hk * 64:(hk + 1) * 64, :, bi * 128:(bi + 1) * 128], True, False,
                                   r=[lk_, 'qrT'], w=[stk])
                                mm(st_[:, :], identb[:], mprev if kb == 0 else masks[:, 0, :], False, True,
                                   r=['identb', 'masks'], w=[stk])
                                e_, ek = tb()
                                act(e_[:, :], st_[:, :], AF.Exp, r=[stk], w=[ek], scale=0.125)
                                E[(hk, kb)] = (e_, ek)
                                yield
                        den, dk_ = sm()
                        Ob = []
                        for hk in range(2):
                            o_, ok_ = bank()
                            ov = o_[:, 0:260].rearrange("p (g d) -> p g d", g=4)
                            for g in range(4):
                                e0, e0k = E[(hk, 0)]
                                e1, e1k = E[(hk, 1)]
                                mm(ov[:, g, :], e0[:, g * 128:(g + 1) * 128], vprev(hk), True, False, r=[e0k, vprev_key], w=[ok_])
                                mm(ov[:, g, :], e1[:, g * 128:(g + 1) * 128], vext[:, bi, hk, :], False, True,
                                   r=[e1k, 'vext%d' % bi], w=[ok_])
                            stt(den[:, hk * 4:(hk + 1) * 4], ov[:, :, 64], 2.0, esink2[:, l * 8 + hk * 4: l * 8 + hk * 4 + 4], ALU.mult, ALU.add,
                                r=[ok_, 'esink2'], w=[dk_ + 'h%d' % hk])
                            Ob.append((ov, ok_))
                            yield
                        S.op('dve', lambda e: e.reciprocal(den[:, 8:16], den[:, 0:8]), r=[dk_ + 'h0', dk_ + 'h1'], w=[dk_ + 'r'])
                        yield
                        yc, yck = tb()
                        yield
                        ycv = yc[:, :].rearrange("p (g h d) -> p g h d", g=4, h=2)
                        for hk in range(2):
                            ov, ok_ = Ob[hk]
                            o32, o32k = tf()
                            tten('dve', o32[:, 0:256].rearrange("p (g d) -> p g d", g=4), ov[:, :, 0:64],
                                 den[:, 8 + hk * 4: 12 + hk * 4].unsqueeze(2).to_broadcast([128, 4, 64]), ALU.mult,
                                 r=[ok_, dk_ + 'r'], w=[o32k])
                            for gp in range(2):
                                gt_, gk_ = gz[bi][gp]
                                tten('dve', ycv[:, 2 * gp:2 * gp + 2, hk, :], o32[:, gp * 128:(gp + 1) * 128].rearrange("p (g d) -> p g d", g=2),
                                     gt_[:, 0:256].rearrange("p (g h d) -> p g h d", g=2, h=2)[:, :, hk, :], ALU.mult,
                                     r=[o32k, gk_], w=[yck])
                        tp, tk = tph()
                        yield
                        for c in range(4):
                            tr(tp[:, c * 128:(c + 1) * 128], yc[:, c * 128:(c + 1) * 128], r=[yck], w=[tk])
                        act(mixT[:, 4:8, bi * 128:(bi + 1) * 128], tp[:, :].rearrange("p (c t) -> p c t", c=4), AF.Copy,
                            r=[tk], w=['mixTa'])
                    ckpt('p4')
                    cpy('pool', kst[:, l, :], krT[:, N - 128:N], r=['krT'], w=['kst%d' % l])
                    cpy('pool', vst[:, l, :, :], vext[:, nb - 1, :, :], r=['vext%d' % (nb - 1)], w=['vst%d' % l])

                    yield

                def chainY():
                    cpy('pool', aT[:, :, 0:30], cst[:, l, :, :], r=['cst%d' % l], w=['aT'])
                    pmean, pmk = pstatA, 'pstatA'
                    pex2, pek = pstatB, 'pstatB'
                    for c in range(2):
                        bu, buk = proj(UOFF + c * 128)
                        yield
                        bg, bgk = proj(GOFF + c * 128)
                        th, thk = tf()
                        act(th[:, :N], bg[:, :N], AF.Tanh, r=[bgk], w=[thk], scale=0.5)
                        stt(aT[:, c, 30:30 + N], th[:, :N], 1.0, bu[:, :N], ALU.add, ALU.mult, r=[thk, buk], w=['aT'])
                        yield
                        if last_seq:
                            a32, a32k = tf()
                            stt(a32[:, 0:30], th[:, N - 30:N], 1.0, bu[:, N - 30:N], ALU.add, ALU.mult, r=[thk, buk], w=[a32k])
                            ts('dve', a32[:, 32:62], a32[:, 0:30], 0.5, None, ALU.mult, None, r=[a32k], w=[a32k + 'o'])
                            out_evs.append(S.dma('sp', convp_d[l, :, c, :], a32[:, 32:62], r=[a32k + 'o']))
                        if partial:
                            continue
                        bc, bck = bank()
                        for j in range(31):
                            mm(bc[:, :N], diag[:, c, j, :], aT[:, c, j:j + N], j == 0, j == 30, r=['diagA', 'diagP', 'aT'], w=[bck])
                        cbias = pvec[:, l, 62 + c:63 + c]
                        yield
                        ts('dve', cf[:, c, :N], bc[:, :N], cbias, None, ALU.add, None, r=[bck, 'pvec'], w=['cf%d' % c])
                        cb_, cbk = tb()
                        act(cb_[:, :N], bc[:, :N], AF.Identity, r=[bck, 'pvec'], w=[cbk], bias=cbias)
                        cq_, cqk = tb()
                        act(cq_[:, :N], bc[:, :N], AF.Square, r=[bck, 'pvec'], w=[cqk], bias=cbias)
                        mm(pmean[0:1, :N], onesb[:, 0:1], cb_[:, :N], c == 0, c == 1, r=['onesb', cbk], w=[pmk])
                        mm(pex2[0:1, :N], onesb[:, 0:1], cq_[:, :N], c == 0, c == 1, r=['onesb', cqk], w=[pek])
                        yield
                    cpy('pool', cst[:, l, :, :], aT[:, :, N:N + 30], r=['aT'], w=['cst%d' % l])
                    conv_done[0] = True
                    if partial:
                        cpy('pool', pvT[:, :, 0:15], pst[:, l, :, :], r=['pst%d' % l], w=['pvT'])
                        for c in range(2):
                            bp, bpk = proj(PVOFF + c * 128)
                            act(pvT[:, c, 15:15 + N], bp[:, :N], AF.Copy, r=[bpk], w=['pvT'])
                            yield
                        cpy('pool', pst[:, l, :, :], pvT[:, :, N:N + 15], r=['pvT'], w=['pst%d' % l])
                        if 6 in hooks:
                            hooks[6]()
                        return
                    msq, msk = tf()
                    act(msq[0:1, :N], pmean[0:1, :N], AF.Square, r=[pmk], w=[msk])
                    mrow, mrk = tf()
                    act(mrow[0:1, :N], pmean[0:1, :N], AF.Copy, r=[pmk], w=[mrk])
                    var, vk = tf()
                    tten('dve', var[0:1, :N], pex2[0:1, :N], msq[0:1, :N], ALU.subtract, r=[pek, msk], w=[vk])
                    ts('dve', var[0:1, :N], var[0:1, :N], LN_EPS, None, ALU.add, None, r=[vk], w=[vk])
                    rs, rsk = tf()
                    S.op('dve', lambda e: e.reciprocal(rs[0:1, :N], var[0:1, :N]), r=[vk], w=[rsk])
                    act(rs[0:1, :N], rs[0:1, :N], AF.Sqrt, r=[rsk], w=[rsk])
                    yield
                    bmean, bmk = bank()
                    mm(bmean[:, :N], ones32[0:1, :], mrow[0:1, :N], True, True, r=['ones32', mrk], w=[bmk])
                    brs, brk = bank()
                    mm(brs[:, :N], ones32[0:1, :], rs[0:1, :N], True, True, r=['ones32', rsk], w=[brk])
                    yield
                    dts = []
                    for c in range(2):
                        d_, dk2 = tf()
                        tten('dve', d_[:, :N], cf[:, c, :N], bmean[:, :N], ALU.subtract, r=['cf%d' % c, bmk], w=[dk2])
                        tten('dve', d_[:, :N], d_[:, :N], brs[:, :N], ALU.mult, r=[dk2, brk], w=[dk2])
                        dts.append((d_, dk2))
                    yield
                    for c in range(2):
                        d_, dk2 = dts[c]
                        sg, sgk = tf()
                        act(sg[:, :N], d_[:, :N], AF.Tanh, r=[dk2, 'dvh'], w=[sgk],
                            scale=dvh[:, l, 64 + c:65 + c], bias=dvh[:, l, 66 + c:67 + c])
                        y_, yk = tf()
                        ts('dve', y_[:, :N], d_[:, :N], dvq[:, l, 64 + c:65 + c], dvq[:, l, 66 + c:67 + c], ALU.mult, ALU.add,
                           r=[dk2, 'dvq'], w=[yk])
                        stt(y_[:, :N], sg[:, :N], 1.0, y_[:, :N], ALU.add, ALU.mult, r=[sgk, yk], w=[yk])
                        yield
                        bz, bzk = proj(ZCOFF + c * 128)
                        yield
                        tz, tzk = tf()
                        act(tz[:, :N], bz[:, :N], AF.Tanh, r=[bzk], w=[tzk], scale=0.5)
                        stt(tz[:, :N], tz[:, :N], 1.0, bz[:, :N], ALU.add, ALU.mult, r=[tzk, bzk], w=[tzk])
                        tten('pool', mixT[:, c, :N], y_[:, :N], tz[:, :N], ALU.mult, r=[yk, tzk], w=['mixTc'])
                        yield

                    ckpt('p5')
                    cpy('pool', pvT[:, :, 0:15], pst[:, l, :, :], r=['pst%d' % l], w=['pvT'])
                    for c in range(2):
                        bp, bpk = proj(PVOFF + c * 128)
                        act(pvT[:, c, 15:15 + N], bp[:, :N], AF.Copy, r=[bpk], w=['pvT'])
                        yield
                    yield from pool_mix(l, N, pvT, 'pvT', first_seq, proj, mixT, 'mixTp', lambda ap: ap)
                    cpy('pool', pst[:, l, :, :], pvT[:, :, N:N + 15], r=['pvT'], w=['pst%d' % l])
                    if last_seq:
                        out_evs.append(S.dma('sp', poolp_d[l], pvT[:, :, N:N + 15], r=['pvT']))

                    if 6 in hooks:
                        hooks[6]()
                    yield

                xy = [('X', chainX()), ('Y', chainY())]
                if diag_next is not None:
                    xy.append(('D', diag_gen(diag_next, conv_done)))
                for ti, tg in enumerate(pending or []):
                    xy.append(('T%d' % ti, tg))
                run_interleaved(xy)
                if after_xy is not None:
                    after_xy()
                ckpt('p6')
                if partial:
                    if next_s1 is not None:
                        run_interleaved([('W', next_s1)])
                    return None
                chains = [('Z%d' % bi, out_head(l, mixT[:, :, bi * 128:(bi + 1) * 128], ['mixTa', 'mixTc', 'mixTp'], xb, xk, 128, g1a, 'g1a'))
                          for bi, (xb, xk) in enumerate(blocks)]
                if next_s1 is not None:
                    chains.append(('W', next_s1))
                run_interleaved(chains)
                tails = [out_tail(l, xb, xk, 128) for (xb, xk) in blocks]
                if defer:
                    return tails
                run_interleaved([('T%d' % ti, tg) for ti, tg in enumerate(tails)])
                return None

                ckpt('p7')

            def pool_mix(l, N, pv, pvk, first_seq, proj, mix, mixk, fv):
                W = 15 + N
                for c in range(2):
                    x = pv[:, c, :]
                    sw, swk = tf()
                    if c == 0:
                        t2, t2k = tf()
                        tten('pool', t2[:, 1:W], x[:, 1:W], x[:, 0:W - 1], ALU.add, r=[pvk], w=[t2k])
                        tten('pool', sw[0:64, 15:W], x[0:64, 15:W], x[0:64, 14:W - 1], ALU.add, r=[pvk], w=[swk + 'a'])
                        tten('pool', sw[64:128, 15:W], t2[64:128, 15:W], t2[64:128, 13:W - 2], ALU.add, r=[t2k], w=[swk + 'b'])
                    else:
                        t2, t2k = tf()
                        tten('pool', t2[:, 1:W], x[:, 1:W], x[:, 0:W - 1], ALU.add, r=[pvk], w=[t2k])
                        t4, t4k = tf()
                        tten('pool', t4[:, 3:W], t2[:, 3:W], t2[:, 1:W - 2], ALU.add, r=[t2k], w=[t4k])
                        tten('pool', sw[0:64, 15:W], t4[0:64, 15:W], t4[0:64, 11:W - 4], ALU.add, r=[t4k], w=[swk + 'a'])
                        t8, t8k = tf()
                        tten('pool', t8[64:128, 7:W], t4[64:128, 7:W], t4[64:128, 3:W - 4], ALU.add, r=[t4k], w=[t8k])
                        tten('pool', sw[64:128, 15:W], t8[64:128, 15:W], t8[64:128, 7:W - 8], ALU.add, r=[t8k], w=[swk + 'b'])
                    rk = [swk + 'a', swk + 'b']
                    if first_seq:
                        tten('pool', sw[:, 15:31], sw[:, 15:31], misc[:, 4 + c * 16: 4 + (c + 1) * 16], ALU.mult,
                             r=rk + ['misc'], w=[swk + 'c'])
                        rk = rk + [swk + 'c']
                    yield
                    pl, plk = tb()
                    stt(pl[:, :N], sw[:, 15:W], misc[:, 1 + c:2 + c], x[:, 15:W], ALU.mult, ALU.subtract,
                        r=rk + ['misc', pvk], w=[plk])
                    by, byk = bank()
                    mm(by[:, :N], poolbd[:, l, c, :], pl[:, :N], True, True, r=['poolbd', plk], w=[byk])
                    yield
                    bz, bzk = proj(ZPOFF + c * 128)
                    yield
                    tz, tzk = tf()
                    act(tz[:, :N], bz[:, :N], AF.Tanh, r=[bzk], w=[tzk], scale=0.5)
                    stt(tz[:, :N], tz[:, :N], 1.0, bz[:, :N], ALU.add, ALU.mult, r=[tzk, bzk], w=[tzk])
                    stt(mix[:, 2 + c, :N], by[:, :N], dvh[:, l, 68 + c:69 + c], tz[:, :N], ALU.mult, ALU.mult,
                        r=[byk, 'dvh', tzk], w=[mixk])
                    yield

            def out_head(l, mixblk, mixk, xb, xk, nrows, gtile, gk):
                T, tk_ = tt()
                for hf in range(2):
                    bo, bok = bank()
                    for kc in range(8):
                        mm(bo[:nrows, :], mixblk[:, kc, :], wobf[:, kc, hf * 512:(hf + 1) * 512], kc == 0, kc == 7,
                           r=list(mixk) + ['wout'], w=[bok])
                    yield
                    tten('dve', T[:nrows, hf * 512:(hf + 1) * 512], bo[:nrows, :], gtile[:nrows, hf * 512:(hf + 1) * 512], ALU.mult,
                         r=[bok, gk], w=[tk_ + 'h%d' % hf])
                    tten('dve', xb[:nrows, hf * 512:(hf + 1) * 512], T[:nrows, hf * 512:(hf + 1) * 512], xb[:nrows, hf * 512:(hf + 1) * 512],
                         ALU.add, r=[tk_ + 'h%d' % hf, xk[hf]], w=[xk[hf]])
                    yield

            def out_tail(l, xb, xk, nrows):
                rstd, nmr, lk = ln_stats(xb, nrows, EPS2, list(xk))
                yield
                act(xb[:nrows, :], xb[:nrows, :], AF.Identity, r=list(xk) + lk, w=list(xk), scale=rstd, bias=nmr)
                yield
                for hf in range(2):
                    tten('dve', xb[:nrows, hf * 512:(hf + 1) * 512], xb[:nrows, hf * 512:(hf + 1) * 512], lng[:nrows, 0, hf * 512:(hf + 1) * 512],
                         ALU.mult, r=[xk[hf], 'lng'], w=[xk[hf]])
                    yield
                for hf in range(2):
                    tten('dve', xb[:nrows, hf * 512:(hf + 1) * 512], xb[:nrows, hf * 512:(hf + 1) * 512], lng[:nrows, 1, hf * 512:(hf + 1) * 512],
                         ALU.add, r=[xk[hf], 'lng'], w=[xk[hf]])
                    yield

            def sample_group(l, hooks=None):
                hooks = hooks or {}
                N = NS
                wA, wB = 'winA', 'winB'
                rstd, nmr, lk = ln_stats(xsm, 64, LN_EPS, ['xsm'])
                act(xn_all[0:64, 0, :], xsm[0:64, :], AF.Identity, r=['xsm'] + lk, w=['xn0'], scale=rstd, bias=nmr)
                tp, tk = tph()
                for kc in range(8):
                    tr(tp[:, kc * 64:(kc + 1) * 64], xn_all[0:64, 0, kc * 128:(kc + 1) * 128], r=['xn0'], w=[tk])
                hv = hTs[:].rearrange("p k (b t) -> p k b t", b=NSB)
                tpv = tp[:, :].rearrange("p (k b t) -> p k b t", k=8, b=NSB)
                sc = modT[:, l, 8:16, 1:17].unsqueeze(3).to_broadcast([128, 8, NSB, 4])
                sh = modT[:, l, 0:8, 1:17].unsqueeze(3).to_broadcast([128, 8, NSB, 4])
                hf32, hfk = tt()
                hfv = hf32[:, 0:512].rearrange("p (k b t) -> p k b t", k=8, b=NSB)
                tten('dve', hfv, tpv, sc, ALU.mult, r=[tk, 'modT'], w=[hfk])
                tten('dve', hv, hfv, sh, ALU.add, r=[hfk, 'modT'], w=['hTs'])

                def proj(off):
                    bk, bkey = bank()
                    for kc in range(8):
                        mm(bk[:, :N], wbf[:, kc, off:off + 128], hTs[:, kc, :], kc == 0, kc == 7,
                           r=[wA if off < 1280 else wB, 'hTs'], w=[bkey])
                    return bk, bkey

                def chainX():
                    ckpt('s1')
                    for c in range(5):
                        bk, bkey = proj(QOFF + c * 128)
                        A_, ak = tf()
                        tten('dve', A_[:, :N], bk[:, :N], ropes[:, 0, :], ALU.mult, r=[bkey, 'ropes'], w=[ak])
                        B_, bkk = tb()
                        tten('dve', B_[:, :N], bk[:, :N], ropes[:, 1, :], ALU.mult, r=[bkey, 'ropes'], w=[bkk])
                        yield
                        pr, prk = bank()
                        mm(pr[:, :N], protb[:], B_[:, :N], True, True, r=['protb', bkk], w=[prk])
                        if c < 4:
                            tten('dve', qrTs[:, c, :], pr[:, :N], A_[:, :N], ALU.add, r=[prk, ak], w=['qrTs'])
                            yield
                        else:
                            tten('dve', krTs[:, :], pr[:, :N], A_[:, :N], ALU.add, r=[prk, ak], w=['krTs'])
                            ko, kok = tf()
                            tten('dve', ko[:, 0:N], pr[:, :N], A_[:, :N], ALU.add, r=[prk, ak], w=[kok])
                            out_evs.append(S.dma('sp', ks_new[l], ko[:, 0:N], r=[kok]))
                    ckpt('s2')
                    bk, bkey = bank()
                    for kc in range(8):
                        mm(bk[0:64, 0:128], hTs[:, kc, :], wbf[:, kc, VOFF:VOFF + 128], kc == 0, kc == 7, r=[wB, 'hTs'], w=[bkey])
                    act(vnew[0:64, :], bk[0:64, 0:128], AF.Copy, r=[bkey], w=['vnew'])
                    vo, vok = tf()
                    cpy('dve', vo[0:64, 0:128], bk[0:64, 0:128], r=[bkey], w=[vok])
                    out_evs.append(S.dma('sp', vs_new[l], vo[0:64, 0:128], r=[vok]))
                    yield
                    gzT, gzk = tf()
                    for g in range(4):
                        bz, bzk = proj(ZAOFF + g * 128)
                        th, thk = tf()
                        act(th[:, :N], bz[:, :N], AF.Tanh, r=[bzk], w=[thk], scale=0.5)
                        stt(gzT[:, g * 64:(g + 1) * 64], th[:, :N], 1.0, bz[:, :N], ALU.add, ALU.mult, r=[thk, bzk], w=[gzk + 'g%d' % g])
                        yield
                    gzks = [gzk + 'g%d' % g for g in range(4)]
                    if 3 in hooks:
                        hooks[3]()
                    ckpt('s3')
                    Ec, eck = tb()
                    for hk in range(2):
                        stc, stck = bank()
                        for b in range(NSB):
                            ov = stc[:, 0:256].rearrange("p (g c) -> p g c", g=4)[:, :, b * 4:(b + 1) * 4]
                            mm(ov, ckTb[hk * 64:(hk + 1) * 64, b, :], qrTs[hk * 64:(hk + 1) * 64, :, b * 4:(b + 1) * 4], True, True,
                               r=['ckTb', 'qrTs'], w=[stck])
                        act(Ec[:, hk * 256:(hk + 1) * 256], stc[:, 0:256], AF.Exp, r=[stck], w=[eck + 'h%d' % hk], scale=0.125)
                        yield
                    tten('pool', Ec[:, :], Ec[:, :], smask[:, 0, :], ALU.mult, r=[eck + 'h0', eck + 'h1', 'smask'], w=[eck])
                    En, enk = tb()
                    for hk in range(2):
                        stn, stnk = bank()
                        mm(stn[0:64, 0:256], krTs[hk * 64:(hk + 1) * 64, :], qrTs[hk * 64:(hk + 1) * 64, :, :], True, True,
                           r=['krTs', 'qrTs'], w=[stnk])
                        act(En[0:64, hk * 256:(hk + 1) * 256], stn[0:64, 0:256], AF.Exp, r=[stnk], w=[enk + 'h%d' % hk], scale=0.125)
                        yield
                    tten('pool', En[0:64, :], En[0:64, :], smask[0:64, 1, :], ALU.mult, r=[enk + 'h0', enk + 'h1', 'smask'], w=[enk])
                    dn, dnk = bank()
                    mm(dn[:, :], oneb1[0:64, :], En[0:64, :], True, False, r=['oneb1', enk], w=[dnk])
                    mm(dn[:, :], oneb1[:, :], Ec[:, :], False, True, r=['oneb1', eck], w=[dnk])
                    yield
                    ot, otk = bank()
                    for hk in range(2):
                        mm(ot[hk * 64:(hk + 1) * 64, 0:256], vnew[0:64, hk * 64:(hk + 1) * 64], En[0:64, hk * 256:(hk + 1) * 256], True, False,
                           r=['vnew', enk], w=[otk])
                        for b in range(NSB):
                            ov = ot[hk * 64:(hk + 1) * 64, 0:256].rearrange("p (g c) -> p g c", g=4)[:, :, b * 4:(b + 1) * 4]
                            ev = Ec[:, hk * 256:(hk + 1) * 256].rearrange("p (g c) -> p g c", g=4)[:, :, b * 4:(b + 1) * 4]
                            mm(ov, cvb[:, b, hk * 64:(hk + 1) * 64], ev, False, b == NSB - 1, r=['cvb', eck], w=[otk])
                    rd, rdk = tt()
                    yield
                    tten('dve', rd[:, 0:512], dn[:, :], esinkS[:, :], ALU.add, r=[dnk, 'esinkS'], w=[rdk])
                    ts('dve', rd[:, 0:512], rd[:, 0:512], 2.0, None, ALU.mult, None, r=[rdk], w=[rdk])
                    S.op('dve', lambda e: e.reciprocal(rd[:, 512:1024], rd[:, 0:512]), r=[rdk], w=[rdk + 'r'])
                    yield
                    for hk in range(2):
                        o32, o32k = tf()
                        tten('dve', o32[hk * 64:(hk + 1) * 64, 0:256], ot[hk * 64:(hk + 1) * 64, 0:256],
                             rd[hk * 64:(hk + 1) * 64, 512 + hk * 256: 512 + (hk + 1) * 256], ALU.mult, r=[otk, rdk + 'r'], w=[o32k])
                        tten('dve', mixTs[hk * 64:(hk + 1) * 64, 4:8, :], o32[hk * 64:(hk + 1) * 64, 0:256].rearrange("p (g c) -> p g c", g=4),
                             gzT[hk * 64:(hk + 1) * 64, 0:256].rearrange("p (g c) -> p g c", g=4), ALU.mult, r=[o32k] + gzks, w=['mixTsa'])
                    yield

                def chainY():
                    ckpt('s4')
                    pmean, pmk = pstatA, 'pstatA'
                    pex2, pek = pstatB, 'pstatB'
                    for c in range(2):
                        bu, buk = proj(UOFF + c * 128)
                        yield
                        bg, bgk = proj(GOFF + c * 128)
                        th, thk = tf()
                        act(th[:, :N], bg[:, :N], AF.Tanh, r=[bgk], w=[thk], scale=0.5)
                        a32, a32k = tf()
                        stt(a32[:, 0:N], th[:, :N], 1.0, bu[:, :N], ALU.add, ALU.mult, r=[thk, buk], w=[a32k])
                        cpy('dve', aTs[:, c, :, 30:34], a32[:, 0:N].rearrange("p (b t) -> p b t", b=NSB), r=[a32k], w=['aTs'])
                        yield
                        ts('dve', a32[:, 64:128], a32[:, 0:64], 0.5, None, ALU.mult, None, r=[a32k], w=[a32k + 'o'])
                        out_evs.append(S.dma('sp', convs_new[l, :, c, :, :], a32[:, 64:128].rearrange("p (b t) -> p b t", b=NSB), r=[a32k + 'o']))
                        bc, bck = bank()
                        for j in range(31):
                            mm(bc[:, :N].rearrange("p (b t) -> p b t", b=NSB), diag[:, c, j, :], aTs[:, c, :, j:j + 4], j == 0, j == 30,
                               r=['diagA', 'diagP', 'aTs'], w=[bck])
                        cbias = pvec[:, l, 62 + c:63 + c]
                        yield
                        ts('dve', cf[:, c, :N], bc[:, :N], cbias, None, ALU.add, None, r=[bck, 'pvec'], w=['cf%d' % c])
                        cb_, cbk = tb()
                        act(cb_[:, :N], bc[:, :N], AF.Identity, r=[bck, 'pvec'], w=[cbk], bias=cbias)
                        cq_, cqk = tb()
                        act(cq_[:, :N], bc[:, :N], AF.Square, r=[bck, 'pvec'], w=[cqk], bias=cbias)
                        mm(pmean[0:1, :N], onesb[:, 0:1], cb_[:, :N], c == 0, c == 1, r=['onesb', cbk], w=[pmk])
                        mm(pex2[0:1, :N], onesb[:, 0:1], cq_[:, :N], c == 0, c == 1, r=['onesb', cqk], w=[pek])
                        yield
                    msq, msk = tf()
                    act(msq[0:1, :N], pmean[0:1, :N], AF.Square, r=[pmk], w=[msk])
                    mrow, mrk = tf()
                    act(mrow[0:1, :N], pmean[0:1, :N], AF.Copy, r=[pmk], w=[mrk])
                    var, vk = tf()
                    tten('dve', var[0:1, :N], pex2[0:1, :N], msq[0:1, :N], ALU.subtract, r=[pek, msk], w=[vk])
                    ts('dve', var[0:1, :N], var[0:1, :N], LN_EPS, None, ALU.add, None, r=[vk], w=[vk])
                    rs, rsk = tf()
                    S.op('dve', lambda e: e.reciprocal(rs[0:1, :N], var[0:1, :N]), r=[vk], w=[rsk])
                    act(rs[0:1, :N], rs[0:1, :N], AF.Sqrt, r=[rsk], w=[rsk])
                    yield
                    bmean, bmk = bank()
                    mm(bmean[:, :N], ones32[0:1, :], mrow[0:1, :N], True, True, r=['ones32', mrk], w=[bmk])
                    brs, brk = bank()
                    mm(brs[:, :N], ones32[0:1, :], rs[0:1, :N], True, True, r=['ones32', rsk], w=[brk])
                    dts = []
                    for c in range(2):
                        d_, dk2 = tf()
                        tten('dve', d_[:, :N], cf[:, c, :N], bmean[:, :N], ALU.subtract, r=['cf%d' % c, bmk], w=[dk2])
                        tten('dve', d_[:, :N], d_[:, :N], brs[:, :N], ALU.mult, r=[dk2, brk], w=[dk2])
                        dts.append((d_, dk2))
                    yield
                    for c in range(2):
                        d_, dk2 = dts[c]
                        sg, sgk = tf()
                        act(sg[:, :N], d_[:, :N], AF.Tanh, r=[dk2, 'dvh'], w=[sgk],
                            scale=dvh[:, l, 64 + c:65 + c], bias=dvh[:, l, 66 + c:67 + c])
                        y_, yk = tf()
                        ts('dve', y_[:, :N], d_[:, :N], dvq[:, l, 64 + c:65 + c], dvq[:, l, 66 + c:67 + c], ALU.mult, ALU.add,
                           r=[dk2, 'dvq'], w=[yk])
                        stt(y_[:, :N], sg[:, :N], 1.0, y_[:, :N], ALU.add, ALU.mult, r=[sgk, yk], w=[yk])
                        yield
                        bz, bzk = proj(ZCOFF + c * 128)
                        yield
                        tz, tzk = tf()
                        act(tz[:, :N], bz[:, :N], AF.Tanh, r=[bzk], w=[tzk], scale=0.5)
                        stt(tz[:, :N], tz[:, :N], 1.0, bz[:, :N], ALU.add, ALU.mult, r=[tzk, bzk], w=[tzk])
                        tten('pool', mixTs[:, c, :], y_[:, :N], tz[:, :N], ALU.mult, r=[yk, tzk], w=['mixTsc'])
                    ckpt('s5')
                    for c in range(2):
                        bp, bpk = proj(PVOFF + c * 128)
                        act(pvTs[:, c, :, 15:19], bp[:, :N].rearrange("p (b t) -> p b t", b=NSB), AF.Copy, r=[bpk], w=['pvTs'])
                        out_evs.append(S.dma('sp', pools_new[l, :, c, :, :], pvTs[:, c, :, 15:19], r=['pvTs']))
                        yield
                    for c in range(2):
                        x = pvTs[:, c, :, :]
                        sw, swk = tf()
                        swv = sw[:, 0:NSB * 19].rearrange("p (b w) -> p b w", b=NSB)
                        t2, t2k = tf()
                        t2v = t2[:, 0:NSB * 19].rearrange("p (b w) -> p b w", b=NSB)
                        tten('pool', t2v[:, :, 1:19], x[:, :, 1:19], x[:, :, 0:18], ALU.add, r=['pvTs'], w=[t2k])
                        if c == 0:
                            tten('pool', swv[0:64, :, 15:19], x[0:64, :, 15:19], x[0:64, :, 14:18], ALU.add, r=['pvTs'], w=[swk + 'a'])
                            tten('pool', swv[64:128, :, 15:19], t2v[64:128, :, 15:19], t2v[64:128, :, 13:17], ALU.add, r=[t2k], w=[swk + 'b'])
                        else:
                            t4, t4k = tf()
                            t4v = t4[:, 0:NSB * 19].rearrange("p (b w) -> p b w", b=NSB)
                            tten('pool', t4v[:, :, 3:19], t2v[:, :, 3:19], t2v[:, :, 1:17], ALU.add, r=[t2k], w=[t4k])
                            tten('pool', swv[0:64, :, 15:19], t4v[0:64, :, 15:19], t4v[0:64, :, 11:15], ALU.add, r=[t4k], w=[swk + 'a'])
                            t8, t8k = tf()
                            t8v = t8[:, 0:NSB * 19].rearrange("p (b w) -> p b w", b=NSB)
                            tten('pool', t8v[64:128, :, 7:19], t4v[64:128, :, 7:19], t4v[64:128, :, 3:15], ALU.add, r=[t4k], w=[t8k])
                            tten('pool', swv[64:128, :, 15:19], t8v[64:128, :, 15:19], t8v[64:128, :, 7:11], ALU.add, r=[t8k], w=[swk + 'b'])
                        pl, plk = tb()
                        yield
                        stt(pl[:, 0:N].rearrange("p (b t) -> p b t", b=NSB), swv[:, :, 15:19], misc[:, 1 + c:2 + c], x[:, :, 15:19],
                            ALU.mult, ALU.subtract, r=[swk + 'a', swk + 'b', 'misc', 'pvTs'], w=[plk])
                        by, byk = bank()
                        mm(by[:, :N], poolbd[:, l, c, :], pl[:, :N], True, True, r=['poolbd', plk], w=[byk])
                        bz, bzk = proj(ZPOFF + c * 128)
                        yield
                        tz, tzk = tf()
                        act(tz[:, :N], bz[:, :N], AF.Tanh, r=[bzk], w=[tzk], scale=0.5)
                        stt(tz[:, :N], tz[:, :N], 1.0, bz[:, :N], ALU.add, ALU.mult, r=[tzk, bzk], w=[tzk])
                        stt(mixTs[:, 2 + c, :], by[:, :N], dvh[:, l, 68 + c:69 + c], tz[:, :N], ALU.mult, ALU.mult,
                            r=[byk, 'dvh', tzk], w=['mixTsp'])
                    if 6 in hooks:
                        hooks[6]()
                    yield

                run_interleaved([('X', chainX()), ('Y', chainY())])
                ckpt('s6')
                gs, gsk = tt()
                for hf in range(2):
                    bk, bkey = bank()
                    for c4 in range(4):
                        ch = hf * 4 + c4
                        L_, lk_ = tf()
                        cpy('dve', L_[:, 0:64].rearrange("p (b t) -> p b t", b=NSB),
                            modT[:, l, 16 + ch, 1:17].unsqueeze(2).to_broadcast([128, NSB, 4]), r=['modT'], w=[lk_])
                        mm(bk[0:64, c4 * 128:(c4 + 1) * 128], L_[:, 0:64], ident32[:], True, True, r=[lk_, 'ident32'], w=[bkey])
                    act(gs[0:64, hf * 512:(hf + 1) * 512], bk[0:64, :], AF.Copy, r=[bkey], w=[gsk + 'g%d' % hf])
                T, tk_ = tt()
                for hf in range(2):
                    bo, bok = bank()
                    for kc in range(8):
                        mm(bo[0:64, :], mixTs[:, kc, :], wobf[:, kc, hf * 512:(hf + 1) * 512], kc == 0, kc == 7, r=['mixTsa', 'mixTsc', 'mixTsp', 'wout'], w=[bok])
                    tten('dve', T[0:64, hf * 512:(hf + 1) * 512], bo[0:64, :], gs[0:64, hf * 512:(hf + 1) * 512], ALU.mult,
                         r=[bok, gsk + 'g%d' % hf], w=[tk_ + 'h%d' % hf])
                tten('pool', T[0:64, :], T[0:64, :], xsm[0:64, :], ALU.add, r=[tk_ + 'h0', tk_ + 'h1', 'xsm'], w=[tk_])
                rstd, nmr, lk = ln_stats(T, 64, EPS2, [tk_])
                act(T[0:64, :], T[0:64, :], AF.Identity, r=[tk_] + lk, w=[tk_], scale=rstd, bias=nmr)
                tten('pool', T[0:64, :], T[0:64, :], lng[0:64, 0, :], ALU.mult, r=[tk_, 'lng'], w=[tk_])
                tten('pool', xsm[0:64, :], T[0:64, :], lng[0:64, 1, :], ALU.add, r=[tk_, 'lng'], w=['xsm'])

            xblk = [(xo[i], ['xo%dL' % i, 'xo%dR' % i]) for i in range(4)]
            sched_pl = [(p, l) for p in passes for l in range(nl)]
            carry_s1 = None
            diag_prebuilt = False
            pending_tails = None
            load_w_in(sched_pl[0][1], 1)
            load_w_in(sched_pl[0][1], 0)
            load_w_out(sched_pl[0][1])
            for si, (p, l) in enumerate(sched_pl):
                nxt = sched_pl[si + 1][1] if si + 1 < len(sched_pl) else None
                hooks = {}
                if nxt is not None:
                    hooks = {3: (lambda n=nxt: load_w_in(n, 1)), 6: (lambda n=nxt: load_w_in(n, 0))}
                if l == 0:
                    if p == 'H':
                        for i in range(4):
                            S.dma('sp', xo[i][:], xin_d[i * 128:(i + 1) * 128, :], w=['xo%dL' % i, 'xo%dR' % i])
                        S.dma('sp', xsm[0:64, :], xs_d, w=['xsm'])
                        S.dma('sp', [ropes[:, 0, :], ropes[:, 1, :]], [ropeCs_d, ropeSs_d], w=['ropes'])
                    else:
                        r0 = 512 + p * 512
                        for i in range(4):
                            S.dma('sp', xo[i][:], xin_d[r0 + i * 128: r0 + (i + 1) * 128, :], w=['xo%dL' % i, 'xo%dR' % i])
                ckpt('wload')
                layer_setup(l, p == 'H', build_diag=not diag_prebuilt, load_lng_now=(pending_tails is None))
                lng_loaded = pending_tails is None
                diag_prebuilt = False
                ckpt('setup')
                if p == 'H':
                    blist = list(range(l, 4))
                    groups = []
                    while blist:
                        take = 2 if len(blist) % 2 == 0 else 1
                        groups.append(blist[:take]); blist = blist[take:]
                    glist = [dict(blocks=[xblk[i] for i in gb], tcol=gb[0] * 128, first=False, last=False, hooks=None,
                                  partial=(len(gb) == 1 and gb[0] == l)) for gb in groups]
                else:
                    glist = [dict(blocks=[xblk[2 * gi], xblk[2 * gi + 1]], tcol=512 + p * 512 + gi * 256,
                                  first=(p == 0 and gi == 0), last=(p == 3 and gi == 1), hooks=hooks if gi == 1 else None)
                             for gi in range(2)]
                for gi, g in enumerate(glist):
                    nx = None
                    if gi + 1 < len(glist):
                        nx = (l, glist[gi + 1]['blocks'])
                    elif p != 'H' and l + 1 < nl:
                        nx = (l + 1, [xblk[0], xblk[1]])
                    nxt_gen = make_s1(*nx) if nx is not None else None
                    dn_ = None
                    if p != 'H' and gi == len(glist) - 1 and nxt is not None:
                        dn_ = nxt
                        diag_prebuilt = True
                    def _after_xy(l_=l):
                        load_lng(l_)
                    need_lng = not lng_loaded
                    lng_loaded = True
                    defer_ = (p != 'H') and not (l == nl - 1 and gi == len(glist) - 1)
                    pending_tails = prompt_group(l, g['blocks'], g['tcol'], g['first'], g['last'], hooks=g['hooks'],
                                                 s1_done=carry_s1, next_s1=nxt_gen, diag_next=dn_,
                                                 pending=pending_tails, defer=defer_,
                                                 after_xy=_after_xy if need_lng else None, partial=g.get('partial', False))
                    carry_s1 = True if nx is not None else None
                if p == 'H':
                    ts('pool', cst[:, l, :, :], cst[:, l, :, :], misc[:, 0:1], 1.0, ALU.mult, ALU.mult, r=['cst%d' % l, 'misc'], w=['cst%d' % l])
                    ts('pool', pst[:, l, :, :], pst[:, l, :, :], misc[:, 0:1], 1.0, ALU.mult, ALU.mult, r=['pst%d' % l, 'misc'], w=['pst%d' % l])
                    sample_group(l, hooks)
                if nxt is not None:
                    load_w_out(nxt)
                if l == nl - 1:
                    if p == 'H':
                        out_evs.append(S.dma('sp', ys_d, xsm[0:64, :], r=['xsm']))
                    else:
                        for i in range(4):
                            out_evs.append(S.dma('sp', yp_d[p * 512 + i * 128: p * 512 + (i + 1) * 128, :], xo[i][:], r=['xo%dL' % i, 'xo%dR' % i]))
        except _Stop:
            pass
        S.finish(out_evs)
        stats = dict(ops=S.nops, waits=S.nwaits, same=S.nsame, skipped=S.nskip)
    return nc, stats


def _rope_tables(pos):
    half = 32
    inv_freq = (np.float32(10000.0) ** (-np.arange(half, dtype=np.float32) * np.float32(2.0 / 64))).astype(np.float32)
    ang = pos.astype(np.float32)[None, :] * inv_freq[:, None]
    c = np.cos(ang).astype(np.float32)
    s = np.sin(ang).astype(np.float32)
    return np.tile(c, (4, 1)), np.tile(s, (4, 1))


def make_in_maps(inp):
    f = np.float32
    perm = np.array([(hkv * 4 + g) * 64 + d for g in range(4) for hkv in range(2) for d in range(64)])
    w_in = np.ascontiguousarray(inp['w_in'], dtype=f).copy()
    w_in[:, :, 1280:1792] = inp['w_in'][:, :, 1280 + perm]
    w_in[:, :, 2048:2560] = inp['w_in'][:, :, 2048 + perm]
    w_out = np.ascontiguousarray(inp['w_out'], dtype=f).copy()
    w_out[:, 512:1024, :] = inp['w_out'][:, 512 + perm, :]
    w_mod = np.ascontiguousarray(inp['w_mod'], dtype=f)
    bmodT = np.ascontiguousarray(inp['b_mod'].reshape(NL, 24, 128).transpose(2, 0, 1), dtype=f)
    pvec = np.zeros((128, NL, 70), f)
    cw = inp['conv_w'].reshape(NL, 31, 2, 128)
    pvec[:, :, 0:62] = cw.transpose(3, 0, 2, 1).reshape(128, NL, 62)
    for i, nm in enumerate(['conv_b', 'cnorm_g', 'cnorm_b', 'pool_scale']):
        pvec[:, :, 62 + 2 * i: 64 + 2 * i] = inp[nm].reshape(NL, 2, 128).transpose(2, 0, 1)
    poolbd = np.zeros((128, NL, 2, 128), f)
    for c in range(2):
        for gl in range(2):
            poolbd[gl * 64:(gl + 1) * 64, :, c, gl * 64:(gl + 1) * 64] = inp['pool_w'][:, 2 * c + gl].transpose(1, 0, 2)
    lngb = np.ascontiguousarray(np.stack([inp['ln_g'], inp['ln_b']], axis=1), dtype=f)
    sinks = np.ascontiguousarray(inp['sinks'].reshape(1, NL * 8), dtype=f)
    ident = np.eye(128, dtype=f)
    prot = np.zeros((128, 128), f)
    for m in range(128):
        if m % 64 < 32:
            prot[m + 32, m] = -1.0
        else:
            prot[m - 32, m] = 1.0
    kk = np.arange(128)[:, None]
    qq = np.arange(128)[None, :]
    m_own = np.tile((qq >= kk).astype(f), (1, 4))
    m_prev = np.tile((kk > qq).astype(f), (1, 4))
    col_t = np.tile(np.arange(4), 128)[None, :]
    col_b = np.tile(np.repeat(np.arange(16), 4), 8)[None, :]
    smask = np.zeros((128, 2, 512), f)
    smask[:, 0, :] = (np.arange(128)[:, None] > col_t).astype(f)
    rb = np.repeat(np.arange(16), 4)[:, None]
    rj = np.tile(np.arange(4), 16)[:, None]
    smask[0:64, 1, :] = ((rb == col_b) & (rj <= col_t)).astype(f)
    ropeCs, ropeSs = _rope_tables(np.tile(PAST_LEN + np.arange(4), 16))
    wch = np.array([[2, 4], [8, 16]])
    in_maps = []
    for c in range(NCORE):
        bi, hf = c // 2, c % 2
        xin = np.zeros((2560, D), f)
        if hf == 1:
            xin[:] = inp['x_prompt'][bi, 2048 - 512:4096]
        else:
            xin[512:] = inp['x_prompt'][bi, 0:2048]
        pos = hf * 2048 - 512 + np.arange(2560)
        ropeC, ropeS = _rope_tables(pos)
        masks = (np.stack([m_own, m_prev, m_prev * f(hf)], axis=1) - f(1.0)) * f(240000.0)
        misc = np.zeros((128, 36), f)
        misc[:, 0] = hf
        for ch in range(2):
            for ph in range(2):
                w = wch[ch, ph]
                misc[ph * 64:(ph + 1) * 64, 1 + ch] = 1.0 / w
                misc[ph * 64:(ph + 1) * 64, 4 + ch * 16: 4 + (ch + 1) * 16] = (
                    1.0 if hf == 1 else (w / np.minimum(w, np.arange(16) + 1.0))[None, :])
        sb_ = slice(c * NSB, (c + 1) * NSB)
        crows = np.concatenate([inp['c_prompt'][bi:bi + 1], inp['c_sample'][sb_]], axis=0)
        cT = np.ascontiguousarray(crows.reshape(17, 8, 128).transpose(2, 1, 0), dtype=f)
        cc = inp['cache_conv'][:, sb_]
        cpl = inp['cache_pool'][:, sb_]
        ck = inp['cache_k'][:, sb_].reshape(NL, NSB, 128, 128)
        cv = inp['cache_v'][:, sb_].reshape(NL, NSB, 128, 128)
        m = {
            'xin': xin, 'xsin': np.ascontiguousarray(inp['x_sample'][sb_].reshape(NS, D), dtype=f),
            'w_in': w_in, 'w_out': w_out, 'w_mod': w_mod, 'cT': cT, 'bmodT': bmodT, 'pvec': pvec, 'poolbd': poolbd,
            'lngb': lngb, 'sinks': sinks, 'ident': ident, 'prot': prot, 'masks': np.ascontiguousarray(masks),
            'smask': smask, 'ropeC': ropeC, 'ropeS': ropeS, 'ropeCs': ropeCs, 'ropeSs': ropeSs, 'misc': misc,
            'cconvT': np.ascontiguousarray(cc.reshape(NL, NSB, 30, 2, 128).transpose(0, 4, 3, 1, 2), dtype=f),
            'cpoolT': np.ascontiguousarray(cpl.reshape(NL, NSB, 15, 2, 128).transpose(0, 4, 3, 1, 2), dtype=f),
            'ckT': np.ascontiguousarray(ck.transpose(0, 3, 1, 2), dtype=f),
            'cv': np.ascontiguousarray(cv.transpose(0, 2, 1, 3), dtype=f),
            'cconv_n': np.ascontiguousarray(cc, dtype=f), 'cpool_n': np.ascontiguousarray(cpl, dtype=f),
            'ck_n': np.ascontiguousarray(ck, dtype=f), 'cv_n': np.ascontiguousarray(cv, dtype=f),
        }
        in_maps.append(m)
    return in_maps


def assemble(res):
    f = np.float32
    B, SEQ = 4, 4096
    y_p = np.zeros((B, SEQ, D), f)
    y_s = np.zeros((128, 4, D), f)
    conv_p = np.zeros((NL, B, 30, 256), f)
    pool_p = np.zeros((NL, B, 15, 256), f)
    k_p = np.zeros((NL, B, 128, 2, 64), f)
    v_p = np.zeros((NL, B, 128, 2, 64), f)
    conv_s = np.zeros((NL, 128, 30, 256), f)
    pool_s = np.zeros((NL, 128, 15, 256), f)
    k_s = np.zeros((NL, 128, 128, 2, 64), f)
    v_s = np.zeros((NL, 128, 128, 2, 64), f)
    for c in range(NCORE):
        r = res[c]
        bi, hf = c // 2, c % 2
        y_p[bi, hf * 2048:(hf + 1) * 2048] = r['y_p']
        sb_ = slice(c * NSB, (c + 1) * NSB)
        y_s[sb_] = r['y_s'].reshape(NSB, 4, D)
        if hf == 1:
            conv_p[:, bi] = r['convp'].transpose(0, 3, 2, 1).reshape(NL, 30, 256)
            pool_p[:, bi] = r['poolp'].transpose(0, 3, 2, 1).reshape(NL, 15, 256)
            k_p[:, bi] = r['kp'].transpose(0, 2, 1).reshape(NL, 128, 2, 64)
            v_p[:, bi] = r['vp'].reshape(NL, 128, 2, 64)
        conv_s[:, sb_, 0:26] = r['convs_old']
        conv_s[:, sb_, 26:30] = r['convs_new'].transpose(0, 3, 4, 2, 1).reshape(NL, NSB, 4, 256)
        pool_s[:, sb_, 0:11] = r['pools_old']
        pool_s[:, sb_, 11:15] = r['pools_new'].transpose(0, 3, 4, 2, 1).reshape(NL, NSB, 4, 256)
        k_s[:, sb_, 0:124] = r['ks_old'].reshape(NL, NSB, 124, 2, 64)
        k_s[:, sb_, 124:128] = r['ks_new'].transpose(0, 2, 1).reshape(NL, NSB, 4, 2, 64)
        v_s[:, sb_, 0:124] = r['vs_old'].reshape(NL, NSB, 124, 2, 64)
        v_s[:, sb_, 124:128] = r['vs_new'].reshape(NL, NSB, 4, 2, 64)
    return (y_p, y_s, conv_p, pool_p, k_p, v_p, conv_s, pool_s, k_s, v_s)


def kernel(**inputs):
    inp = {k: np.asarray(v) for k, v in inputs.items()}
    in_maps = make_in_maps(inp)
    nc, _ = build_program()
    res = run_bass_kernel_spmd(nc, in_maps, core_ids=list(range(NCORE)))
    return assemble(res.results)
```

```python
import contextlib
import re
import numpy as np
import concourse.bass as bass
import concourse.mybir as mybir
from concourse.bass_utils import run_bass_kernel_spmd

F32 = mybir.dt.float32
BF16 = mybir.dt.bfloat16
AF = mybir.ActivationFunctionType
ALU = mybir.AluOpType

D = 1024
NL = 4
NCORE = 8
OWN_BLOCKS = 16
HALO_BLOCKS = 4
NSB = 16
NS = 64
PAST_LEN = 8192
LN_EPS = 1e-5
ALPHA = (2.0 * NL) ** 0.25
EPS2 = LN_EPS / (ALPHA * ALPHA)
UOFF, GOFF, ZCOFF, PVOFF, ZPOFF = 0, 256, 512, 768, 1024
QOFF, KOFF, VOFF, ZAOFF = 1280, 1792, 1920, 2048
FW = 304
GB = 2
SAME_DIST = 0


class Sched:
    NDS = 24

    def __init__(self, nc, es, same_engine_sync=True):
        self.nc = nc
        self.eng = {'pe': nc.tensor, 'act': nc.scalar, 'dve': nc.vector, 'pool': nc.gpsimd, 'sp': nc.sync}
        self.sem = {e: es.enter_context(nc.semaphore('s_' + e)) for e in self.eng}
        self.cnt = {e: 0 for e in self.eng}
        self.waited = {e: {} for e in self.eng}
        self.dsem = [es.enter_context(nc.semaphore('d%d' % i)) for i in range(self.NDS)]
        self.dtarget = [0] * self.NDS
        self.dnext = 0
        self.lastw = {}
        self.readers = {}
        self.children = {}
        self.same = same_engine_sync
        self.nwaits = 0
        self.nops = 0
        self.nsame = 0
        self.nskip = 0
        self.same_dist = SAME_DIST

    def _wait(self, e, ev):
        if ev[0] == 'e':
            _, f, k = ev
            if f == e and (e == 'pe' or not self.same):
                return
            if f == e:
                self.nsame += 1
                if self.same_dist and self.cnt[e] - k >= self.same_dist:
                    self.nskip += 1
                    return
            key = f
            sem = self.sem[f]
        else:
            _, idx, k = ev
            key = ('d', idx)
            sem = self.dsem[idx]
        if self.waited[e].get(key, 0) >= k:
            return
        self.eng[e].wait_ge(sem, k)
        self.waited[e][key] = k
        self.nwaits += 1

    _BASE = re.compile(r'^(fr|br|tr|sm)\d+')

    def _base(self, x):
        m = self._BASE.match(x)
        return m.group(0) if m else x

    def _deps(self, e, r, w):
        extra = []
        for x in w:
            if self._base(x) == x and x in self.children:
                extra.extend(self.children[x])
        if extra:
            w = list(w) + extra
        best = {}
        def add(ev):
            if ev is None:
                return
            key = ev[1] if ev[0] == 'e' else ('d', ev[1])
            if key not in best or best[key][2] < ev[2]:
                best[key] = ev
        for x in r:
            add(self.lastw.get(x))
        for x in w:
            add(self.lastw.get(x))
            for ev in self.readers.get(x, {}).values():
                add(ev)
        for ev in best.values():
            self._wait(e, ev)

    def _record(self, ev, r, w):
        for x in list(r) + list(w):
            b = self._base(x)
            if b != x:
                self.children.setdefault(b, set()).add(x)
        key = ev[1] if ev[0] == 'e' else ('d', ev[1])
        for x in w:
            self.lastw[x] = ev
            self.readers[x] = {}
        for x in r:
            d = self.readers.setdefault(x, {})
            if key not in d or d[key][2] < ev[2]:
                d[key] = ev

    def op(self, e, fn, r=(), w=()):
        px = [x for x in r if x.startswith('pb') or x.startswith('pstat') or x.startswith('ptp')]
        if px:
            w = list(w) + px
        self._deps(e, r, w)
        inst = fn(self.eng[e])
        self.cnt[e] += 1
        inst.then_inc(self.sem[e], 1)
        ev = ('e', e, self.cnt[e])
        self._record(ev, r, w)
        self.nops += 1
        return ev

    def dma(self, q, out, in_, r=(), w=(), **kw):
        self._deps(q, r, w)
        idx = self.dnext
        self.dnext = (self.dnext + 1) % self.NDS
        if self.dtarget[idx] > 0:
            self._wait(q, ('d', idx, self.dtarget[idx]))
        pairs = list(zip(out, in_)) if isinstance(out, (list, tuple)) else [(out, in_)]
        for o, i in pairs:
            self.eng[q].dma_start(out=o, in_=i, **kw).then_inc(self.dsem[idx], 16)
            self.dtarget[idx] += 16
        ev = ('d', idx, self.dtarget[idx])
        self._record(ev, r, w)
        return ev

    def finish(self, evs):
        for ev in evs:
            self._wait('sp', ev)


class _Stop(Exception):
    pass


def build_program(passes=('H', 0, 1, 2, 3), nl=NL, same_engine_sync=True, stop_at=None):
    nc = bass.Bass("TRN2", target_bir_lowering=False)
    seen = []

    def ckpt(name):
        seen.append(name)
        if stop_at is not None and name == stop_at:
            raise _Stop()

    def din(name, shape):
        return nc.dram_tensor(name, list(shape), F32, kind="ExternalInput").ap()

    def dout(name, shape):
        return nc.dram_tensor(name, list(shape), F32, kind="ExternalOutput").ap()

    xin_d = din("xin", (2560, D))
    xs_d = din("xsin", (NS, D))
    w_in_d = din("w_in", (NL, D, 2560))
    w_out_d = din("w_out", (NL, D, D))
    w_mod_d = din("w_mod", (NL, D, 3 * D))
    cT_d = din("cT", (128, 8, 17))
    bmodT_d = din("bmodT", (128, NL, 24))
    pvec_d = din("pvec", (128, NL, 70))
    poolbd_d = din("poolbd", (128, NL, 2, 128))
    lngb_d = din("lngb", (NL, 2, D))
    sinks_d = din("sinks", (1, NL * 8))
    ident_d = din("ident", (128, 128))
    prot_d = din("prot", (128, 128))
    masks_d = din("masks", (128, 3, 512))
    smask_d = din("smask", (128, 2, 512))
    ropeC_d = din("ropeC", (128, 2560))
    ropeS_d = din("ropeS", (128, 2560))
    ropeCs_d = din("ropeCs", (128, NS))
    ropeSs_d = din("ropeSs", (128, NS))
    misc_d = din("misc", (128, 36))
    cconvT_d = din("cconvT", (NL, 128, 2, NSB, 30))
    cpoolT_d = din("cpoolT", (NL, 128, 2, NSB, 15))
    ckT_d = din("ckT", (NL, 128, NSB, 128))
    cv_d = din("cv", (NL, 128, NSB, 128))
    cconv_n = din("cconv_n", (NL, NSB, 30, 256))
    cpool_n = din("cpool_n", (NL, NSB, 15, 256))
    ck_n = din("ck_n", (NL, NSB, 128, 128))
    cv_n = din("cv_n", (NL, NSB, 128, 128))

    yp_d = dout("y_p", (OWN_BLOCKS * 128, D))
    ys_d = dout("y_s", (NS, D))
    convp_d = dout("convp", (NL, 128, 2, 30))
    poolp_d = dout("poolp", (NL, 128, 2, 15))
    kp_d = dout("kp", (NL, 128, 128))
    vp_d = dout("vp", (NL, 128, 128))
    convs_old = dout("convs_old", (NL, NSB, 26, 256))
    pools_old = dout("pools_old", (NL, NSB, 11, 256))
    ks_old = dout("ks_old", (NL, NSB, 124, 128))
    vs_old = dout("vs_old", (NL, NSB, 124, 128))
    convs_new = dout("convs_new", (NL, 128, 2, NSB, 4))
    pools_new = dout("pools_new", (NL, 128, 2, NSB, 4))
    ks_new = dout("ks_new", (NL, 128, NS))
    vs_new = dout("vs_new", (NL, NS, 128))

    out_evs = []
    with contextlib.ExitStack() as es:
        S = Sched(nc, es, same_engine_sync)

        def sb(name, shape, dt=F32):
            return es.enter_context(nc.sbuf_tensor("sb_" + name, list(shape), dt))

        def ps(name, shape, dt=F32):
            return es.enter_context(nc.psum_tensor("ps_" + name, list(shape), dt))

        xo = [sb("xo%d" % i, (128, D)) for i in range(4)]
        xsm = sb("xsm", (128, D))
        wbf = sb("wbf", (128, 8, 2560), BF16)
        wobf = sb("wobf", (128, 8, D), BF16)
        hT = sb("hT", (128, 8, 256), BF16)
        mixT = sb("mixT", (128, 8, 256), BF16)
        xn_all = sb("xn_all", (128, GB, D), BF16)
        qrT = sb("qrT", (128, 4, 256), BF16)
        krT = sb("krT", (128, 256), BF16)
        vext = sb("vext", (128, GB, 2, 65), BF16)
        diag = sb("diag", (128, 2, 31, 128), BF16)
        cf = sb("cf", (128, 2, 256))
        aT = sb("aT", (128, 2, 30 + 256), BF16)
        pvT = sb("pvT", (128, 2, 15 + 256))
        ropet = sb("ropet", (128, 2, 256))
        lng = sb("lng", (128, 2, D))
        g1a = sb("g1a", (128, D))
        modT = sb("modT", (128, NL, 24, 17))
        masks = sb("masks", (128, 3, 512), BF16)
        smask = sb("smask", (128, 2, 512), BF16)
        ident32 = sb("ident32", (128, 128))
        ones32 = sb("ones32", (128, 128))
        identb = sb("identb", (128, 128), BF16)
        protb = sb("protb", (128, 128), BF16)
        onesb = sb("onesb", (128, 128), BF16)
        oneb1 = sb("oneb1", (128, 128), BF16)
        pvec = sb("pvec", (128, NL, 70))
        dvh = sb("dvh", (128, NL, 70))
        dvq = sb("dvq", (128, NL, 70))
        poolbd = sb("poolbd", (128, NL, 2, 128), BF16)
        bmodT = sb("bmodT", (128, NL, 24))
        esink = sb("esink", (128, NL * 8))
        esink2 = sb("esink2", (128, NL * 8))
        misc = sb("misc", (128, 36))
        mhalf = sb("mhalf", (128, 4))
        cT = sb("cT", (128, 8, 17))
        siluT = sb("siluT", (128, 8, 17), BF16)
        kst = sb("kst", (128, NL, 128), BF16)
        vst = sb("vst", (128, NL, 2, 65), BF16)
        cst = sb("cst", (128, NL, 2, 30), BF16)
        pst = sb("pst", (128, NL, 2, 15))
        small = sb("small", (128, 16, 16))
        hTs = sb("hTs", (128, 8, NS), BF16)
        mixTs = sb("mixTs", (128, 8, NS), BF16)
        qrTs = sb("qrTs", (128, 4, NS), BF16)
        krTs = sb("krTs", (128, NS), BF16)
        vnew = sb("vnew", (128, 128), BF16)
        ckTb = sb("ckTb", (128, NSB, 128), BF16)
        cvb = sb("cvb", (128, NSB, 128), BF16)
        aTs = sb("aTs", (128, 2, NSB, 34), BF16)
        pvTs = sb("pvTs", (128, 2, NSB, 19))
        ropes = sb("ropes", (128, 2, NS))
        esinkS = sb("esinkS", (128, 512))

        NFR = 16
        fring = [sb("fr%d" % i, (128, FW)) for i in range(NFR)]
        NBR = 11
        bring = [sb("br%d" % i, (128, 512), BF16) for i in range(NBR)]
        tring = [sb("tr%d" % i, (128, D)) for i in range(2)]
        NPB = 4
        pbank = [ps("pb%d" % i, (128, 512)) for i in range(NPB)]
        pstatA = ps("pstatA", (128, 512))
        pstatB = ps("pstatB", (128, 512))
        ptp = [ps("ptp%d" % i, (128, 1024), BF16) for i in range(2)]
        ctr = {'f': 0, 'b': 0, 't': 0, 'p': 0, 'tp': 0, 's': 0,
               'fX': 0, 'fY': 0, 'bX': 0, 'bY': 0, 'pX': 0, 'pY': 0, 'pZ0': 0, 'pZ1': 0}
        ctx = {'chain': None}
        FR = {'X': list(range(0, 8)), 'Y': list(range(8, 16))}
        BR = {'X': list(range(0, 8)), 'Y': list(range(8, 11))}
        PB = {'X': [0, 1], 'Y': [2, 3], 'Z0': [0, 1], 'Z1': [2, 3]}

        def tf():
            ch = ctx['chain']
            if ch is None:
                i = ctr['f']; ctr['f'] = (i + 1) % NFR
            else:
                j = ctr['f' + ch]; ctr['f' + ch] = (j + 1) % len(FR[ch]); i = FR[ch][j]
            return fring[i], 'fr%d' % i

        def tb():
            ch = ctx['chain']
            if ch is None:
                i = ctr['b']; ctr['b'] = (i + 1) % NBR
            else:
                j = ctr['b' + ch]; ctr['b' + ch] = (j + 1) % len(BR[ch]); i = BR[ch][j]
            return bring[i], 'br%d' % i

        def tt():
            i = ctr['t']; ctr['t'] = (i + 1) % 2
            return tring[i], 'tr%d' % i

        def bank():
            ch = ctx['chain']
            if ch is None:
                i = ctr['p']; ctr['p'] = (i + 1) % NPB
            else:
                j = ctr['p' + ch]; ctr['p' + ch] = (j + 1) % len(PB[ch]); i = PB[ch][j]
            return pbank[i], 'pb%d' % i

        def run_interleaved(chains):
            live = list(chains)
            while live:
                for item in list(live):
                    ctx['chain'] = item[0]
                    try:
                        for _ in range(2 if item[0] == 'Y' else 1):
                            next(item[1])
                    except StopIteration:
                        live.remove(item)
            ctx['chain'] = None

        def tph():
            i = ctr['tp']; ctr['tp'] = (i + 1) % 2
            return ptp[i][:, 0:512], 'ptp%d' % i

        def sm():
            i = ctr['s']; ctr['s'] = (i + 1) % 16
            return small[:, i, :], 'sm%d' % i

        def mm(out, lhsT, rhs, start, stop, r, w):
            S.op('pe', lambda e: e.matmul(out, lhsT=lhsT, rhs=rhs, start=start, stop=stop, skip_group_check=True), r=r, w=w)

        def tr(out, in_, r, w):
            S.op('pe', lambda e: e.transpose(out, in_, identb[:in_.shape[0], :in_.shape[0]]), r=list(r) + ['identb'], w=w)

        def act(out, in_, func, r, w, scale=None, bias=None):
            kw = {}
            if scale is not None:
                kw['scale'] = scale
            if bias is not None:
                kw['bias'] = bias
            S.op('act', lambda e: e.activation(out=out, in_=in_, func=func, **kw), r=r, w=w)

        def ts(eng, out, in0, s1, s2, op0, op1, r, w):
            if s2 is None:
                S.op(eng, lambda e: e.tensor_scalar(out=out, in0=in0, scalar1=s1, scalar2=None, op0=op0), r=r, w=w)
            else:
                S.op(eng, lambda e: e.tensor_scalar(out=out, in0=in0, scalar1=s1, scalar2=s2, op0=op0, op1=op1), r=r, w=w)

        def stt(out, in0, scalar, in1, op0, op1, r, w):
            S.op('dve', lambda e: e.scalar_tensor_tensor(out=out, in0=in0, scalar=scalar, in1=in1, op0=op0, op1=op1), r=r, w=w)

        def tten(eng, out, in0, in1, op, r, w):
            S.op(eng, lambda e: e.tensor_tensor(out=out, in0=in0, in1=in1, op=op), r=r, w=w)

        def cpy(eng, out, in_, r, w):
            S.op(eng, lambda e: e.tensor_copy(out=out, in_=in_), r=r, w=w)

        def mset(eng, ap, val, w):
            S.op(eng, lambda e: e.memset(ap, val), w=w)

        try:
            S.dma('sp', ident32[:], ident_d, w=['ident32'])
            S.dma('pool', identb[:], ident_d, w=['identb'])
            S.dma('pool', protb[:], prot_d, w=['protb'])
            S.dma('pool', masks[:], masks_d, w=['masks'])
            S.dma('pool', smask[:], smask_d, w=['smask'])
            S.dma('sp', pvec[:], pvec_d, w=['pvec'])
            S.dma('pool', poolbd[:], poolbd_d, w=['poolbd'])
            S.dma('sp', bmodT[:], bmodT_d, w=['bmodT'])
            S.dma('sp', misc[:], misc_d, w=['misc'])
            S.dma('sp', cT[:], cT_d, w=['cT'])
            S.dma('sp', esink[:], sinks_d.partition_broadcast(128), w=['esink'])
            mset('pool', ones32[:], 1.0, ['ones32'])
            mset('pool', onesb[:], 1.0 / 256.0, ['onesb'])
            mset('pool', oneb1[:], 1.0, ['oneb1'])
            mset('pool', mhalf[:], -0.5, ['mhalf'])
            mset('pool', kst[:], 0.0, ['kst%d' % l for l in range(NL)])
            mset('pool', vst[:], 0.0, ['vst%d' % l for l in range(NL)])
            mset('pool', cst[:], 0.0, ['cst%d' % l for l in range(NL)])
            mset('pool', pst[:], 0.0, ['pst%d' % l for l in range(NL)])
            mset('pool', vext[:], 1.0, ['vext%d' % b for b in range(GB)])
            mset('pool', xsm[:], 0.0, ['xsm'])
            act(esink[:], esink[:], AF.Exp, r=['esink'], w=['esink'])
            ts('dve', esink2[:], esink[:], 2.0, None, ALU.mult, None, r=['esink'], w=['esink2'])
            ts('dve', dvh[:], pvec[:], 0.5, None, ALU.mult, None, r=['pvec'], w=['dvh'])
            ts('dve', dvq[:], pvec[:], 0.25, None, ALU.mult, None, r=['pvec'], w=['dvq'])

            ckpt('consts')
            if 'H' in passes:
                for l in range(nl):
                    out_evs.append(S.dma('sp', convs_old[l], cconv_n[l, :, 4:30, :]))
                    out_evs.append(S.dma('sp', pools_old[l], cpool_n[l, :, 4:15, :]))
                    out_evs.append(S.dma('sp', ks_old[l], ck_n[l, :, 4:128, :]))
                    out_evs.append(S.dma('sp', vs_old[l], cv_n[l, :, 4:128, :]))

            ckpt('oldcopy')
            t1, k1 = tf()
            act(t1[:, 0:136], cT[:].rearrange("p a b -> p (a b)"), AF.Tanh, r=['cT'], w=[k1], scale=0.5)
            t2, k2 = tf()
            stt(t2[:, 0:136], t1[:, 0:136], 1.0, cT[:].rearrange("p a b -> p (a b)"), ALU.add, ALU.mult, r=[k1, 'cT'], w=[k2])
            ts('dve', siluT[:].rearrange("p a b -> p (a b)"), t2[:, 0:136], 0.5, None, ALU.mult, None, r=[k2], w=['siluT'])
            pieces = [(l, j) for l in range(nl) for j in range(3)]
            for idx, (l, j) in enumerate(pieces):
                half = idx % 2
                key = 'winA' if half == 0 else 'winB'
                dst = wbf[:, :, half * 1280: half * 1280 + 1024]
                src = w_mod_d[l].rearrange("(kc p) n -> p kc n", p=128)[:, :, j * 1024:(j + 1) * 1024]
                S.dma('pool', [dst[:, 0:4, :], dst[:, 4:8, :]], [src[:, 0:4, :], src[:, 4:8, :]], w=[key])
                for oc in range(8):
                    bk, bkey = bank()
                    for kc in range(8):
                        mm(bk[:, 0:17], dst[:, kc, oc * 128:(oc + 1) * 128], siluT[:, kc, :], kc == 0, kc == 7,
                           r=[key, 'siluT'], w=[bkey])
                    ch = j * 8 + oc
                    if j == 0:
                        ts('dve', modT[:, l, ch, :], bk[:, 0:17], bmodT[:, l, ch:ch + 1], None, ALU.add, None,
                           r=[bkey, 'bmodT'], w=['modT'])
                    else:
                        ts('dve', modT[:, l, ch, :], bk[:, 0:17], bmodT[:, l, ch:ch + 1], 1.0, ALU.add, ALU.add,
                           r=[bkey, 'bmodT'], w=['modT'])
            for l in range(nl):
                ts('dve', modT[:, l, 16:24, :], modT[:, l, 16:24, :], 1.0 / ALPHA, None, ALU.mult, None, r=['modT'], w=['modT'])

            ckpt('prologue')
            def load_w_in(l, half):
                key = 'winA' if half == 0 else 'winB'
                src = w_in_d[l].rearrange("(kc p) n -> p kc n", p=128)[:, :, half * 1280:(half + 1) * 1280]
                dst = wbf[:, :, half * 1280:(half + 1) * 1280]
                S.dma('pool', [dst[:, 0:4, :], dst[:, 4:8, :]], [src[:, 0:4, :], src[:, 4:8, :]], w=[key])

            def load_w_out(l):
                src = w_out_d[l].rearrange("(kc p) n -> p kc n", p=128)
                S.dma('pool', [wobf[:, 0:4, :], wobf[:, 4:8, :]], [src[:, 0:4, :], src[:, 4:8, :]], w=['wout'])

            def diag_gen(l, gate):
                while gate is not None and not gate[0]:
                    yield
                k = 0
                for c in range(2):
                    for j in range(31):
                        if k % 2 == 0:
                            act(diag[:, c, j, :], ident32[:], AF.Identity, r=['ident32', 'dvh'], w=['diagA'],
                                scale=dvh[:, l, c * 31 + j: c * 31 + j + 1])
                        else:
                            ts('pool', diag[:, c, j, :], ident32[:], dvh[:, l, c * 31 + j: c * 31 + j + 1], 1.0, ALU.mult, ALU.mult,
                               r=['ident32', 'dvh'], w=['diagP'])
                        k += 1
                        if k % 4 == 0:
                            yield

            def load_lng(l):
                S.dma('sp', [lng[:, 0, :], lng[:, 1, :]],
                      [lngb_d[l, 0:1, :].partition_broadcast(128), lngb_d[l, 1:2, :].partition_broadcast(128)], w=['lng'])

            def layer_setup(l, with_sample, build_diag=True, load_lng_now=True):
                if load_lng_now:
                    load_lng(l)
                for hf in range(2):
                    bk, bkey = bank()
                    for c4 in range(4):
                        ch = hf * 4 + c4
                        dg, dk = tf()
                        ts('dve', dg[:, 0:128], ident32[:], modT[:, l, 16 + ch, 0:1], None, ALU.mult, None,
                           r=['ident32', 'modT'], w=[dk])
                        mm(bk[:, c4 * 128:(c4 + 1) * 128], ones32[:], dg[:, 0:128], True, True, r=['ones32', dk], w=[bkey])
                    act(g1a[:, hf * 512:(hf + 1) * 512], bk[:], AF.Copy, r=[bkey], w=['g1a'])
                if build_diag:
                    for _ in diag_gen(l, None):
                        pass
                if with_sample:
                    S.dma('pool', ckTb[:], ckT_d[l], w=['ckTb'])
                    S.dma('pool', cvb[:], cv_d[l], w=['cvb'])
                    S.dma('pool', aTs[:, :, :, 0:30], cconvT_d[l], w=['aTs'])
                    S.dma('sp', pvTs[:, :, :, 0:15], cpoolT_d[l], w=['pvTs'])
                    ts('pool', aTs[:, :, :, 0:30], aTs[:, :, :, 0:30], 2.0, 1.0, ALU.mult, ALU.mult, r=['aTs'], w=['aTs'])
                    cpy('dve', esinkS[:].rearrange("p (h r) -> p h r", h=8),
                        esink[:, l * 8:(l + 1) * 8].unsqueeze(2).to_broadcast([128, 8, 64]), r=['esink'], w=['esinkS'])

            def ln_stats(src, nrows, eps, rkeys):
                st, sk = sm()
                S.op('dve', lambda e: e.bn_stats(st[:nrows, 0:6], src[:nrows, 0:512]), r=rkeys, w=[sk + 'a'])
                S.op('dve', lambda e: e.bn_stats(st[:nrows, 6:12], src[:nrows, 512:1024]), r=rkeys, w=[sk + 'b'])
                mv, mk = sm()
                S.op('dve', lambda e: e.bn_aggr(mv[:nrows, 0:2], st[:nrows, 0:12].rearrange("p (a b) -> p a b", a=2)),
                     r=[sk + 'a', sk + 'b'], w=[mk])
                ts('pool', mv[:nrows, 2:3], mv[:nrows, 1:2], eps, 1.0, ALU.add, ALU.mult, r=[mk], w=[mk + 'v'])
                tten('pool', mv[:nrows, 3:4], mv[:nrows, 2:3], mhalf[:nrows, 0:1], ALU.pow, r=[mk + 'v', 'mhalf'], w=[mk + 'r'])
                stt(mv[:nrows, 4:5], mv[:nrows, 0:1], -1.0, mv[:nrows, 3:4], ALU.mult, ALU.mult, r=[mk, mk + 'r'], w=[mk + 'n'])
                return mv[:nrows, 3:4], mv[:nrows, 4:5], [mk + 'r', mk + 'n']

            def make_s1(l, blocks):
                nb = len(blocks)
                N = nb * 128
                for bi, (xb, xk) in enumerate(blocks):
                    rstd, nmr, lk = ln_stats(xb, 128, LN_EPS, list(xk))
                    act(xn_all[:, bi, :], xb[:], AF.Identity, r=list(xk) + lk, w=['xn%d' % bi], scale=rstd, bias=nmr)
                    yield
                for kc in range(8):
                    tp, tk = tph()
                    for bi in range(nb):
                        tr(tp[:, bi * 128:(bi + 1) * 128], xn_all[:, bi, kc * 128:(kc + 1) * 128], r=['xn%d' % bi], w=[tk])
                    if kc % 2 == 0:
                        ts('dve', hT[:, kc, :N], tp[:, :N], modT[:, l, 8 + kc, 0:1], modT[:, l, kc, 0:1], ALU.mult, ALU.add,
                           r=[tk, 'modT'], w=['hT'])
                    else:
                        act(hT[:, kc, :N], tp[:, :N], AF.Identity, r=[tk, 'modT'], w=['hT'],
                            scale=modT[:, l, 8 + kc, 0:1], bias=modT[:, l, kc, 0:1])
                    yield

            def prompt_group(l, blocks, tcol, first_seq, last_seq, hooks=None, s1_done=None, next_s1=None, diag_next=None, pending=None, defer=False, after_xy=None, partial=False):
                conv_done = [False]
                hooks = hooks or {}
                nb = len(blocks)
                N = nb * 128
                wA, wB = 'winA', 'winB'
                if s1_done is None:
                    for _ in make_s1(l, blocks):
                        pass

                def proj(off):
                    bk, bkey = bank()
                    for kc in range(8):
                        mm(bk[:, :N], wbf[:, kc, off:off + 128], hT[:, kc, :N], kc == 0, kc == 7,
                           r=[wA if off < 1280 else wB, 'hT'], w=[bkey])
                    return bk, bkey

                def chainX():
                    S.dma('sp', [ropet[:, 0, :N], ropet[:, 1, :N]], [ropeC_d[:, tcol:tcol + N], ropeS_d[:, tcol:tcol + N]], w=['ropet'])
                    ckpt('p1')
                    for c in ([4] if partial else range(5)):
                        bk, bkey = proj(QOFF + c * 128)
                        A_, ak = tf()
                        tten('dve', A_[:, :N], bk[:, :N], ropet[:, 0, :N], ALU.mult, r=[bkey, 'ropet'], w=[ak])
                        B_, bkk = tb()
                        tten('dve', B_[:, :N], bk[:, :N], ropet[:, 1, :N], ALU.mult, r=[bkey, 'ropet'], w=[bkk])
                        yield
                        pr, prk = bank()
                        mm(pr[:, :N], protb[:], B_[:, :N], True, True, r=['protb', bkk], w=[prk])
                        if c < 4:
                            tten('dve', qrT[:, c, :N], pr[:, :N], A_[:, :N], ALU.add, r=[prk, ak], w=['qrT'])
                            yield
                        else:
                            tten('dve', krT[:, :N], pr[:, :N], A_[:, :N], ALU.add, r=[prk, ak], w=['krT'])
                            if last_seq:
                                ko, kok = tf()
                                tten('dve', ko[:, 0:128], pr[:, N - 128:N], A_[:, N - 128:N], ALU.add, r=[prk, ak], w=[kok])
                                out_evs.append(S.dma('sp', kp_d[l], ko[:, 0:128], r=[kok]))
                    ckpt('p2')
                    gz = []
                    for bi in range(nb):
                        bk, bkey = bank()
                        for kc in range(8):
                            mm(bk[:, 0:128], hT[:, kc, bi * 128:(bi + 1) * 128], wbf[:, kc, VOFF:VOFF + 128], kc == 0, kc == 7,
                               r=[wB, 'hT'], w=[bkey])
                        act(vext[:, bi, :, 0:64], bk[:, 0:128].rearrange("p (h d) -> p h d", h=2), AF.Copy, r=[bkey], w=['vext%d' % bi])
                        yield
                        if last_seq and bi == nb - 1:
                            vo, vok = tf()
                            cpy('dve', vo[:, 0:128], bk[:, 0:128], r=[bkey], w=[vok])
                            out_evs.append(S.dma('sp', vp_d[l], vo[:, 0:128], r=[vok]))
                        if partial:
                            continue
                        bz, bzk = bank()
                        for kc in range(8):
                            mm(bz[:, :], hT[:, kc, bi * 128:(bi + 1) * 128], wbf[:, kc, ZAOFF:ZAOFF + 512], kc == 0, kc == 7,
                               r=[wB, 'hT'], w=[bzk])
                        gpair = []
                        for hh in range(2):
                            g_, gk_ = tf()
                            act(g_[:, 0:256], bz[:, hh * 256:(hh + 1) * 256], AF.Tanh, r=[bzk], w=[gk_], scale=0.5)
                            stt(g_[:, 0:256], g_[:, 0:256], 1.0, bz[:, hh * 256:(hh + 1) * 256], ALU.add, ALU.mult, r=[gk_, bzk], w=[gk_])
                            gpair.append((g_, gk_))
                        gz.append(tuple(gpair))
                        yield
                    if 3 in hooks:
                        hooks[3]()
                    ckpt('p3')
                    for bi in ([] if partial else range(nb)):
                        if bi == 0:
                            kprev = lambda hk: kst[hk * 64:(hk + 1) * 64, l, :]
                            kprev_key = 'kst%d' % l
                            vprev = lambda hk: vst[:, l, hk, :]
                            vprev_key = 'vst%d' % l
                        else:
                            kprev = lambda hk, b_=bi: krT[hk * 64:(hk + 1) * 64, (b_ - 1) * 128:b_ * 128]
                            kprev_key = 'krT'
                            vprev = lambda hk, b_=bi: vext[:, b_ - 1, hk, :]
                            vprev_key = 'vext%d' % (bi - 1)
                        mprev = masks[:, 2, :] if (first_seq and bi == 0) else masks[:, 1, :]
                        E = {}
                        for hk in range(2):
                            for kb in range(2):
                                st_, stk = bank()
                                lhs = kprev(hk) if kb == 0 else krT[hk * 64:(hk + 1) * 64, bi * 128:(bi + 1) * 128]
                                lk_ = kprev_key if kb == 0 else 'krT'
                                mm(st_[:, :], lhs, qrT[hk * 64:(hk + 1) * 64, :, bi * 128:(bi + 1) * 128], True, False,
                                   r=[lk_, 'qrT'], w=[stk])
                                mm(st_[:, :], identb[:], mprev if kb == 0 else masks[:, 0, :], False, True,
                                   r=['identb', 'masks'], w=[stk])
                                e_, ek = tb()
                                act(e_[:, :], st_[:, :], AF.Exp, r=[stk], w=[ek], scale=0.125)
                                E[(hk, kb)] = (e_, ek)
                                yield
                        den, dk_ = sm()
                        Ob = []
                        for hk in range(2):
                            o_, ok_ = bank()
                            ov = o_[:, 0:260].rearrange("p (g d) -> p g d", g=4)
                            for g in range(4):
                                e0, e0k = E[(hk, 0)]
                                e1, e1k = E[(hk, 1)]
                                mm(ov[:, g, :], e0[:, g * 128:(g + 1) * 128], vprev(hk), True, False, r=[e0k, vprev_key], w=[ok_])
                                mm(ov[:, g, :], e1[:, g * 128:(g + 1) * 128], vext[:, bi, hk, :], False, True,
                                   r=[e1k, 'vext%d' % bi], w=[ok_])
                            stt(den[:, hk * 4:(hk + 1) * 4], ov[:, :, 64], 2.0, esink2[:, l * 8 + hk * 4: l * 8 + hk * 4 + 4], ALU.mult, ALU.add,
                                r=[ok_, 'esink2'], w=[dk_ + 'h%d' % hk])
                            Ob.append((ov, ok_))
                            yield
                        S.op('dve', lambda e: e.reciprocal(den[:, 8:16], den[:, 0:8]), r=[dk_ + 'h0', dk_ + 'h1'], w=[dk_ + 'r'])
                        yield
                        yc, yck = tb()
                        yield
                        ycv = yc[:, :].rearrange("p (g h d) -> p g h d", g=4, h=2)
                        for hk in range(2):
                            ov, ok_ = Ob[hk]
                            o32, o32k = tf()
                            tten('dve', o32[:, 0:256].rearrange("p (g d) -> p g d", g=4), ov[:, :, 0:64],
                                 den[:, 8 + hk * 4: 12 + hk * 4].unsqueeze(2).to_broadcast([128, 4, 64]), ALU.mult,
                                 r=[ok_, dk_ + 'r'], w=[o32k])
                            for gp in range(2):
                                gt_, gk_ = gz[bi][gp]
                                tten('dve', ycv[:, 2 * gp:2 * gp + 2, hk, :], o32[:, gp * 128:(gp + 1) * 128].rearrange("p (g d) -> p g d", g=2),
                                     gt_[:, 0:256].rearrange("p (g h d) -> p g h d", g=2, h=2)[:, :, hk, :], ALU.mult,
                                     r=[o32k, gk_], w=[yck])
                        tp, tk = tph()
                        yield
                        for c in range(4):
                            tr(tp[:, c * 128:(c + 1) * 128], yc[:, c * 128:(c + 1) * 128], r=[yck], w=[tk])
                        act(mixT[:, 4:8, bi * 128:(bi + 1) * 128], tp[:, :].rearrange("p (c t) -> p c t", c=4), AF.Copy,
                            r=[tk], w=['mixTa'])
                    ckpt('p4')
                    cpy('pool', kst[:, l, :], krT[:, N - 128:N], r=['krT'], w=['kst%d' % l])
                    cpy('pool', vst[:, l, :, :], vext[:, nb - 1, :, :], r=['vext%d' % (nb - 1)], w=['vst%d' % l])

                    yield

                def chainY():
                    cpy('pool', aT[:, :, 0:30], cst[:, l, :, :], r=['cst%d' % l], w=['aT'])
                    pmean, pmk = pstatA, 'pstatA'
                    pex2, pek = pstatB, 'pstatB'
                    for c in range(2):
                        bu, buk = proj(UOFF + c * 128)
                        yield
                        bg, bgk = proj(GOFF + c * 128)
                        th, thk = tf()
                        act(th[:, :N], bg[:, :N], AF.Tanh, r=[bgk], w=[thk], scale=0.5)
                        stt(aT[:, c, 30:30 + N], th[:, :N], 1.0, bu[:, :N], ALU.add, ALU.mult, r=[thk, buk], w=['aT'])
                        yield
                        if last_seq:
                            a32, a32k = tf()
                            stt(a32[:, 0:30], th[:, N - 30:N], 1.0, bu[:, N - 30:N], ALU.add, ALU.mult, r=[thk, buk], w=[a32k])
                            ts('dve', a32[:, 32:62], a32[:, 0:30], 0.5, None, ALU.mult, None, r=[a32k], w=[a32k + 'o'])
                            out_evs.append(S.dma('sp', convp_d[l, :, c, :], a32[:, 32:62], r=[a32k + 'o']))
                        if partial:
                            continue
                        bc, bck = bank()
                        for j in range(31):
                            mm(bc[:, :N], diag[:, c, j, :], aT[:, c, j:j + N], j == 0, j == 30, r=['diagA', 'diagP', 'aT'], w=[bck])
                        cbias = pvec[:, l, 62 + c:63 + c]
                        yield
                        ts('dve', cf[:, c, :N], bc[:, :N], cbias, None, ALU.add, None, r=[bck, 'pvec'], w=['cf%d' % c])
                        cb_, cbk = tb()
                        act(cb_[:, :N], bc[:, :N], AF.Identity, r=[bck, 'pvec'], w=[cbk], bias=cbias)
                        cq_, cqk = tb()
                        act(cq_[:, :N], bc[:, :N], AF.Square, r=[bck, 'pvec'], w=[cqk], bias=cbias)
                        mm(pmean[0:1, :N], onesb[:, 0:1], cb_[:, :N], c == 0, c == 1, r=['onesb', cbk], w=[pmk])
                        mm(pex2[0:1, :N], onesb[:, 0:1], cq_[:, :N], c == 0, c == 1, r=['onesb', cqk], w=[pek])
                        yield
                    cpy('pool', cst[:, l, :, :], aT[:, :, N:N + 30], r=['aT'], w=['cst%d' % l])
                    conv_done[0] = True
                    if partial:
                        cpy('pool', pvT[:, :, 0:15], pst[:, l, :, :], r=['pst%d' % l], w=['pvT'])
                        for c in range(2):
                            bp, bpk = proj(PVOFF + c * 128)
                            act(pvT[:, c, 15:15 + N], bp[:, :N], AF.Copy, r=[bpk], w=['pvT'])
                            yield
                        cpy('pool', pst[:, l, :, :], pvT[:, :, N:N + 15], r=['pvT'], w=['pst%d' % l])
                        if 6 in hooks:
                            hooks[6]()
                        return
                    msq, msk = tf()
                    act(msq[0:1, :N], pmean[0:1, :N], AF.Square, r=[pmk], w=[msk])
                    mrow, mrk = tf()
                    act(mrow[0:1, :N], pmean[0:1, :N], AF.Copy, r=[pmk], w=[mrk])
                    var, vk = tf()
                    tten('dve', var[0:1, :N], pex2[0:1, :N], msq[0:1, :N], ALU.subtract, r=[pek, msk], w=[vk])
                    ts('dve', var[0:1, :N], var[0:1, :N], LN_EPS, None, ALU.add, None, r=[vk], w=[vk])
                    rs, rsk = tf()
                    S.op('dve', lambda e: e.reciprocal(rs[0:1, :N], var[0:1, :N]), r=[vk], w=[rsk])
                    act(rs[0:1, :N], rs[0:1, :N], AF.Sqrt, r=[rsk], w=[rsk])
                    yield
                    bmean, bmk = bank()
                    mm(bmean[:, :N], ones32[0:1, :], mrow[0:1, :N], True, True, r=['ones32', mrk], w=[bmk])
                    brs, brk = bank()
                    mm(brs[:, :N], ones32[0:1, :], rs[0:1, :N], True, True, r=['ones32', rsk], w=[brk])
                    yield
                    dts = []
                    for c in range(2):
                        d_, dk2 = tf()
                        tten('dve', d_[:, :N], cf[:, c, :N], bmean[:, :N], ALU.subtract, r=['cf%d' % c, bmk], w=[dk2])
                        tten('dve', d_[:, :N], d_[:, :N], brs[:, :N], ALU.mult, r=[dk2, brk], w=[dk2])
                        dts.append((d_, dk2))
                    yield
                    for c in range(2):
                        d_, dk2 = dts[c]
                        sg, sgk = tf()
                        act(sg[:, :N], d_[:, :N], AF.Tanh, r=[dk2, 'dvh'], w=[sgk],
                            scale=dvh[:, l, 64 + c:65 + c], bias=dvh[:, l, 66 + c:67 + c])
                        y_, yk = tf()
                        ts('dve', y_[:, :N], d_[:, :N], dvq[:, l, 64 + c:65 + c], dvq[:, l, 66 + c:67 + c], ALU.mult, ALU.add,
                           r=[dk2, 'dvq'], w=[yk])
                        stt(y_[:, :N], sg[:, :N], 1.0, y_[:, :N], ALU.add, ALU.mult, r=[sgk, yk], w=[yk])
                        yield
                        bz, bzk = proj(ZCOFF + c * 128)
                        yield
                        tz, tzk = tf()
                        act(tz[:, :N], bz[:, :N], AF.Tanh, r=[bzk], w=[tzk], scale=0.5)
                        stt(tz[:, :N], tz[:, :N], 1.0, bz[:, :N], ALU.add, ALU.mult, r=[tzk, bzk], w=[tzk])
                        tten('pool', mixT[:, c, :N], y_[:, :N], tz[:, :N], ALU.mult, r=[yk, tzk], w=['mixTc'])
                        yield

                    ckpt('p5')
                    cpy('pool', pvT[:, :, 0:15], pst[:, l, :, :], r=['pst%d' % l], w=['pvT'])
                    for c in range(2):
                        bp, bpk = proj(PVOFF + c * 128)
                        act(pvT[:, c, 15:15 + N], bp[:, :N], AF.Copy, r=[bpk], w=['pvT'])
                        yield
                    yield from pool_mix(l, N, pvT, 'pvT', first_seq, proj, mixT, 'mixTp', lambda ap: ap)
                    cpy('pool', pst[:, l, :, :], pvT[:, :, N:N + 15], r=['pvT'], w=['pst%d' % l])
                    if last_seq:
                        out_evs.append(S.dma('sp', poolp_d[l], pvT[:, :, N:N + 15], r=['pvT']))

                    if 6 in hooks:
                        hooks[6]()
                    yield

                xy = [('X', chainX()), ('Y', chainY())]
                if diag_next is not None:
                    xy.append(('D', diag_gen(diag_next, conv_done)))
                for ti, tg in enumerate(pending or []):
                    xy.append(('T%d' % ti, tg))
                run_interleaved(xy)
                if after_xy is not None:
                    after_xy()
                ckpt('p6')
                if partial:
                    if next_s1 is not None:
                        run_interleaved([('W', next_s1)])
                    return None
                chains = [('Z%d' % bi, out_head(l, mixT[:, :, bi * 128:(bi + 1) * 128], ['mixTa', 'mixTc', 'mixTp'], xb, xk, 128, g1a, 'g1a'))
                          for bi, (xb, xk) in enumerate(blocks)]
                if next_s1 is not None:
                    chains.append(('W', next_s1))
                run_interleaved(chains)
                tails = [out_tail(l, xb, xk, 128) for (xb, xk) in blocks]
                if defer:
                    return tails
                run_interleaved([('T%d' % ti, tg) for ti, tg in enumerate(tails)])
                return None

                ckpt('p7')

            def pool_mix(l, N, pv, pvk, first_seq, proj, mix, mixk, fv):
                W = 15 + N
                for c in range(2):
                    x = pv[:, c, :]
                    sw, swk = tf()
                    if c == 0:
                        t2, t2k = tf()
                        tten('pool', t2[:, 1:W], x[:, 1:W], x[:, 0:W - 1], ALU.add, r=[pvk], w=[t2k])
                        tten('pool', sw[0:64, 15:W], x[0:64, 15:W], x[0:64, 14:W - 1], ALU.add, r=[pvk], w=[swk + 'a'])
                        tten('pool', sw[64:128, 15:W], t2[64:128, 15:W], t2[64:128, 13:W - 2], ALU.add, r=[t2k], w=[swk + 'b'])
                    else:
                        t2, t2k = tf()
                        tten('pool', t2[:, 1:W], x[:, 1:W], x[:, 0:W - 1], ALU.add, r=[pvk], w=[t2k])
                        t4, t4k = tf()
                        tten('pool', t4[:, 3:W], t2[:, 3:W], t2[:, 1:W - 2], ALU.add, r=[t2k], w=[t4k])
                        tten('pool', sw[0:64, 15:W], t4[0:64, 15:W], t4[0:64, 11:W - 4], ALU.add, r=[t4k], w=[swk + 'a'])
                        t8, t8k = tf()
                        tten('pool', t8[64:128, 7:W], t4[64:128, 7:W], t4[64:128, 3:W - 4], ALU.add, r=[t4k], w=[t8k])
                        tten('pool', sw[64:128, 15:W], t8[64:128, 15:W], t8[64:128, 7:W - 8], ALU.add, r=[t8k], w=[swk + 'b'])
                    rk = [swk + 'a', swk + 'b']
                    if first_seq:
                        tten('pool', sw[:, 15:31], sw[:, 15:31], misc[:, 4 + c * 16: 4 + (c + 1) * 16], ALU.mult,
                             r=rk + ['misc'], w=[swk + 'c'])
                        rk = rk + [swk + 'c']
                    yield
                    pl, plk = tb()
                    stt(pl[:, :N], sw[:, 15:W], misc[:, 1 + c:2 + c], x[:, 15:W], ALU.mult, ALU.subtract,
                        r=rk + ['misc', pvk], w=[plk])
                    by, byk = bank()
                    mm(by[:, :N], poolbd[:, l, c, :], pl[:, :N], True, True, r=['poolbd', plk], w=[byk])
                    yield
                    bz, bzk = proj(ZPOFF + c * 128)
                    yield
                    tz, tzk = tf()
                    act(tz[:, :N], bz[:, :N], AF.Tanh, r=[bzk], w=[tzk], scale=0.5)
                    stt(tz[:, :N], tz[:, :N], 1.0, bz[:, :N], ALU.add, ALU.mult, r=[tzk, bzk], w=[tzk])
                    stt(mix[:, 2 + c, :N], by[:, :N], dvh[:, l, 68 + c:69 + c], tz[:, :N], ALU.mult, ALU.mult,
                        r=[byk, 'dvh', tzk], w=[mixk])
                    yield

            def out_head(l, mixblk, mixk, xb, xk, nrows, gtile, gk):
                T, tk_ = tt()
                for hf in range(2):
                    bo, bok = bank()
                    for kc in range(8):
                        mm(bo[:nrows, :], mixblk[:, kc, :], wobf[:, kc, hf * 512:(hf + 1) * 512], kc == 0, kc == 7,
                           r=list(mixk) + ['wout'], w=[bok])
                    yield
                    tten('dve', T[:nrows, hf * 512:(hf + 1) * 512], bo[:nrows, :], gtile[:nrows, hf * 512:(hf + 1) * 512], ALU.mult,
                         r=[bok, gk], w=[tk_ + 'h%d' % hf])
                    tten('dve', xb[:nrows, hf * 512:(hf + 1) * 512], T[:nrows, hf * 512:(hf + 1) * 512], xb[:nrows, hf * 512:(hf + 1) * 512],
                         ALU.add, r=[tk_ + 'h%d' % hf, xk[hf]], w=[xk[hf]])
                    yield

            def out_tail(l, xb, xk, nrows):
                rstd, nmr, lk = ln_stats(xb, nrows, EPS2, list(xk))
                yield
                act(xb[:nrows, :], xb[:nrows, :], AF.Identity, r=list(xk) + lk, w=list(xk), scale=rstd, bias=nmr)
                yield
                for hf in range(2):
                    tten('dve', xb[:nrows, hf * 512:(hf + 1) * 512], xb[:nrows, hf * 512:(hf + 1) * 512], lng[:nrows, 0, hf * 512:(hf + 1) * 512],
                         ALU.mult, r=[xk[hf], 'lng'], w=[xk[hf]])
                    yield
                for hf in range(2):
                    tten('dve', xb[:nrows, hf * 512:(hf + 1) * 512], xb[:nrows, hf * 512:(hf + 1) * 512], lng[:nrows, 1, hf * 512:(hf + 1) * 512],
                         ALU.add, r=[xk[hf], 'lng'], w=[xk[hf]])
                    yield

            def sample_group(l, hooks=None):
                hooks = hooks or {}
                N = NS
                wA, wB = 'winA', 'winB'
                rstd, nmr, lk = ln_stats(xsm, 64, LN_EPS, ['xsm'])
                act(xn_all[0:64, 0, :], xsm[0:64, :], AF.Identity, r=['xsm'] + lk, w=['xn0'], scale=rstd, bias=nmr)
                tp, tk = tph()
                for kc in range(8):
                    tr(tp[:, kc * 64:(kc + 1) * 64], xn_all[0:64, 0, kc * 128:(kc + 1) * 128], r=['xn0'], w=[tk])
                hv = hTs[:].rearrange("p k (b t) -> p k b t", b=NSB)
                tpv = tp[:, :].rearrange("p (k b t) -> p k b t", k=8, b=NSB)
                sc = modT[:, l, 8:16, 1:17].unsqueeze(3).to_broadcast([128, 8, NSB, 4])
                sh = modT[:, l, 0:8, 1:17].unsqueeze(3).to_broadcast([128, 8, NSB, 4])
                hf32, hfk = tt()
                hfv = hf32[:, 0:512].rearrange("p (k b t) -> p k b t", k=8, b=NSB)
                tten('dve', hfv, tpv, sc, ALU.mult, r=[tk, 'modT'], w=[hfk])
                tten('dve', hv, hfv, sh, ALU.add, r=[hfk, 'modT'], w=['hTs'])

                def proj(off):
                    bk, bkey = bank()
                    for kc in range(8):
                        mm(bk[:, :N], wbf[:, kc, off:off + 128], hTs[:, kc, :], kc == 0, kc == 7,
                           r=[wA if off < 1280 else wB, 'hTs'], w=[bkey])
                    return bk, bkey

                def chainX():
                    ckpt('s1')
                    for c in range(5):
                        bk, bkey = proj(QOFF + c * 128)
                        A_, ak = tf()
                        tten('dve', A_[:, :N], bk[:, :N], ropes[:, 0, :], ALU.mult, r=[bkey, 'ropes'], w=[ak])
                        B_, bkk = tb()
                        tten('dve', B_[:, :N], bk[:, :N], ropes[:, 1, :], ALU.mult, r=[bkey, 'ropes'], w=[bkk])
                        yield
                        pr, prk = bank()
                        mm(pr[:, :N], protb[:], B_[:, :N], True, True, r=['protb', bkk], w=[prk])
                        if c < 4:
                            tten('dve', qrTs[:, c, :], pr[:, :N], A_[:, :N], ALU.add, r=[prk, ak], w=['qrTs'])
                            yield
                        else:
                            tten('dve', krTs[:, :], pr[:, :N], A_[:, :N], ALU.add, r=[prk, ak], w=['krTs'])
                            ko, kok = tf()
                            tten('dve', ko[:, 0:N], pr[:, :N], A_[:, :N], ALU.add, r=[prk, ak], w=[kok])
                            out_evs.append(S.dma('sp', ks_new[l], ko[:, 0:N], r=[kok]))
                    ckpt('s2')
                    bk, bkey = bank()
                    for kc in range(8):
                        mm(bk[0:64, 0:128], hTs[:, kc, :], wbf[:, kc, VOFF:VOFF + 128], kc == 0, kc == 7, r=[wB, 'hTs'], w=[bkey])
                    act(vnew[0:64, :], bk[0:64, 0:128], AF.Copy, r=[bkey], w=['vnew'])
                    vo, vok = tf()
                    cpy('dve', vo[0:64, 0:128], bk[0:64, 0:128], r=[bkey], w=[vok])
                    out_evs.append(S.dma('sp', vs_new[l], vo[0:64, 0:128], r=[vok]))
                    yield
                    gzT, gzk = tf()
                    for g in range(4):
                        bz, bzk = proj(ZAOFF + g * 128)
                        th, thk = tf()
                        act(th[:, :N], bz[:, :N], AF.Tanh, r=[bzk], w=[thk], scale=0.5)
                        stt(gzT[:, g * 64:(g + 1) * 64], th[:, :N], 1.0, bz[:, :N], ALU.add, ALU.mult, r=[thk, bzk], w=[gzk + 'g%d' % g])
                        yield
                    gzks = [gzk + 'g%d' % g for g in range(4)]
                    if 3 in hooks:
                        hooks[3]()
                    ckpt('s3')
                    Ec, eck = tb()
                    for hk in range(2):
                        stc, stck = bank()
                        for b in range(NSB):
                            ov = stc[:, 0:256].rearrange("p (g c) -> p g c", g=4)[:, :, b * 4:(b + 1) * 4]
                            mm(ov, ckTb[hk * 64:(hk + 1) * 64, b, :], qrTs[hk * 64:(hk + 1) * 64, :, b * 4:(b + 1) * 4], True, True,
                               r=['ckTb', 'qrTs'], w=[stck])
                        act(Ec[:, hk * 256:(hk + 1) * 256], stc[:, 0:256], AF.Exp, r=[stck], w=[eck + 'h%d' % hk], scale=0.125)
                        yield
                    tten('pool', Ec[:, :], Ec[:, :], smask[:, 0, :], ALU.mult, r=[eck + 'h0', eck + 'h1', 'smask'], w=[eck])
                    En, enk = tb()
                    for hk in range(2):
                        stn, stnk = bank()
                        mm(stn[0:64, 0:256], krTs[hk * 64:(hk + 1) * 64, :], qrTs[hk * 64:(hk + 1) * 64, :, :], True, True,
                           r=['krTs', 'qrTs'], w=[stnk])
                        act(En[0:64, hk * 256:(hk + 1) * 256], stn[0:64, 0:256], AF.Exp, r=[stnk], w=[enk + 'h%d' % hk], scale=0.125)
                        yield
                    tten('pool', En[0:64, :], En[0:64, :], smask[0:64, 1, :], ALU.mult, r=[enk + 'h0', enk + 'h1', 'smask'], w=[enk])
                    dn, dnk = bank()
                    mm(dn[:, :], oneb1[0:64, :], En[0:64, :], True, False, r=['oneb1', enk], w=[dnk])
                    mm(dn[:, :], oneb1[:, :], Ec[:, :], False, True, r=['oneb1', eck], w=[dnk])
                    yield
                    ot, otk = bank()
                    for hk in range(2):
                        mm(ot[hk * 64:(hk + 1) * 64, 0:256], vnew[0:64, hk * 64:(hk + 1) * 64], En[0:64, hk * 256:(hk + 1) * 256], True, False,
                           r=['vnew', enk], w=[otk])
                        for b in range(NSB):
                            ov = ot[hk * 64:(hk + 1) * 64, 0:256].rearrange("p (g c) -> p g c", g=4)[:, :, b * 4:(b + 1) * 4]
                            ev = Ec[:, hk * 256:(hk + 1) * 256].rearrange("p (g c) -> p g c", g=4)[:, :, b * 4:(b + 1) * 4]
                            mm(ov, cvb[:, b, hk * 64:(hk + 1) * 64], ev, False, b == NSB - 1, r=['cvb', eck], w=[otk])
                    rd, rdk = tt()
                    yield
                    tten('dve', rd[:, 0:512], dn[:, :], esinkS[:, :], ALU.add, r=[dnk, 'esinkS'], w=[rdk])
                    ts('dve', rd[:, 0:512], rd[:, 0:512], 2.0, None, ALU.mult, None, r=[rdk], w=[rdk])
                    S.op('dve', lambda e: e.reciprocal(rd[:, 512:1024], rd[:, 0:512]), r=[rdk], w=[rdk + 'r'])
                    yield
                    for hk in range(2):
                        o32, o32k = tf()
                        tten('dve', o32[hk * 64:(hk + 1) * 64, 0:256], ot[hk * 64:(hk + 1) * 64, 0:256],
                             rd[hk * 64:(hk + 1) * 64, 512 + hk * 256: 512 + (hk + 1) * 256], ALU.mult, r=[otk, rdk + 'r'], w=[o32k])
                        tten('dve', mixTs[hk * 64:(hk + 1) * 64, 4:8, :], o32[hk * 64:(hk + 1) * 64, 0:256].rearrange("p (g c) -> p g c", g=4),
                             gzT[hk * 64:(hk + 1) * 64, 0:256].rearrange("p (g c) -> p g c", g=4), ALU.mult, r=[o32k] + gzks, w=['mixTsa'])
                    yield

                def chainY():
                    ckpt('s4')
                    pmean, pmk = pstatA, 'pstatA'
                    pex2, pek = pstatB, 'pstatB'
                    for c in range(2):
                        bu, buk = proj(UOFF + c * 128)
                        yield
                        bg, bgk = proj(GOFF + c * 128)
                        th, thk = tf()
                        act(th[:, :N], bg[:, :N], AF.Tanh, r=[bgk], w=[thk], scale=0.5)
                        a32, a32k = tf()
                        stt(a32[:, 0:N], th[:, :N], 1.0, bu[:, :N], ALU.add, ALU.mult, r=[thk, buk], w=[a32k])
                        cpy('dve', aTs[:, c, :, 30:34], a32[:, 0:N].rearrange("p (b t) -> p b t", b=NSB), r=[a32k], w=['aTs'])
                        yield
                        ts('dve', a32[:, 64:128], a32[:, 0:64], 0.5, None, ALU.mult, None, r=[a32k], w=[a32k + 'o'])
                        out_evs.append(S.dma('sp', convs_new[l, :, c, :, :], a32[:, 64:128].rearrange("p (b t) -> p b t", b=NSB), r=[a32k + 'o']))
                        bc, bck = bank()
                        for j in range(31):
                            mm(bc[:, :N].rearrange("p (b t) -> p b t", b=NSB), diag[:, c, j, :], aTs[:, c, :, j:j + 4], j == 0, j == 30,
                               r=['diagA', 'diagP', 'aTs'], w=[bck])
                        cbias = pvec[:, l, 62 + c:63 + c]
                        yield
                        ts('dve', cf[:, c, :N], bc[:, :N], cbias, None, ALU.add, None, r=[bck, 'pvec'], w=['cf%d' % c])
                        cb_, cbk = tb()
                        act(cb_[:, :N], bc[:, :N], AF.Identity, r=[bck, 'pvec'], w=[cbk], bias=cbias)
                        cq_, cqk = tb()
                        act(cq_[:, :N], bc[:, :N], AF.Square, r=[bck, 'pvec'], w=[cqk], bias=cbias)
                        mm(pmean[0:1, :N], onesb[:, 0:1], cb_[:, :N], c == 0, c == 1, r=['onesb', cbk], w=[pmk])
                        mm(pex2[0:1, :N], onesb[:, 0:1], cq_[:, :N], c == 0, c == 1, r=['onesb', cqk], w=[pek])
                        yield
                    msq, msk = tf()
                    act(msq[0:1, :N], pmean[0:1, :N], AF.Square, r=[pmk], w=[msk])
                    mrow, mrk = tf()
                    act(mrow[0:1, :N], pmean[0:1, :N], AF.Copy, r=[pmk], w=[mrk])
                    var, vk = tf()
                    tten('dve', var[0:1, :N], pex2[0:1, :N], msq[0:1, :N], ALU.subtract, r=[pek, msk], w=[vk])
                    ts('dve', var[0:1, :N], var[0:1, :N], LN_EPS, None, ALU.add, None, r=[vk], w=[vk])
                    rs, rsk = tf()
                    S.op('dve', lambda e: e.reciprocal(rs[0:1, :N], var[0:1, :N]), r=[vk], w=[rsk])
                    act(rs[0:1, :N], rs[0:1, :N], AF.Sqrt, r=[rsk], w=[rsk])
                    yield
                    bmean, bmk = bank()
                    mm(bmean[:, :N], ones32[0:1, :], mrow[0:1, :N], True, True, r=['ones32', mrk], w=[bmk])
                    brs, brk = bank()
                    mm(brs[:, :N], ones32[0:1, :], rs[0:1, :N], True, True, r=['ones32', rsk], w=[brk])
                    dts = []
                    for c in range(2):
                        d_, dk2 = tf()
                        tten('dve', d_[:, :N], cf[:, c, :N], bmean[:, :N], ALU.subtract, r=['cf%d' % c, bmk], w=[dk2])
                        tten('dve', d_[:, :N], d_[:, :N], brs[:, :N], ALU.mult, r=[dk2, brk], w=[dk2])
                        dts.append((d_, dk2))
                    yield
                    for c in range(2):
                        d_, dk2 = dts[c]
                        sg, sgk = tf()
                        act(sg[:, :N], d_[:, :N], AF.Tanh, r=[dk2, 'dvh'], w=[sgk],
                            scale=dvh[:, l, 64 + c:65 + c], bias=dvh[:, l, 66 + c:67 + c])
                        y_, yk = tf()
                        ts('dve', y_[:, :N], d_[:, :N], dvq[:, l, 64 + c:65 + c], dvq[:, l, 66 + c:67 + c], ALU.mult, ALU.add,
                           r=[dk2, 'dvq'], w=[yk])
                        stt(y_[:, :N], sg[:, :N], 1.0, y_[:, :N], ALU.add, ALU.mult, r=[sgk, yk], w=[yk])
                        yield
                        bz, bzk = proj(ZCOFF + c * 128)
                        yield
                        tz, tzk = tf()
                        act(tz[:, :N], bz[:, :N], AF.Tanh, r=[bzk], w=[tzk], scale=0.5)
                        stt(tz[:, :N], tz[:, :N], 1.0, bz[:, :N], ALU.add, ALU.mult, r=[tzk, bzk], w=[tzk])
                        tten('pool', mixTs[:, c, :], y_[:, :N], tz[:, :N], ALU.mult, r=[yk, tzk], w=['mixTsc'])
                    ckpt('s5')
                    for c in range(2):
                        bp, bpk = proj(PVOFF + c * 128)
                        act(pvTs[:, c, :, 15:19], bp[:, :N].rearrange("p (b t) -> p b t", b=NSB), AF.Copy, r=[bpk], w=['pvTs'])
                        out_evs.append(S.dma('sp', pools_new[l, :, c, :, :], pvTs[:, c, :, 15:19], r=['pvTs']))
                        yield
                    for c in range(2):
                        x = pvTs[:, c, :, :]
                        sw, swk = tf()
                        swv = sw[:, 0:NSB * 19].rearrange("p (b w) -> p b w", b=NSB)
                        t2, t2k = tf()
                        t2v = t2[:, 0:NSB * 19].rearrange("p (b w) -> p b w", b=NSB)
                        tten('pool', t2v[:, :, 1:19], x[:, :, 1:19], x[:, :, 0:18], ALU.add, r=['pvTs'], w=[t2k])
                        if c == 0:
                            tten('pool', swv[0:64, :, 15:19], x[0:64, :, 15:19], x[0:64, :, 14:18], ALU.add, r=['pvTs'], w=[swk + 'a'])
                            tten('pool', swv[64:128, :, 15:19], t2v[64:128, :, 15:19], t2v[64:128, :, 13:17], ALU.add, r=[t2k], w=[swk + 'b'])
                        else:
                            t4, t4k = tf()
                            t4v = t4[:, 0:NSB * 19].rearrange("p (b w) -> p b w", b=NSB)
                            tten('pool', t4v[:, :, 3:19], t2v[:, :, 3:19], t2v[:, :, 1:17], ALU.add, r=[t2k], w=[t4k])
                            tten('pool', swv[0:64, :, 15:19], t4v[0:64, :, 15:19], t4v[0:64, :, 11:15], ALU.add, r=[t4k], w=[swk + 'a'])
                            t8, t8k = tf()
                            t8v = t8[:, 0:NSB * 19].rearrange("p (b w) -> p b w", b=NSB)
                            tten('pool', t8v[64:128, :, 7:19], t4v[64:128, :, 7:19], t4v[64:128, :, 3:15], ALU.add, r=[t4k], w=[t8k])
                            tten('pool', swv[64:128, :, 15:19], t8v[64:128, :, 15:19], t8v[64:128, :, 7:11], ALU.add, r=[t8k], w=[swk + 'b'])
                        pl, plk = tb()
                        yield
                        stt(pl[:, 0:N].rearrange("p (b t) -> p b t", b=NSB), swv[:, :, 15:19], misc[:, 1 + c:2 + c], x[:, :, 15:19],
                            ALU.mult, ALU.subtract, r=[swk + 'a', swk + 'b', 'misc', 'pvTs'], w=[plk])
                        by, byk = bank()
                        mm(by[:, :N], poolbd[:, l, c, :], pl[:, :N], True, True, r=['poolbd', plk], w=[byk])
                        bz, bzk = proj(ZPOFF + c * 128)
                        yield
                        tz, tzk = tf()
                        act(tz[:, :N], bz[:, :N], AF.Tanh, r=[bzk], w=[tzk], scale=0.5)
                        stt(tz[:, :N], tz[:, :N], 1.0, bz[:, :N], ALU.add, ALU.mult, r=[tzk, bzk], w=[tzk])
                        stt(mixTs[:, 2 + c, :], by[:, :N], dvh[:, l, 68 + c:69 + c], tz[:, :N], ALU.mult, ALU.mult,
                            r=[byk, 'dvh', tzk], w=['mixTsp'])
                    if 6 in hooks:
                        hooks[6]()
                    yield

                run_interleaved([('X', chainX()), ('Y', chainY())])
                ckpt('s6')
                gs, gsk = tt()
                for hf in range(2):
                    bk, bkey = bank()
                    for c4 in range(4):
                        ch = hf * 4 + c4
                        L_, lk_ = tf()
                        cpy('dve', L_[:, 0:64].rearrange("p (b t) -> p b t", b=NSB),
                            modT[:, l, 16 + ch, 1:17].unsqueeze(2).to_broadcast([128, NSB, 4]), r=['modT'], w=[lk_])
                        mm(bk[0:64, c4 * 128:(c4 + 1) * 128], L_[:, 0:64], ident32[:], True, True, r=[lk_, 'ident32'], w=[bkey])
                    act(gs[0:64, hf * 512:(hf + 1) * 512], bk[0:64, :], AF.Copy, r=[bkey], w=[gsk + 'g%d' % hf])
                T, tk_ = tt()
                for hf in range(2):
                    bo, bok = bank()
                    for kc in range(8):
                        mm(bo[0:64, :], mixTs[:, kc, :], wobf[:, kc, hf * 512:(hf + 1) * 512], kc == 0, kc == 7, r=['mixTsa', 'mixTsc', 'mixTsp', 'wout'], w=[bok])
                    tten('dve', T[0:64, hf * 512:(hf + 1) * 512], bo[0:64, :], gs[0:64, hf * 512:(hf + 1) * 512], ALU.mult,
                         r=[bok, gsk + 'g%d' % hf], w=[tk_ + 'h%d' % hf])
                tten('pool', T[0:64, :], T[0:64, :], xsm[0:64, :], ALU.add, r=[tk_ + 'h0', tk_ + 'h1', 'xsm'], w=[tk_])
                rstd, nmr, lk = ln_stats(T, 64, EPS2, [tk_])
                act(T[0:64, :], T[0:64, :], AF.Identity, r=[tk_] + lk, w=[tk_], scale=rstd, bias=nmr)
                tten('pool', T[0:64, :], T[0:64, :], lng[0:64, 0, :], ALU.mult, r=[tk_, 'lng'], w=[tk_])
                tten('pool', xsm[0:64, :], T[0:64, :], lng[0:64, 1, :], ALU.add, r=[tk_, 'lng'], w=['xsm'])

            xblk = [(xo[i], ['xo%dL' % i, 'xo%dR' % i]) for i in range(4)]
            sched_pl = [(p, l) for p in passes for l in range(nl)]
            carry_s1 = None
            diag_prebuilt = False
            pending_tails = None
            load_w_in(sched_pl[0][1], 1)
            load_w_in(sched_pl[0][1], 0)
            load_w_out(sched_pl[0][1])
            for si, (p, l) in enumerate(sched_pl):
                nxt = sched_pl[si + 1][1] if si + 1 < len(sched_pl) else None
                hooks = {}
                if nxt is not None:
                    hooks = {3: (lambda n=nxt: load_w_in(n, 1)), 6: (lambda n=nxt: load_w_in(n, 0))}
                if l == 0:
                    if p == 'H':
                        for i in range(4):
                            S.dma('sp', xo[i][:], xin_d[i * 128:(i + 1) * 128, :], w=['xo%dL' % i, 'xo%dR' % i])
                        S.dma('sp', xsm[0:64, :], xs_d, w=['xsm'])
                        S.dma('sp', [ropes[:, 0, :], ropes[:, 1, :]], [ropeCs_d, ropeSs_d], w=['ropes'])
                    else:
                        r0 = 512 + p * 512
                        for i in range(4):
                            S.dma('sp', xo[i][:], xin_d[r0 + i * 128: r0 + (i + 1) * 128, :], w=['xo%dL' % i, 'xo%dR' % i])
                ckpt('wload')
                layer_setup(l, p == 'H', build_diag=not diag_prebuilt, load_lng_now=(pending_tails is None))
                lng_loaded = pending_tails is None
                diag_prebuilt = False
                ckpt('setup')
                if p == 'H':
                    blist = list(range(l, 4))
                    groups = []
                    while blist:
                        take = 2 if len(blist) % 2 == 0 else 1
                        groups.append(blist[:take]); blist = blist[take:]
                    glist = [dict(blocks=[xblk[i] for i in gb], tcol=gb[0] * 128, first=False, last=False, hooks=None,
                                  partial=(len(gb) == 1 and gb[0] == l)) for gb in groups]
                else:
                    glist = [dict(blocks=[xblk[2 * gi], xblk[2 * gi + 1]], tcol=512 + p * 512 + gi * 256,
                                  first=(p == 0 and gi == 0), last=(p == 3 and gi == 1), hooks=hooks if gi == 1 else None)
                             for gi in range(2)]
                for gi, g in enumerate(glist):
                    nx = None
                    if gi + 1 < len(glist):
                        nx = (l, glist[gi + 1]['blocks'])
                    elif p != 'H' and l + 1 < nl:
                        nx = (l + 1, [xblk[0], xblk[1]])
                    nxt_gen = make_s1(*nx) if nx is not None else None
                    dn_ = None
                    if p != 'H' and gi == len(glist) - 1 and nxt is not None:
                        dn_ = nxt
                        diag_prebuilt = True
                    def _after_xy(l_=l):
                        load_lng(l_)
                    need_lng = not lng_loaded
                    lng_loaded = True
                    defer_ = (p != 'H') and not (l == nl - 1 and gi == len(glist) - 1)
                    pending_tails = prompt_group(l, g['blocks'], g['tcol'], g['first'], g['last'], hooks=g['hooks'],
                                                 s1_done=carry_s1, next_s1=nxt_gen, diag_next=dn_,
                                                 pending=pending_tails, defer=defer_,
                                                 after_xy=_after_xy if need_lng else None, partial=g.get('partial', False))
                    carry_s1 = True if nx is not None else None
                if p == 'H':
                    ts('pool', cst[:, l, :, :], cst[:, l, :, :], misc[:, 0:1], 1.0, ALU.mult, ALU.mult, r=['cst%d' % l, 'misc'], w=['cst%d' % l])
                    ts('pool', pst[:, l, :, :], pst[:, l, :, :], misc[:, 0:1], 1.0, ALU.mult, ALU.mult, r=['pst%d' % l, 'misc'], w=['pst%d' % l])
                    sample_group(l, hooks)
                if nxt is not None:
                    load_w_out(nxt)
                if l == nl - 1:
                    if p == 'H':
                        out_evs.append(S.dma('sp', ys_d, xsm[0:64, :], r=['xsm']))
                    else:
                        for i in range(4):
                            out_evs.append(S.dma('sp', yp_d[p * 512 + i * 128: p * 512 + (i + 1) * 128, :], xo[i][:], r=['xo%dL' % i, 'xo%dR' % i]))
        except _Stop:
            pass
        S.finish(out_evs)
        stats = dict(ops=S.nops, waits=S.nwaits, same=S.nsame, skipped=S.nskip)
    return nc, stats


def _rope_tables(pos):
    half = 32
    inv_freq = (np.float32(10000.0) ** (-np.arange(half, dtype=np.float32) * np.float32(2.0 / 64))).astype(np.float32)
    ang = pos.astype(np.float32)[None, :] * inv_freq[:, None]
    c = np.cos(ang).astype(np.float32)
    s = np.sin(ang).astype(np.float32)
    return np.tile(c, (4, 1)), np.tile(s, (4, 1))


def make_in_maps(inp):
    f = np.float32
    perm = np.array([(hkv * 4 + g) * 64 + d for g in range(4) for hkv in range(2) for d in range(64)])
    w_in = np.ascontiguousarray(inp['w_in'], dtype=f).copy()
    w_in[:, :, 1280:1792] = inp['w_in'][:, :, 1280 + perm]
    w_in[:, :, 2048:2560] = inp['w_in'][:, :, 2048 + perm]
    w_out = np.ascontiguousarray(inp['w_out'], dtype=f).copy()
    w_out[:, 512:1024, :] = inp['w_out'][:, 512 + perm, :]
    w_mod = np.ascontiguousarray(inp['w_mod'], dtype=f)
    bmodT = np.ascontiguousarray(inp['b_mod'].reshape(NL, 24, 128).transpose(2, 0, 1), dtype=f)
    pvec = np.zeros((128, NL, 70), f)
    cw = inp['conv_w'].reshape(NL, 31, 2, 128)
    pvec[:, :, 0:62] = cw.transpose(3, 0, 2, 1).reshape(128, NL, 62)
    for i, nm in enumerate(['conv_b', 'cnorm_g', 'cnorm_b', 'pool_scale']):
        pvec[:, :, 62 + 2 * i: 64 + 2 * i] = inp[nm].reshape(NL, 2, 128).transpose(2, 0, 1)
    poolbd = np.zeros((128, NL, 2, 128), f)
    for c in range(2):
        for gl in range(2):
            poolbd[gl * 64:(gl + 1) * 64, :, c, gl * 64:(gl + 1) * 64] = inp['pool_w'][:, 2 * c + gl].transpose(1, 0, 2)
    lngb = np.ascontiguousarray(np.stack([inp['ln_g'], inp['ln_b']], axis=1), dtype=f)
    sinks = np.ascontiguousarray(inp['sinks'].reshape(1, NL * 8), dtype=f)
    ident = np.eye(128, dtype=f)
    prot = np.zeros((128, 128), f)
    for m in range(128):
        if m % 64 < 32:
            prot[m + 32, m] = -1.0
        else:
            prot[m - 32, m] = 1.0
    kk = np.arange(128)[:, None]
    qq = np.arange(128)[None, :]
    m_own = np.tile((qq >= kk).astype(f), (1, 4))
    m_prev = np.tile((kk > qq).astype(f), (1, 4))
    col_t = np.tile(np.arange(4), 128)[None, :]
    col_b = np.tile(np.repeat(np.arange(16), 4), 8)[None, :]
    smask = np.zeros((128, 2, 512), f)
    smask[:, 0, :] = (np.arange(128)[:, None] > col_t).astype(f)
    rb = np.repeat(np.arange(16), 4)[:, None]
    rj = np.tile(np.arange(4), 16)[:, None]
    smask[0:64, 1, :] = ((rb == col_b) & (rj <= col_t)).astype(f)
    ropeCs, ropeSs = _rope_tables(np.tile(PAST_LEN + np.arange(4), 16))
    wch = np.array([[2, 4], [8, 16]])
    in_maps = []
    for c in range(NCORE):
        bi, hf = c // 2, c % 2
        xin = np.zeros((2560, D), f)
        if hf == 1:
            xin[:] = inp['x_prompt'][bi, 2048 - 512:4096]
        else:
            xin[512:] = inp['x_prompt'][bi, 0:2048]
        pos = hf * 2048 - 512 + np.arange(2560)
        ropeC, ropeS = _rope_tables(pos)
        masks = (np.stack([m_own, m_prev, m_prev * f(hf)], axis=1) - f(1.0)) * f(240000.0)
        misc = np.zeros((128, 36), f)
        misc[:, 0] = hf
        for ch in range(2):
            for ph in range(2):
                w = wch[ch, ph]
                misc[ph * 64:(ph + 1) * 64, 1 + ch] = 1.0 / w
                misc[ph * 64:(ph + 1) * 64, 4 + ch * 16: 4 + (ch + 1) * 16] = (
                    1.0 if hf == 1 else (w / np.minimum(w, np.arange(16) + 1.0))[None, :])
        sb_ = slice(c * NSB, (c + 1) * NSB)
        crows = np.concatenate([inp['c_prompt'][bi:bi + 1], inp['c_sample'][sb_]], axis=0)
        cT = np.ascontiguousarray(crows.reshape(17, 8, 128).transpose(2, 1, 0), dtype=f)
        cc = inp['cache_conv'][:, sb_]
        cpl = inp['cache_pool'][:, sb_]
        ck = inp['cache_k'][:, sb_].reshape(NL, NSB, 128, 128)
        cv = inp['cache_v'][:, sb_].reshape(NL, NSB, 128, 128)
        m = {
            'xin': xin, 'xsin': np.ascontiguousarray(inp['x_sample'][sb_].reshape(NS, D), dtype=f),
            'w_in': w_in, 'w_out': w_out, 'w_mod': w_mod, 'cT': cT, 'bmodT': bmodT, 'pvec': pvec, 'poolbd': poolbd,
            'lngb': lngb, 'sinks': sinks, 'ident': ident, 'prot': prot, 'masks': np.ascontiguousarray(masks),
            'smask': smask, 'ropeC': ropeC, 'ropeS': ropeS, 'ropeCs': ropeCs, 'ropeSs': ropeSs, 'misc': misc,
            'cconvT': np.ascontiguousarray(cc.reshape(NL, NSB, 30, 2, 128).transpose(0, 4, 3, 1, 2), dtype=f),
            'cpoolT': np.ascontiguousarray(cpl.reshape(NL, NSB, 15, 2, 128).transpose(0, 4, 3, 1, 2), dtype=f),
            'ckT': np.ascontiguousarray(ck.transpose(0, 3, 1, 2), dtype=f),
            'cv': np.ascontiguousarray(cv.transpose(0, 2, 1, 3), dtype=f),
            'cconv_n': np.ascontiguousarray(cc, dtype=f), 'cpool_n': np.ascontiguousarray(cpl, dtype=f),
            'ck_n': np.ascontiguousarray(ck, dtype=f), 'cv_n': np.ascontiguousarray(cv, dtype=f),
        }
        in_maps.append(m)
    return in_maps


def assemble(res):
    f = np.float32
    B, SEQ = 4, 4096
    y_p = np.zeros((B, SEQ, D), f)
    y_s = np.zeros((128, 4, D), f)
    conv_p = np.zeros((NL, B, 30, 256), f)
    pool_p = np.zeros((NL, B, 15, 256), f)
    k_p = np.zeros((NL, B, 128, 2, 64), f)
    v_p = np.zeros((NL, B, 128, 2, 64), f)
    conv_s = np.zeros((NL, 128, 30, 256), f)
    pool_s = np.zeros((NL, 128, 15, 256), f)
    k_s = np.zeros((NL, 128, 128, 2, 64), f)
    v_s = np.zeros((NL, 128, 128, 2, 64), f)
    for c in range(NCORE):
        r = res[c]
        bi, hf = c // 2, c % 2
        y_p[bi, hf * 2048:(hf + 1) * 2048] = r['y_p']
        sb_ = slice(c * NSB, (c + 1) * NSB)
        y_s[sb_] = r['y_s'].reshape(NSB, 4, D)
        if hf == 1:
            conv_p[:, bi] = r['convp'].transpose(0, 3, 2, 1).reshape(NL, 30, 256)
            pool_p[:, bi] = r['poolp'].transpose(0, 3, 2, 1).reshape(NL, 15, 256)
            k_p[:, bi] = r['kp'].transpose(0, 2, 1).reshape(NL, 128, 2, 64)
            v_p[:, bi] = r['vp'].reshape(NL, 128, 2, 64)
        conv_s[:, sb_, 0:26] = r['convs_old']
        conv_s[:, sb_, 26:30] = r['convs_new'].transpose(0, 3, 4, 2, 1).reshape(NL, NSB, 4, 256)
        pool_s[:, sb_, 0:11] = r['pools_old']
        pool_s[:, sb_, 11:15] = r['pools_new'].transpose(0, 3, 4, 2, 1).reshape(NL, NSB, 4, 256)
        k_s[:, sb_, 0:124] = r['ks_old'].reshape(NL, NSB, 124, 2, 64)
        k_s[:, sb_, 124:128] = r['ks_new'].transpose(0, 2, 1).reshape(NL, NSB, 4, 2, 64)
        v_s[:, sb_, 0:124] = r['vs_old'].reshape(NL, NSB, 124, 2, 64)
        v_s[:, sb_, 124:128] = r['vs_new'].reshape(NL, NSB, 4, 2, 64)
    return (y_p, y_s, conv_p, pool_p, k_p, v_p, conv_s, pool_s, k_s, v_s)


def kernel(**inputs):
    inp = {k: np.asarray(v) for k, v in inputs.items()}
    in_maps = make_in_maps(inp)
    nc, _ = build_program()
    res = run_bass_kernel_spmd(nc, in_maps, core_ids=list(range(NCORE)))
    return assemble(res.results)
```

```python
import contextlib
import re
import numpy as np
import concourse.bass as bass
import concourse.mybir as mybir
from concourse.bass_utils import run_bass_kernel_spmd

F32 = mybir.dt.float32
BF16 = mybir.dt.bfloat16
AF = mybir.ActivationFunctionType
ALU = mybir.AluOpType

D = 1024
NL = 4
NCORE = 8
OWN_BLOCKS = 16
HALO_BLOCKS = 4
NSB = 16
NS = 64
PAST_LEN = 8192
LN_EPS = 1e-5
ALPHA = (2.0 * NL) ** 0.25
EPS2 = LN_EPS / (ALPHA * ALPHA)
UOFF, GOFF, ZCOFF, PVOFF, ZPOFF = 0, 256, 512, 768, 1024
QOFF, KOFF, VOFF, ZAOFF = 1280, 1792, 1920, 2048
FW = 304
GB = 2
SAME_DIST = 0


class Sched:
    NDS = 24

    def __init__(self, nc, es, same_engine_sync=True):
        self.nc = nc
        self.eng = {'pe': nc.tensor, 'act': nc.scalar, 'dve': nc.vector, 'pool': nc.gpsimd, 'sp': nc.sync}
        self.sem = {e: es.enter_context(nc.semaphore('s_' + e)) for e in self.eng}
        self.cnt = {e: 0 for e in self.eng}
        self.waited = {e: {} for e in self.eng}
        self.dsem = [es.enter_context(nc.semaphore('d%d' % i)) for i in range(self.NDS)]
        self.dtarget = [0] * self.NDS
        self.dnext = 0
        self.lastw = {}
        self.readers = {}
        self.children = {}
        self.same = same_engine_sync
        self.nwaits = 0
        self.nops = 0
        self.nsame = 0
        self.nskip = 0
        self.same_dist = SAME_DIST

    def _wait(self, e, ev):
        if ev[0] == 'e':
            _, f, k = ev
            if f == e and (e == 'pe' or not self.same):
                return
            if f == e:
                self.nsame += 1
                if self.same_dist and self.cnt[e] - k >= self.same_dist:
                    self.nskip += 1
                    return
            key = f
            sem = self.sem[f]
        else:
            _, idx, k = ev
            key = ('d', idx)
            sem = self.dsem[idx]
        if self.waited[e].get(key, 0) >= k:
            return
        self.eng[e].wait_ge(sem, k)
        self.waited[e][key] = k
        self.nwaits += 1

    _BASE = re.compile(r'^(fr|br|tr|sm)\d+')

    def _base(self, x):
        m = self._BASE.match(x)
        return m.group(0) if m else x

    def _deps(self, e, r, w):
        extra = []
        for x in w:
            if self._base(x) == x and x in self.children:
                extra.extend(self.children[x])
        if extra:
            w = list(w) + extra
        best = {}
        def add(ev):
            if ev is None:
                return
            key = ev[1] if ev[0] == 'e' else ('d', ev[1])
            if key not in best or best[key][2] < ev[2]:
                best[key] = ev
        for x in r:
            add(self.lastw.get(x))
        for x in w:
            add(self.lastw.get(x))
            for ev in self.readers.get(x, {}).values():
                add(ev)
        for ev in best.values():
            self._wait(e, ev)

    def _record(self, ev, r, w):
        for x in list(r) + list(w):
            b = self._base(x)
            if b != x:
                self.children.setdefault(b, set()).add(x)
        key = ev[1] if ev[0] == 'e' else ('d', ev[1])
        for x in w:
            self.lastw[x] = ev
            self.readers[x] = {}
        for x in r:
            d = self.readers.setdefault(x, {})
            if key not in d or d[key][2] < ev[2]:
                d[key] = ev

    def op(self, e, fn, r=(), w=()):
        px = [x for x in r if x.startswith('pb') or x.startswith('pstat') or x.startswith('ptp')]
        if px:
            w = list(w) + px
        self._deps(e, r, w)
        inst = fn(self.eng[e])
        self.cnt[e] += 1
        inst.then_inc(self.sem[e], 1)
        ev = ('e', e, self.cnt[e])
        self._record(ev, r, w)
        self.nops += 1
        return ev

    def dma(self, q, out, in_, r=(), w=(), **kw):
        self._deps(q, r, w)
        idx = self.dnext
        self.dnext = (self.dnext + 1) % self.NDS
        if self.dtarget[idx] > 0:
            self._wait(q, ('d', idx, self.dtarget[idx]))
        pairs = list(zip(out, in_)) if isinstance(out, (list, tuple)) else [(out, in_)]
        for o, i in pairs:
            self.eng[q].dma_start(out=o, in_=i, **kw).then_inc(self.dsem[idx], 16)
            self.dtarget[idx] += 16
        ev = ('d', idx, self.dtarget[idx])
        self._record(ev, r, w)
        return ev

    def finish(self, evs):
        for ev in evs:
            self._wait('sp', ev)


class _Stop(Exception):
    pass


def build_program(passes=('H', 0, 1, 2, 3), nl=NL, same_engine_sync=True, stop_at=None):
    nc = bass.Bass("TRN2", target_bir_lowering=False)
    seen = []

    def ckpt(name):
        seen.append(name)
        if stop_at is not None and name == stop_at:
            raise _Stop()

    def din(name, shape):
        return nc.dram_tensor(name, list(shape), F32, kind="ExternalInput").ap()

    def dout(name, shape):
        return nc.dram_tensor(name, list(shape), F32, kind="ExternalOutput").ap()

    xin_d = din("xin", (2560, D))
    xs_d = din("xsin", (NS, D))
    w_in_d = din("w_in", (NL, D, 2560))
    w_out_d = din("w_out", (NL, D, D))
    w_mod_d = din("w_mod", (NL, D, 3 * D))
    cT_d = din("cT", (128, 8, 17))
    bmodT_d = din("bmodT", (128, NL, 24))
    pvec_d = din("pvec", (128, NL, 70))
    poolbd_d = din("poolbd", (128, NL, 2, 128))
    lngb_d = din("lngb", (NL, 2, D))
    sinks_d = din("sinks", (1, NL * 8))
    ident_d = din("ident", (128, 128))
    prot_d = din("prot", (128, 128))
    masks_d = din("masks", (128, 3, 512))
    smask_d = din("smask", (128, 2, 512))
    ropeC_d = din("ropeC", (128, 2560))
    ropeS_d = din("ropeS", (128, 2560))
    ropeCs_d = din("ropeCs", (128, NS))
    ropeSs_d = din("ropeSs", (128, NS))
    misc_d = din("misc", (128, 36))
    cconvT_d = din("cconvT", (NL, 128, 2, NSB, 30))
    cpoolT_d = din("cpoolT", (NL, 128, 2, NSB, 15))
    ckT_d = din("ckT", (NL, 128, NSB, 128))
    cv_d = din("cv", (NL, 128, NSB, 128))
    cconv_n = din("cconv_n", (NL, NSB, 30, 256))
    cpool_n = din("cpool_n", (NL, NSB, 15, 256))
    ck_n = din("ck_n", (NL, NSB, 128, 128))
    cv_n = din("cv_n", (NL, NSB, 128, 128))

    yp_d = dout("y_p", (OWN_BLOCKS * 128, D))
    ys_d = dout("y_s", (NS, D))
    convp_d = dout("convp", (NL, 128, 2, 30))
    poolp_d = dout("poolp", (NL, 128, 2, 15))
    kp_d = dout("kp", (NL, 128, 128))
    vp_d = dout("vp", (NL, 128, 128))
    convs_old = dout("convs_old", (NL, NSB, 26, 256))
    pools_old = dout("pools_old", (NL, NSB, 11, 256))
    ks_old = dout("ks_old", (NL, NSB, 124, 128))
    vs_old = dout("vs_old", (NL, NSB, 124, 128))
    convs_new = dout("convs_new", (NL, 128, 2, NSB, 4))
    pools_new = dout("pools_new", (NL, 128, 2, NSB, 4))
    ks_new = dout("ks_new", (NL, 128, NS))
    vs_new = dout("vs_new", (NL, NS, 128))

    out_evs = []
    with contextlib.ExitStack() as es:
        S = Sched(nc, es, same_engine_sync)

        def sb(name, shape, dt=F32):
            return es.enter_context(nc.sbuf_tensor("sb_" + name, list(shape), dt))

        def ps(name, shape, dt=F32):
            return es.enter_context(nc.psum_tensor("ps_" + name, list(shape), dt))

        xo = [sb("xo%d" % i, (128, D)) for i in range(4)]
        xsm = sb("xsm", (128, D))
        wbf = sb("wbf", (128, 8, 2560), BF16)
        wobf = sb("wobf", (128, 8, D), BF16)
        hT = sb("hT", (128, 8, 256), BF16)
        mixT = sb("mixT", (128, 8, 256), BF16)
        xn_all = sb("xn_all", (128, GB, D), BF16)
        qrT = sb("qrT", (128, 4, 256), BF16)
        krT = sb("krT", (128, 256), BF16)
        vext = sb("vext", (128, GB, 2, 65), BF16)
        diag = sb("diag", (128, 2, 31, 128), BF16)
        cf = sb("cf", (128, 2, 256))
        aT = sb("aT", (128, 2, 30 + 256), BF16)
        pvT = sb("pvT", (128, 2, 15 + 256))
        ropet = sb("ropet", (128, 2, 256))
        lng = sb("lng", (128, 2, D))
        g1a = sb("g1a", (128, D))
        modT = sb("modT", (128, NL, 24, 17))
        masks = sb("masks", (128, 3, 512), BF16)
        smask = sb("smask", (128, 2, 512), BF16)
        ident32 = sb("ident32", (128, 128))
        ones32 = sb("ones32", (128, 128))
        identb = sb("identb", (128, 128), BF16)
        protb = sb("protb", (128, 128), BF16)
        onesb = sb("onesb", (128, 128), BF16)
        oneb1 = sb("oneb1", (128, 128), BF16)
        pvec = sb("pvec", (128, NL, 70))
        dvh = sb("dvh", (128, NL, 70))
        dvq = sb("dvq", (128, NL, 70))
        poolbd = sb("poolbd", (128, NL, 2, 128), BF16)
        bmodT = sb("bmodT", (128, NL, 24))
        esink = sb("esink", (128, NL * 8))
        esink2 = sb("esink2", (128, NL * 8))
        misc = sb("misc", (128, 36))
        mhalf = sb("mhalf", (128, 4))
        cT = sb("cT", (128, 8, 17))
        siluT = sb("siluT", (128, 8, 17), BF16)
        kst = sb("kst", (128, NL, 128), BF16)
        vst = sb("vst", (128, NL, 2, 65), BF16)
        cst = sb("cst", (128, NL, 2, 30), BF16)
        pst = sb("pst", (128, NL, 2, 15))
        small = sb("small", (128, 16, 16))
        hTs = sb("hTs", (128, 8, NS), BF16)
        mixTs = sb("mixTs", (128, 8, NS), BF16)
        qrTs = sb("qrTs", (128, 4, NS), BF16)
        krTs = sb("krTs", (128, NS), BF16)
        vnew = sb("vnew", (128, 128), BF16)
        ckTb = sb("ckTb", (128, NSB, 128), BF16)
        cvb = sb("cvb", (128, NSB, 128), BF16)
        aTs = sb("aTs", (128, 2, NSB, 34), BF16)
        pvTs = sb("pvTs", (128, 2, NSB, 19))
        ropes = sb("ropes", (128, 2, NS))
        esinkS = sb("esinkS", (128, 512))

        NFR = 16
        fring = [sb("fr%d" % i, (128, FW)) for i in range(NFR)]
        NBR = 11
        bring = [sb("br%d" % i, (128, 512), BF16) for i in range(NBR)]
        tring = [sb("tr%d" % i, (128, D)) for i in range(2)]
        NPB = 4
        pbank = [ps("pb%d" % i, (128, 512)) for i in range(NPB)]
        pstatA = ps("pstatA", (128, 512))
        pstatB = ps("pstatB", (128, 512))
        ptp = [ps("ptp%d" % i, (128, 1024), BF16) for i in range(2)]
        ctr = {'f': 0, 'b': 0, 't': 0, 'p': 0, 'tp': 0, 's': 0,
               'fX': 0, 'fY': 0, 'bX': 0, 'bY': 0, 'pX': 0, 'pY': 0, 'pZ0': 0, 'pZ1': 0}
        ctx = {'chain': None}
        FR = {'X': list(range(0, 8)), 'Y': list(range(8, 16))}
        BR = {'X': list(range(0, 8)), 'Y': list(range(8, 11))}
        PB = {'X': [0, 1], 'Y': [2, 3], 'Z0': [0, 1], 'Z1': [2, 3]}

        def tf():
            ch = ctx['chain']
            if ch is None:
                i = ctr['f']; ctr['f'] = (i + 1) % NFR
            else:
                j = ctr['f' + ch]; ctr['f' + ch] = (j + 1) % len(FR[ch]); i = FR[ch][j]
            return fring[i], 'fr%d' % i

        def tb():
            ch = ctx['chain']
            if ch is None:
                i = ctr['b']; ctr['b'] = (i + 1) % NBR
            else:
                j = ctr['b' + ch]; ctr['b' + ch] = (j + 1) % len(BR[ch]); i = BR[ch][j]
            return bring[i], 'br%d' % i

        def tt():
            i = ctr['t']; ctr['t'] = (i + 1) % 2
            return tring[i], 'tr%d' % i

        def bank():
            ch = ctx['chain']
            if ch is None:
                i = ctr['p']; ctr['p'] = (i + 1) % NPB
            else:
                j = ctr['p' + ch]; ctr['p' + ch] = (j + 1) % len(PB[ch]); i = PB[ch][j]
            return pbank[i], 'pb%d' % i

        def run_interleaved(chains):
            live = list(chains)
            while live:
                for item in list(live):
                    ctx['chain'] = item[0]
                    try:
                        next(item[1])
                    except StopIteration:
                        live.remove(item)
            ctx['chain'] = None

        def tph():
            i = ctr['tp']; ctr['tp'] = (i + 1) % 2
            return ptp[i][:, 0:512], 'ptp%d' % i

        def sm():
            i = ctr['s']; ctr['s'] = (i + 1) % 16
            return small[:, i, :], 'sm%d' % i

        def mm(out, lhsT, rhs, start, stop, r, w):
            S.op('pe', lambda e: e.matmul(out, lhsT=lhsT, rhs=rhs, start=start, stop=stop, skip_group_check=True), r=r, w=w)

        def tr(out, in_, r, w):
            S.op('pe', lambda e: e.transpose(out, in_, identb[:in_.shape[0], :in_.shape[0]]), r=list(r) + ['identb'], w=w)

        def act(out, in_, func, r, w, scale=None, bias=None):
            kw = {}
            if scale is not None:
                kw['scale'] = scale
            if bias is not None:
                kw['bias'] = bias
            S.op('act', lambda e: e.activation(out=out, in_=in_, func=func, **kw), r=r, w=w)

        def ts(eng, out, in0, s1, s2, op0, op1, r, w):
            if s2 is None:
                S.op(eng, lambda e: e.tensor_scalar(out=out, in0=in0, scalar1=s1, scalar2=None, op0=op0), r=r, w=w)
            else:
                S.op(eng, lambda e: e.tensor_scalar(out=out, in0=in0, scalar1=s1, scalar2=s2, op0=op0, op1=op1), r=r, w=w)

        def stt(out, in0, scalar, in1, op0, op1, r, w):
            S.op('dve', lambda e: e.scalar_tensor_tensor(out=out, in0=in0, scalar=scalar, in1=in1, op0=op0, op1=op1), r=r, w=w)

        def tten(eng, out, in0, in1, op, r, w):
            S.op(eng, lambda e: e.tensor_tensor(out=out, in0=in0, in1=in1, op=op), r=r, w=w)

        def cpy(eng, out, in_, r, w):
            S.op(eng, lambda e: e.tensor_copy(out=out, in_=in_), r=r, w=w)

        def mset(eng, ap, val, w):
            S.op(eng, lambda e: e.memset(ap, val), w=w)

        try:
            S.dma('sp', ident32[:], ident_d, w=['ident32'])
            S.dma('pool', identb[:], ident_d, w=['identb'])
            S.dma('pool', protb[:], prot_d, w=['protb'])
            S.dma('pool', masks[:], masks_d, w=['masks'])
            S.dma('pool', smask[:], smask_d, w=['smask'])
            S.dma('sp', pvec[:], pvec_d, w=['pvec'])
            S.dma('pool', poolbd[:], poolbd_d, w=['poolbd'])
            S.dma('sp', bmodT[:], bmodT_d, w=['bmodT'])
            S.dma('sp', misc[:], misc_d, w=['misc'])
            S.dma('sp', cT[:], cT_d, w=['cT'])
            S.dma('sp', esink[:], sinks_d.partition_broadcast(128), w=['esink'])
            mset('pool', ones32[:], 1.0, ['ones32'])
            mset('pool', onesb[:], 1.0 / 256.0, ['onesb'])
            mset('pool', oneb1[:], 1.0, ['oneb1'])
            mset('pool', mhalf[:], -0.5, ['mhalf'])
            mset('pool', kst[:], 0.0, ['kst%d' % l for l in range(NL)])
            mset('pool', vst[:], 0.0, ['vst%d' % l for l in range(NL)])
            mset('pool', cst[:], 0.0, ['cst%d' % l for l in range(NL)])
            mset('pool', pst[:], 0.0, ['pst%d' % l for l in range(NL)])
            mset('pool', vext[:], 1.0, ['vext%d' % b for b in range(GB)])
            mset('pool', xsm[:], 0.0, ['xsm'])
            act(esink[:], esink[:], AF.Exp, r=['esink'], w=['esink'])
            ts('dve', esink2[:], esink[:], 2.0, None, ALU.mult, None, r=['esink'], w=['esink2'])
            ts('dve', dvh[:], pvec[:], 0.5, None, ALU.mult, None, r=['pvec'], w=['dvh'])
            ts('dve', dvq[:], pvec[:], 0.25, None, ALU.mult, None, r=['pvec'], w=['dvq'])

            ckpt('consts')
            if 'H' in passes:
                for l in range(nl):
                    out_evs.append(S.dma('sp', convs_old[l], cconv_n[l, :, 4:30, :]))
                    out_evs.append(S.dma('sp', pools_old[l], cpool_n[l, :, 4:15, :]))
                    out_evs.append(S.dma('sp', ks_old[l], ck_n[l, :, 4:128, :]))
                    out_evs.append(S.dma('sp', vs_old[l], cv_n[l, :, 4:128, :]))

            ckpt('oldcopy')
            t1, k1 = tf()
            act(t1[:, 0:136], cT[:].rearrange("p a b -> p (a b)"), AF.Tanh, r=['cT'], w=[k1], scale=0.5)
            t2, k2 = tf()
            stt(t2[:, 0:136], t1[:, 0:136], 1.0, cT[:].rearrange("p a b -> p (a b)"), ALU.add, ALU.mult, r=[k1, 'cT'], w=[k2])
            ts('dve', siluT[:].rearrange("p a b -> p (a b)"), t2[:, 0:136], 0.5, None, ALU.mult, None, r=[k2], w=['siluT'])
            pieces = [(l, j) for l in range(nl) for j in range(3)]
            for idx, (l, j) in enumerate(pieces):
                half = idx % 2
                key = 'winA' if half == 0 else 'winB'
                dst = wbf[:, :, half * 1280: half * 1280 + 1024]
                src = w_mod_d[l].rearrange("(kc p) n -> p kc n", p=128)[:, :, j * 1024:(j + 1) * 1024]
                S.dma('pool', [dst[:, 0:4, :], dst[:, 4:8, :]], [src[:, 0:4, :], src[:, 4:8, :]], w=[key])
                for oc in range(8):
                    bk, bkey = bank()
                    for kc in range(8):
                        mm(bk[:, 0:17], dst[:, kc, oc * 128:(oc + 1) * 128], siluT[:, kc, :], kc == 0, kc == 7,
                           r=[key, 'siluT'], w=[bkey])
                    ch = j * 8 + oc
                    if j == 0:
                        ts('dve', modT[:, l, ch, :], bk[:, 0:17], bmodT[:, l, ch:ch + 1], None, ALU.add, None,
                           r=[bkey, 'bmodT'], w=['modT'])
                    else:
                        ts('dve', modT[:, l, ch, :], bk[:, 0:17], bmodT[:, l, ch:ch + 1], 1.0, ALU.add, ALU.add,
                           r=[bkey, 'bmodT'], w=['modT'])
            for l in range(nl):
                ts('dve', modT[:, l, 16:24, :], modT[:, l, 16:24, :], 1.0 / ALPHA, None, ALU.mult, None, r=['modT'], w=['modT'])

            ckpt('prologue')
            def load_w_in(l, half):
                key = 'winA' if half == 0 else 'winB'
                src = w_in_d[l].rearrange("(kc p) n -> p kc n", p=128)[:, :, half * 1280:(half + 1) * 1280]
                dst = wbf[:, :, half * 1280:(half + 1) * 1280]
                S.dma('pool', [dst[:, 0:4, :], dst[:, 4:8, :]], [src[:, 0:4, :], src[:, 4:8, :]], w=[key])

            def load_w_out(l):
                src = w_out_d[l].rearrange("(kc p) n -> p kc n", p=128)
                S.dma('pool', [wobf[:, 0:4, :], wobf[:, 4:8, :]], [src[:, 0:4, :], src[:, 4:8, :]], w=['wout'])

            def diag_gen(l, gate):
                while gate is not None and not gate[0]:
                    yield
                k = 0
                for c in range(2):
                    for j in range(31):
                        if k % 2 == 0:
                            act(diag[:, c, j, :], ident32[:], AF.Identity, r=['ident32', 'dvh'], w=['diagA'],
                                scale=dvh[:, l, c * 31 + j: c * 31 + j + 1])
                        else:
                            ts('pool', diag[:, c, j, :], ident32[:], dvh[:, l, c * 31 + j: c * 31 + j + 1], 1.0, ALU.mult, ALU.mult,
                               r=['ident32', 'dvh'], w=['diagP'])
                        k += 1
                        if k % 4 == 0:
                            yield

            def load_lng(l):
                S.dma('sp', [lng[:, 0, :], lng[:, 1, :]],
                      [lngb_d[l, 0:1, :].partition_broadcast(128), lngb_d[l, 1:2, :].partition_broadcast(128)], w=['lng'])

            def layer_setup(l, with_sample, build_diag=True, load_lng_now=True):
                if load_lng_now:
                    load_lng(l)
                for hf in range(2):
                    bk, bkey = bank()
                    for c4 in range(4):
                        ch = hf * 4 + c4
                        dg, dk = tf()
                        ts('dve', dg[:, 0:128], ident32[:], modT[:, l, 16 + ch, 0:1], None, ALU.mult, None,
                           r=['ident32', 'modT'], w=[dk])
                        mm(bk[:, c4 * 128:(c4 + 1) * 128], ones32[:], dg[:, 0:128], True, True, r=['ones32', dk], w=[bkey])
                    act(g1a[:, hf * 512:(hf + 1) * 512], bk[:], AF.Copy, r=[bkey], w=['g1a'])
                if build_diag:
                    for _ in diag_gen(l, None):
                        pass
                if with_sample:
                    S.dma('pool', ckTb[:], ckT_d[l], w=['ckTb'])
                    S.dma('pool', cvb[:], cv_d[l], w=['cvb'])
                    S.dma('pool', aTs[:, :, :, 0:30], cconvT_d[l], w=['aTs'])
                    S.dma('sp', pvTs[:, :, :, 0:15], cpoolT_d[l], w=['pvTs'])
                    ts('pool', aTs[:, :, :, 0:30], aTs[:, :, :, 0:30], 2.0, 1.0, ALU.mult, ALU.mult, r=['aTs'], w=['aTs'])
                    cpy('dve', esinkS[:].rearrange("p (h r) -> p h r", h=8),
                        esink[:, l * 8:(l + 1) * 8].unsqueeze(2).to_broadcast([128, 8, 64]), r=['esink'], w=['esinkS'])

            def ln_stats(src, nrows, eps, rkeys):
                st, sk = sm()
                S.op('dve', lambda e: e.bn_stats(st[:nrows, 0:6], src[:nrows, 0:512]), r=rkeys, w=[sk + 'a'])
                S.op('dve', lambda e: e.bn_stats(st[:nrows, 6:12], src[:nrows, 512:1024]), r=rkeys, w=[sk + 'b'])
                mv, mk = sm()
                S.op('dve', lambda e: e.bn_aggr(mv[:nrows, 0:2], st[:nrows, 0:12].rearrange("p (a b) -> p a b", a=2)),
                     r=[sk + 'a', sk + 'b'], w=[mk])
                ts('pool', mv[:nrows, 2:3], mv[:nrows, 1:2], eps, 1.0, ALU.add, ALU.mult, r=[mk], w=[mk + 'v'])
                tten('pool', mv[:nrows, 3:4], mv[:nrows, 2:3], mhalf[:nrows, 0:1], ALU.pow, r=[mk + 'v', 'mhalf'], w=[mk + 'r'])
                stt(mv[:nrows, 4:5], mv[:nrows, 0:1], -1.0, mv[:nrows, 3:4], ALU.mult, ALU.mult, r=[mk, mk + 'r'], w=[mk + 'n'])
                return mv[:nrows, 3:4], mv[:nrows, 4:5], [mk + 'r', mk + 'n']

            def make_s1(l, blocks):
                nb = len(blocks)
                N = nb * 128
                for bi, (xb, xk) in enumerate(blocks):
                    rstd, nmr, lk = ln_stats(xb, 128, LN_EPS, list(xk))
                    act(xn_all[:, bi, :], xb[:], AF.Identity, r=list(xk) + lk, w=['xn%d' % bi], scale=rstd, bias=nmr)
                    yield
                for kc in range(8):
                    tp, tk = tph()
                    for bi in range(nb):
                        tr(tp[:, bi * 128:(bi + 1) * 128], xn_all[:, bi, kc * 128:(kc + 1) * 128], r=['xn%d' % bi], w=[tk])
                    if kc % 2 == 0:
                        ts('dve', hT[:, kc, :N], tp[:, :N], modT[:, l, 8 + kc, 0:1], modT[:, l, kc, 0:1], ALU.mult, ALU.add,
                           r=[tk, 'modT'], w=['hT'])
                    else:
                        act(hT[:, kc, :N], tp[:, :N], AF.Identity, r=[tk, 'modT'], w=['hT'],
                            scale=modT[:, l, 8 + kc, 0:1], bias=modT[:, l, kc, 0:1])
                    yield

            def prompt_group(l, blocks, tcol, first_seq, last_seq, hooks=None, s1_done=None, next_s1=None, diag_next=None, pending=None, defer=False, after_xy=None, partial=False):
                conv_done = [False]
                hooks = hooks or {}
                nb = len(blocks)
                N = nb * 128
                wA, wB = 'winA', 'winB'
                if s1_done is None:
                    for _ in make_s1(l, blocks):
                        pass

                def proj(off):
                    bk, bkey = bank()
                    for kc in range(8):
                        mm(bk[:, :N], wbf[:, kc, off:off + 128], hT[:, kc, :N], kc == 0, kc == 7,
                           r=[wA if off < 1280 else wB, 'hT'], w=[bkey])
                    return bk, bkey

                def chainX():
                    S.dma('sp', [ropet[:, 0, :N], ropet[:, 1, :N]], [ropeC_d[:, tcol:tcol + N], ropeS_d[:, tcol:tcol + N]], w=['ropet'])
                    ckpt('p1')
                    for c in ([4] if partial else range(5)):
                        bk, bkey = proj(QOFF + c * 128)
                        A_, ak = tf()
                        tten('dve', A_[:, :N], bk[:, :N], ropet[:, 0, :N], ALU.mult, r=[bkey, 'ropet'], w=[ak])
                        B_, bkk = tb()
                        tten('dve', B_[:, :N], bk[:, :N], ropet[:, 1, :N], ALU.mult, r=[bkey, 'ropet'], w=[bkk])
                        yield
                        pr, prk = bank()
                        mm(pr[:, :N], protb[:], B_[:, :N], True, True, r=['protb', bkk], w=[prk])
                        if c < 4:
                            tten('dve', qrT[:, c, :N], pr[:, :N], A_[:, :N], ALU.add, r=[prk, ak], w=['qrT'])
                            yield
                        else:
                            tten('dve', krT[:, :N], pr[:, :N], A_[:, :N], ALU.add, r=[prk, ak], w=['krT'])
                            if last_seq:
                                ko, kok = tf()
                                tten('dve', ko[:, 0:128], pr[:, N - 128:N], A_[:, N - 128:N], ALU.add, r=[prk, ak], w=[kok])
                                out_evs.append(S.dma('sp', kp_d[l], ko[:, 0:128], r=[kok]))
                    ckpt('p2')
                    gz = []
                    for bi in range(nb):
                        bk, bkey = bank()
                        for kc in range(8):
                            mm(bk[:, 0:128], hT[:, kc, bi * 128:(bi + 1) * 128], wbf[:, kc, VOFF:VOFF + 128], kc == 0, kc == 7,
                               r=[wB, 'hT'], w=[bkey])
                        act(vext[:, bi, :, 0:64], bk[:, 0:128].rearrange("p (h d) -> p h d", h=2), AF.Copy, r=[bkey], w=['vext%d' % bi])
                        yield
                        if last_seq and bi == nb - 1:
                            vo, vok = tf()
                            cpy('dve', vo[:, 0:128], bk[:, 0:128], r=[bkey], w=[vok])
                            out_evs.append(S.dma('sp', vp_d[l], vo[:, 0:128], r=[vok]))
                        if partial:
                            continue
                        bz, bzk = bank()
                        for kc in range(8):
                            mm(bz[:, :], hT[:, kc, bi * 128:(bi + 1) * 128], wbf[:, kc, ZAOFF:ZAOFF + 512], kc == 0, kc == 7,
                               r=[wB, 'hT'], w=[bzk])
                        gpair = []
                        for hh in range(2):
                            g_, gk_ = tf()
                            act(g_[:, 0:256], bz[:, hh * 256:(hh + 1) * 256], AF.Tanh, r=[bzk], w=[gk_], scale=0.5)
                            stt(g_[:, 0:256], g_[:, 0:256], 1.0, bz[:, hh * 256:(hh + 1) * 256], ALU.add, ALU.mult, r=[gk_, bzk], w=[gk_])
                            gpair.append((g_, gk_))
                        gz.append(tuple(gpair))
                        yield
                    if 3 in hooks:
                        hooks[3]()
                    ckpt('p3')
                    for bi in ([] if partial else range(nb)):
                        if bi == 0:
                            kprev = lambda hk: kst[hk * 64:(hk + 1) * 64, l, :]
                            kprev_key = 'kst%d' % l
                            vprev = lambda hk: vst[:, l, hk, :]
                            vprev_key = 'vst%d' % l
                        else:
                            kprev = lambda hk, b_=bi: krT[hk * 64:(hk + 1) * 64, (b_ - 1) * 128:b_ * 128]
                            kprev_key = 'krT'
                            vprev = lambda hk, b_=bi: vext[:, b_ - 1, hk, :]
                            vprev_key = 'vext%d' % (bi - 1)
                        mprev = masks[:, 2, :] if (first_seq and bi == 0) else masks[:, 1, :]
                        E = {}
                        for hk in range(2):
                            for kb in range(2):
                                st_, stk = bank()
                                lhs = kprev(hk) if kb == 0 else krT[hk * 64:(hk + 1) * 64, bi * 128:(bi + 1) * 128]
                                lk_ = kprev_key if kb == 0 else 'krT'
                                mm(st_[:, :], lhs, qrT[hk * 64:(hk + 1) * 64, :, bi * 128:(bi + 1) * 128], True, False,
                                   r=[lk_, 'qrT'], w=[stk])
                                mm(st_[:, :], identb[:], mprev if kb == 0 else masks[:, 0, :], False, True,
                                   r=['identb', 'masks'], w=[stk])
                                e_, ek = tb()
                                act(e_[:, :], st_[:, :], AF.Exp, r=[stk], w=[ek], scale=0.125)
                                E[(hk, kb)] = (e_, ek)
                                yield
                        den, dk_ = sm()
                        Ob = []
                        for hk in range(2):
                            o_, ok_ = bank()
                            ov = o_[:, 0:260].rearrange("p (g d) -> p g d", g=4)
                            for g in range(4):
                                e0, e0k = E[(hk, 0)]
                                e1, e1k = E[(hk, 1)]
                                mm(ov[:, g, :], e0[:, g * 128:(g + 1) * 128], vprev(hk), True, False, r=[e0k, vprev_key], w=[ok_])
                                mm(ov[:, g, :], e1[:, g * 128:(g + 1) * 128], vext[:, bi, hk, :], False, True,
                                   r=[e1k, 'vext%d' % bi], w=[ok_])
                            stt(den[:, hk * 4:(hk + 1) * 4], ov[:, :, 64], 2.0, esink2[:, l * 8 + hk * 4: l * 8 + hk * 4 + 4], ALU.mult, ALU.add,
                                r=[ok_, 'esink2'], w=[dk_ + 'h%d' % hk])
                            Ob.append((ov, ok_))
                            yield
                        S.op('dve', lambda e: e.reciprocal(den[:, 8:16], den[:, 0:8]), r=[dk_ + 'h0', dk_ + 'h1'], w=[dk_ + 'r'])
                        yield
                        yc, yck = tb()
                        yield
                        ycv = yc[:, :].rearrange("p (g h d) -> p g h d", g=4, h=2)
                        for hk in range(2):
                            ov, ok_ = Ob[hk]
                            o32, o32k = tf()
                            tten('dve', o32[:, 0:256].rearrange("p (g d) -> p g d", g=4), ov[:, :, 0:64],
                                 den[:, 8 + hk * 4: 12 + hk * 4].unsqueeze(2).to_broadcast([128, 4, 64]), ALU.mult,
                                 r=[ok_, dk_ + 'r'], w=[o32k])
                            for gp in range(2):
                                gt_, gk_ = gz[bi][gp]
                                tten('dve', ycv[:, 2 * gp:2 * gp + 2, hk, :], o32[:, gp * 128:(gp + 1) * 128].rearrange("p (g d) -> p g d", g=2),
                                     gt_[:, 0:256].rearrange("p (g h d) -> p g h d", g=2, h=2)[:, :, hk, :], ALU.mult,
                                     r=[o32k, gk_], w=[yck])
                        tp, tk = tph()
                        yield
                        for c in range(4):
                            tr(tp[:, c * 128:(c + 1) * 128], yc[:, c * 128:(c + 1) * 128], r=[yck], w=[tk])
                        act(mixT[:, 4:8, bi * 128:(bi + 1) * 128], tp[:, :].rearrange("p (c t) -> p c t", c=4), AF.Copy,
                            r=[tk], w=['mixTa'])
                    ckpt('p4')
                    cpy('pool', kst[:, l, :], krT[:, N - 128:N], r=['krT'], w=['kst%d' % l])
                    cpy('pool', vst[:, l, :, :], vext[:, nb - 1, :, :], r=['vext%d' % (nb - 1)], w=['vst%d' % l])

                    yield

                def chainY():
                    cpy('pool', aT[:, :, 0:30], cst[:, l, :, :], r=['cst%d' % l], w=['aT'])
                    pmean, pmk = pstatA, 'pstatA'
                    pex2, pek = pstatB, 'pstatB'
                    for c in range(2):
                        bu, buk = proj(UOFF + c * 128)
                        yield
                        bg, bgk = proj(GOFF + c * 128)
                        th, thk = tf()
                        act(th[:, :N], bg[:, :N], AF.Tanh, r=[bgk], w=[thk], scale=0.5)
                        stt(aT[:, c, 30:30 + N], th[:, :N], 1.0, bu[:, :N], ALU.add, ALU.mult, r=[thk, buk], w=['aT'])
                        yield
                        if last_seq:
                            a32, a32k = tf()
                            stt(a32[:, 0:30], th[:, N - 30:N], 1.0, bu[:, N - 30:N], ALU.add, ALU.mult, r=[thk, buk], w=[a32k])
                            ts('dve', a32[:, 32:62], a32[:, 0:30], 0.5, None, ALU.mult, None, r=[a32k], w=[a32k + 'o'])
                            out_evs.append(S.dma('sp', convp_d[l, :, c, :], a32[:, 32:62], r=[a32k + 'o']))
                        if partial:
                            continue
                        bc, bck = bank()
                        for j in range(31):
                            mm(bc[:, :N], diag[:, c, j, :], aT[:, c, j:j + N], j == 0, j == 30, r=['diagA', 'diagP', 'aT'], w=[bck])
                            if j % 8 == 7:
                                yield
                        cbias = pvec[:, l, 62 + c:63 + c]
                        yield
                        ts('dve', cf[:, c, :N], bc[:, :N], cbias, None, ALU.add, None, r=[bck, 'pvec'], w=['cf%d' % c])
                        cb_, cbk = tb()
                        act(cb_[:, :N], bc[:, :N], AF.Identity, r=[bck, 'pvec'], w=[cbk], bias=cbias)
                        cq_, cqk = tb()
                        act(cq_[:, :N], bc[:, :N], AF.Square, r=[bck, 'pvec'], w=[cqk], bias=cbias)
                        mm(pmean[0:1, :N], onesb[:, 0:1], cb_[:, :N], c == 0, c == 1, r=['onesb', cbk], w=[pmk])
                        mm(pex2[0:1, :N], onesb[:, 0:1], cq_[:, :N], c == 0, c == 1, r=['onesb', cqk], w=[pek])
                        yield
                    cpy('pool', cst[:, l, :, :], aT[:, :, N:N + 30], r=['aT'], w=['cst%d' % l])
                    conv_done[0] = True
                    if partial:
                        cpy('pool', pvT[:, :, 0:15], pst[:, l, :, :], r=['pst%d' % l], w=['pvT'])
                        for c in range(2):
                            bp, bpk = proj(PVOFF + c * 128)
                            act(pvT[:, c, 15:15 + N], bp[:, :N], AF.Copy, r=[bpk], w=['pvT'])
                            yield
                        cpy('pool', pst[:, l, :, :], pvT[:, :, N:N + 15], r=['pvT'], w=['pst%d' % l])
                        if 6 in hooks:
                            hooks[6]()
                        return
                    msq, msk = tf()
                    act(msq[0:1, :N], pmean[0:1, :N], AF.Square, r=[pmk], w=[msk])
                    mrow, mrk = tf()
                    act(mrow[0:1, :N], pmean[0:1, :N], AF.Copy, r=[pmk], w=[mrk])
                    var, vk = tf()
                    tten('dve', var[0:1, :N], pex2[0:1, :N], msq[0:1, :N], ALU.subtract, r=[pek, msk], w=[vk])
                    ts('dve', var[0:1, :N], var[0:1, :N], LN_EPS, None, ALU.add, None, r=[vk], w=[vk])
                    rs, rsk = tf()
                    S.op('dve', lambda e: e.reciprocal(rs[0:1, :N], var[0:1, :N]), r=[vk], w=[rsk])
                    act(rs[0:1, :N], rs[0:1, :N], AF.Sqrt, r=[rsk], w=[rsk])
                    yield
                    bmean, bmk = bank()
                    mm(bmean[:, :N], ones32[0:1, :], mrow[0:1, :N], True, True, r=['ones32', mrk], w=[bmk])
                    brs, brk = bank()
                    mm(brs[:, :N], ones32[0:1, :], rs[0:1, :N], True, True, r=['ones32', rsk], w=[brk])
                    yield
                    dts = []
                    for c in range(2):
                        d_, dk2 = tf()
                        tten('dve', d_[:, :N], cf[:, c, :N], bmean[:, :N], ALU.subtract, r=['cf%d' % c, bmk], w=[dk2])
                        tten('dve', d_[:, :N], d_[:, :N], brs[:, :N], ALU.mult, r=[dk2, brk], w=[dk2])
                        dts.append((d_, dk2))
                    yield
                    for c in range(2):
                        d_, dk2 = dts[c]
                        sg, sgk = tf()
                        act(sg[:, :N], d_[:, :N], AF.Tanh, r=[dk2, 'dvh'], w=[sgk],
                            scale=dvh[:, l, 64 + c:65 + c], bias=dvh[:, l, 66 + c:67 + c])
                        y_, yk = tf()
                        ts('dve', y_[:, :N], d_[:, :N], dvq[:, l, 64 + c:65 + c], dvq[:, l, 66 + c:67 + c], ALU.mult, ALU.add,
                           r=[dk2, 'dvq'], w=[yk])
                        stt(y_[:, :N], sg[:, :N], 1.0, y_[:, :N], ALU.add, ALU.mult, r=[sgk, yk], w=[yk])
                        yield
                        bz, bzk = proj(ZCOFF + c * 128)
                        yield
                        tz, tzk = tf()
                        act(tz[:, :N], bz[:, :N], AF.Tanh, r=[bzk], w=[tzk], scale=0.5)
                        stt(tz[:, :N], tz[:, :N], 1.0, bz[:, :N], ALU.add, ALU.mult, r=[tzk, bzk], w=[tzk])
                        tten('pool', mixT[:, c, :N], y_[:, :N], tz[:, :N], ALU.mult, r=[yk, tzk], w=['mixTc'])
                        yield

                    ckpt('p5')
                    cpy('pool', pvT[:, :, 0:15], pst[:, l, :, :], r=['pst%d' % l], w=['pvT'])
                    for c in range(2):
                        bp, bpk = proj(PVOFF + c * 128)
                        act(pvT[:, c, 15:15 + N], bp[:, :N], AF.Copy, r=[bpk], w=['pvT'])
                        yield
                    yield from pool_mix(l, N, pvT, 'pvT', first_seq, proj, mixT, 'mixTp', lambda ap: ap)
                    cpy('pool', pst[:, l, :, :], pvT[:, :, N:N + 15], r=['pvT'], w=['pst%d' % l])
                    if last_seq:
                        out_evs.append(S.dma('sp', poolp_d[l], pvT[:, :, N:N + 15], r=['pvT']))

                    if 6 in hooks:
                        hooks[6]()
                    yield

                xy = [('X', chainX()), ('Y', chainY())]
                if diag_next is not None:
                    xy.append(('D', diag_gen(diag_next, conv_done)))
                for ti, tg in enumerate(pending or []):
                    xy.append(('T%d' % ti, tg))
                run_interleaved(xy)
                if after_xy is not None:
                    after_xy()
                ckpt('p6')
                if partial:
                    if next_s1 is not None:
                        run_interleaved([('W', next_s1)])
                    return None
                chains = [('Z%d' % bi, out_head(l, mixT[:, :, bi * 128:(bi + 1) * 128], ['mixTa', 'mixTc', 'mixTp'], xb, xk, 128, g1a, 'g1a'))
                          for bi, (xb, xk) in enumerate(blocks)]
                if next_s1 is not None:
                    chains.append(('W', next_s1))
                run_interleaved(chains)
                tails = [out_tail(l, xb, xk, 128) for (xb, xk) in blocks]
                if defer:
                    return tails
                run_interleaved([('T%d' % ti, tg) for ti, tg in enumerate(tails)])
                return None

                ckpt('p7')

            def pool_mix(l, N, pv, pvk, first_seq, proj, mix, mixk, fv):
                W = 15 + N
                for c in range(2):
                    x = pv[:, c, :]
                    sw, swk = tf()
                    if c == 0:
                        t2, t2k = tf()
                        tten('pool', t2[:, 1:W], x[:, 1:W], x[:, 0:W - 1], ALU.add, r=[pvk], w=[t2k])
                        tten('pool', sw[0:64, 15:W], x[0:64, 15:W], x[0:64, 14:W - 1], ALU.add, r=[pvk], w=[swk + 'a'])
                        tten('pool', sw[64:128, 15:W], t2[64:128, 15:W], t2[64:128, 13:W - 2], ALU.add, r=[t2k], w=[swk + 'b'])
                    else:
                        t2, t2k = tf()
                        tten('pool', t2[:, 1:W], x[:, 1:W], x[:, 0:W - 1], ALU.add, r=[pvk], w=[t2k])
                        t4, t4k = tf()
                        tten('pool', t4[:, 3:W], t2[:, 3:W], t2[:, 1:W - 2], ALU.add, r=[t2k], w=[t4k])
                        tten('pool', sw[0:64, 15:W], t4[0:64, 15:W], t4[0:64, 11:W - 4], ALU.add, r=[t4k], w=[swk + 'a'])
                        t8, t8k = tf()
                        tten('pool', t8[64:128, 7:W], t4[64:128, 7:W], t4[64:128, 3:W - 4], ALU.add, r=[t4k], w=[t8k])
                        tten('pool', sw[64:128, 15:W], t8[64:128, 15:W], t8[64:128, 7:W - 8], ALU.add, r=[t8k], w=[swk + 'b'])
                    rk = [swk + 'a', swk + 'b']
                    if first_seq:
                        tten('pool', sw[:, 15:31], sw[:, 15:31], misc[:, 4 + c * 16: 4 + (c + 1) * 16], ALU.mult,
                             r=rk + ['misc'], w=[swk + 'c'])
                        rk = rk + [swk + 'c']
                    yield
                    pl, plk = tb()
                    stt(pl[:, :N], sw[:, 15:W], misc[:, 1 + c:2 + c], x[:, 15:W], ALU.mult, ALU.subtract,
                        r=rk + ['misc', pvk], w=[plk])
                    by, byk = bank()
                    mm(by[:, :N], poolbd[:, l, c, :], pl[:, :N], True, True, r=['poolbd', plk], w=[byk])
                    yield
                    bz, bzk = proj(ZPOFF + c * 128)
                    yield
                    tz, tzk = tf()
                    act(tz[:, :N], bz[:, :N], AF.Tanh, r=[bzk], w=[tzk], scale=0.5)
                    stt(tz[:, :N], tz[:, :N], 1.0, bz[:, :N], ALU.add, ALU.mult, r=[tzk, bzk], w=[tzk])
                    stt(mix[:, 2 + c, :N], by[:, :N], dvh[:, l, 68 + c:69 + c], tz[:, :N], ALU.mult, ALU.mult,
                        r=[byk, 'dvh', tzk], w=[mixk])
                    yield

            def out_head(l, mixblk, mixk, xb, xk, nrows, gtile, gk):
                T, tk_ = tt()
                for hf in range(2):
                    bo, bok = bank()
                    for kc in range(8):
                        mm(bo[:nrows, :], mixblk[:, kc, :], wobf[:, kc, hf * 512:(hf + 1) * 512], kc == 0, kc == 7,
                           r=list(mixk) + ['wout'], w=[bok])
                    yield
                    tten('dve', T[:nrows, hf * 512:(hf + 1) * 512], bo[:nrows, :], gtile[:nrows, hf * 512:(hf + 1) * 512], ALU.mult,
                         r=[bok, gk], w=[tk_ + 'h%d' % hf])
                    tten('dve', xb[:nrows, hf * 512:(hf + 1) * 512], T[:nrows, hf * 512:(hf + 1) * 512], xb[:nrows, hf * 512:(hf + 1) * 512],
                         ALU.add, r=[tk_ + 'h%d' % hf, xk[hf]], w=[xk[hf]])
                    yield

            def out_tail(l, xb, xk, nrows):
                rstd, nmr, lk = ln_stats(xb, nrows, EPS2, list(xk))
                yield
                act(xb[:nrows, :], xb[:nrows, :], AF.Identity, r=list(xk) + lk, w=list(xk), scale=rstd, bias=nmr)
                yield
                for hf in range(2):
                    tten('dve', xb[:nrows, hf * 512:(hf + 1) * 512], xb[:nrows, hf * 512:(hf + 1) * 512], lng[:nrows, 0, hf * 512:(hf + 1) * 512],
                         ALU.mult, r=[xk[hf], 'lng'], w=[xk[hf]])
                    yield
                for hf in range(2):
                    tten('dve', xb[:nrows, hf * 512:(hf + 1) * 512], xb[:nrows, hf * 512:(hf + 1) * 512], lng[:nrows, 1, hf * 512:(hf + 1) * 512],
                         ALU.add, r=[xk[hf], 'lng'], w=[xk[hf]])
                    yield

            def sample_group(l, hooks=None):
                hooks = hooks or {}
                N = NS
                wA, wB = 'winA', 'winB'
                rstd, nmr, lk = ln_stats(xsm, 64, LN_EPS, ['xsm'])
                act(xn_all[0:64, 0, :], xsm[0:64, :], AF.Identity, r=['xsm'] + lk, w=['xn0'], scale=rstd, bias=nmr)
                tp, tk = tph()
                for kc in range(8):
                    tr(tp[:, kc * 64:(kc + 1) * 64], xn_all[0:64, 0, kc * 128:(kc + 1) * 128], r=['xn0'], w=[tk])
                hv = hTs[:].rearrange("p k (b t) -> p k b t", b=NSB)
                tpv = tp[:, :].rearrange("p (k b t) -> p k b t", k=8, b=NSB)
                sc = modT[:, l, 8:16, 1:17].unsqueeze(3).to_broadcast([128, 8, NSB, 4])
                sh = modT[:, l, 0:8, 1:17].unsqueeze(3).to_broadcast([128, 8, NSB, 4])
                hf32, hfk = tt()
                hfv = hf32[:, 0:512].rearrange("p (k b t) -> p k b t", k=8, b=NSB)
                tten('dve', hfv, tpv, sc, ALU.mult, r=[tk, 'modT'], w=[hfk])
                tten('dve', hv, hfv, sh, ALU.add, r=[hfk, 'modT'], w=['hTs'])

                def proj(off):
                    bk, bkey = bank()
                    for kc in range(8):
                        mm(bk[:, :N], wbf[:, kc, off:off + 128], hTs[:, kc, :], kc == 0, kc == 7,
                           r=[wA if off < 1280 else wB, 'hTs'], w=[bkey])
                    return bk, bkey

                def chainX():
                    ckpt('s1')
                    for c in range(5):
                        bk, bkey = proj(QOFF + c * 128)
                        A_, ak = tf()
                        tten('dve', A_[:, :N], bk[:, :N], ropes[:, 0, :], ALU.mult, r=[bkey, 'ropes'], w=[ak])
                        B_, bkk = tb()
                        tten('dve', B_[:, :N], bk[:, :N], ropes[:, 1, :], ALU.mult, r=[bkey, 'ropes'], w=[bkk])
                        yield
                        pr, prk = bank()
                        mm(pr[:, :N], protb[:], B_[:, :N], True, True, r=['protb', bkk], w=[prk])
                        if c < 4:
                            tten('dve', qrTs[:, c, :], pr[:, :N], A_[:, :N], ALU.add, r=[prk, ak], w=['qrTs'])
                            yield
                        else:
                            tten('dve', krTs[:, :], pr[:, :N], A_[:, :N], ALU.add, r=[prk, ak], w=['krTs'])
                            ko, kok = tf()
                            tten('dve', ko[:, 0:N], pr[:, :N], A_[:, :N], ALU.add, r=[prk, ak], w=[kok])
                            out_evs.append(S.dma('sp', ks_new[l], ko[:, 0:N], r=[kok]))
                    ckpt('s2')
                    bk, bkey = bank()
                    for kc in range(8):
                        mm(bk[0:64, 0:128], hTs[:, kc, :], wbf[:, kc, VOFF:VOFF + 128], kc == 0, kc == 7, r=[wB, 'hTs'], w=[bkey])
                    act(vnew[0:64, :], bk[0:64, 0:128], AF.Copy, r=[bkey], w=['vnew'])
                    vo, vok = tf()
                    cpy('dve', vo[0:64, 0:128], bk[0:64, 0:128], r=[bkey], w=[vok])
                    out_evs.append(S.dma('sp', vs_new[l], vo[0:64, 0:128], r=[vok]))
                    yield
                    gzT, gzk = tf()
                    for g in range(4):
                        bz, bzk = proj(ZAOFF + g * 128)
                        th, thk = tf()
                        act(th[:, :N], bz[:, :N], AF.Tanh, r=[bzk], w=[thk], scale=0.5)
                        stt(gzT[:, g * 64:(g + 1) * 64], th[:, :N], 1.0, bz[:, :N], ALU.add, ALU.mult, r=[thk, bzk], w=[gzk + 'g%d' % g])
                        yield
                    gzks = [gzk + 'g%d' % g for g in range(4)]
                    if 3 in hooks:
                        hooks[3]()
                    ckpt('s3')
                    Ec, eck = tb()
                    for hk in range(2):
                        stc, stck = bank()
                        for b in range(NSB):
                            ov = stc[:, 0:256].rearrange("p (g c) -> p g c", g=4)[:, :, b * 4:(b + 1) * 4]
                            mm(ov, ckTb[hk * 64:(hk + 1) * 64, b, :], qrTs[hk * 64:(hk + 1) * 64, :, b * 4:(b + 1) * 4], True, True,
                               r=['ckTb', 'qrTs'], w=[stck])
                        act(Ec[:, hk * 256:(hk + 1) * 256], stc[:, 0:256], AF.Exp, r=[stck], w=[eck + 'h%d' % hk], scale=0.125)
                        yield
                    tten('pool', Ec[:, :], Ec[:, :], smask[:, 0, :], ALU.mult, r=[eck + 'h0', eck + 'h1', 'smask'], w=[eck])
                    En, enk = tb()
                    for hk in range(2):
                        stn, stnk = bank()
                        mm(stn[0:64, 0:256], krTs[hk * 64:(hk + 1) * 64, :], qrTs[hk * 64:(hk + 1) * 64, :, :], True, True,
                           r=['krTs', 'qrTs'], w=[stnk])
                        act(En[0:64, hk * 256:(hk + 1) * 256], stn[0:64, 0:256], AF.Exp, r=[stnk], w=[enk + 'h%d' % hk], scale=0.125)
                        yield
                    tten('pool', En[0:64, :], En[0:64, :], smask[0:64, 1, :], ALU.mult, r=[enk + 'h0', enk + 'h1', 'smask'], w=[enk])
                    dn, dnk = bank()
                    mm(dn[:, :], oneb1[0:64, :], En[0:64, :], True, False, r=['oneb1', enk], w=[dnk])
                    mm(dn[:, :], oneb1[:, :], Ec[:, :], False, True, r=['oneb1', eck], w=[dnk])
                    yield
                    ot, otk = bank()
                    for hk in range(2):
                        mm(ot[hk * 64:(hk + 1) * 64, 0:256], vnew[0:64, hk * 64:(hk + 1) * 64], En[0:64, hk * 256:(hk + 1) * 256], True, False,
                           r=['vnew', enk], w=[otk])
                        for b in range(NSB):
                            ov = ot[hk * 64:(hk + 1) * 64, 0:256].rearrange("p (g c) -> p g c", g=4)[:, :, b * 4:(b + 1) * 4]
                            ev = Ec[:, hk * 256:(hk + 1) * 256].rearrange("p (g c) -> p g c", g=4)[:, :, b * 4:(b + 1) * 4]
                            mm(ov, cvb[:, b, hk * 64:(hk + 1) * 64], ev, False, b == NSB - 1, r=['cvb', eck], w=[otk])
                    rd, rdk = tt()
                    yield
                    tten('dve', rd[:, 0:512], dn[:, :], esinkS[:, :], ALU.add, r=[dnk, 'esinkS'], w=[rdk])
                    ts('dve', rd[:, 0:512], rd[:, 0:512], 2.0, None, ALU.mult, None, r=[rdk], w=[rdk])
                    S.op('dve', lambda e: e.reciprocal(rd[:, 512:1024], rd[:, 0:512]), r=[rdk], w=[rdk + 'r'])
                    yield
                    for hk in range(2):
                        o32, o32k = tf()
                        tten('dve', o32[hk * 64:(hk + 1) * 64, 0:256], ot[hk * 64:(hk + 1) * 64, 0:256],
                             rd[hk * 64:(hk + 1) * 64, 512 + hk * 256: 512 + (hk + 1) * 256], ALU.mult, r=[otk, rdk + 'r'], w=[o32k])
                        tten('dve', mixTs[hk * 64:(hk + 1) * 64, 4:8, :], o32[hk * 64:(hk + 1) * 64, 0:256].rearrange("p (g c) -> p g c", g=4),
                             gzT[hk * 64:(hk + 1) * 64, 0:256].rearrange("p (g c) -> p g c", g=4), ALU.mult, r=[o32k] + gzks, w=['mixTsa'])
                    yield

                def chainY():
                    ckpt('s4')
                    pmean, pmk = pstatA, 'pstatA'
                    pex2, pek = pstatB, 'pstatB'
                    for c in range(2):
                        bu, buk = proj(UOFF + c * 128)
                        yield
                        bg, bgk = proj(GOFF + c * 128)
                        th, thk = tf()
                        act(th[:, :N], bg[:, :N], AF.Tanh, r=[bgk], w=[thk], scale=0.5)
                        a32, a32k = tf()
                        stt(a32[:, 0:N], th[:, :N], 1.0, bu[:, :N], ALU.add, ALU.mult, r=[thk, buk], w=[a32k])
                        cpy('dve', aTs[:, c, :, 30:34], a32[:, 0:N].rearrange("p (b t) -> p b t", b=NSB), r=[a32k], w=['aTs'])
                        yield
                        ts('dve', a32[:, 64:128], a32[:, 0:64], 0.5, None, ALU.mult, None, r=[a32k], w=[a32k + 'o'])
                        out_evs.append(S.dma('sp', convs_new[l, :, c, :, :], a32[:, 64:128].rearrange("p (b t) -> p b t", b=NSB), r=[a32k + 'o']))
                        bc, bck = bank()
                        for j in range(31):
                            mm(bc[:, :N].rearrange("p (b t) -> p b t", b=NSB), diag[:, c, j, :], aTs[:, c, :, j:j + 4], j == 0, j == 30,
                               r=['diagA', 'diagP', 'aTs'], w=[bck])
                        cbias = pvec[:, l, 62 + c:63 + c]
                        yield
                        ts('dve', cf[:, c, :N], bc[:, :N], cbias, None, ALU.add, None, r=[bck, 'pvec'], w=['cf%d' % c])
                        cb_, cbk = tb()
                        act(cb_[:, :N], bc[:, :N], AF.Identity, r=[bck, 'pvec'], w=[cbk], bias=cbias)
                        cq_, cqk = tb()
                        act(cq_[:, :N], bc[:, :N], AF.Square, r=[bck, 'pvec'], w=[cqk], bias=cbias)
                        mm(pmean[0:1, :N], onesb[:, 0:1], cb_[:, :N], c == 0, c == 1, r=['onesb', cbk], w=[pmk])
                        mm(pex2[0:1, :N], onesb[:, 0:1], cq_[:, :N], c == 0, c == 1, r=['onesb', cqk], w=[pek])
                        yield
                    msq, msk = tf()
                    act(msq[0:1, :N], pmean[0:1, :N], AF.Square, r=[pmk], w=[msk])
                    mrow, mrk = tf()
                    act(mrow[0:1, :N], pmean[0:1, :N], AF.Copy, r=[pmk], w=[mrk])
                    var, vk = tf()
                    tten('dve', var[0:1, :N], pex2[0:1, :N], msq[0:1, :N], ALU.subtract, r=[pek, msk], w=[vk])
                    ts('dve', var[0:1, :N], var[0:1, :N], LN_EPS, None, ALU.add, None, r=[vk], w=[vk])
                    rs, rsk = tf()
                    S.op('dve', lambda e: e.reciprocal(rs[0:1, :N], var[0:1, :N]), r=[vk], w=[rsk])
                    act(rs[0:1, :N], rs[0:1, :N], AF.Sqrt, r=[rsk], w=[rsk])
                    yield
                    bmean, bmk = bank()
                    mm(bmean[:, :N], ones32[0:1, :], mrow[0:1, :N], True, True, r=['ones32', mrk], w=[bmk])
                    brs, brk = bank()
                    mm(brs[:, :N], ones32[0:1, :], rs[0:1, :N], True, True, r=['ones32', rsk], w=[brk])
                    dts = []
                    for c in range(2):
                        d_, dk2 = tf()
                        tten('dve', d_[:, :N], cf[:, c, :N], bmean[:, :N], ALU.subtract, r=['cf%d' % c, bmk], w=[dk2])
                        tten('dve', d_[:, :N], d_[:, :N], brs[:, :N], ALU.mult, r=[dk2, brk], w=[dk2])
                        dts.append((d_, dk2))
                    yield
                    for c in range(2):
                        d_, dk2 = dts[c]
                        sg, sgk = tf()
                        act(sg[:, :N], d_[:, :N], AF.Tanh, r=[dk2, 'dvh'], w=[sgk],
                            scale=dvh[:, l, 64 + c:65 + c], bias=dvh[:, l, 66 + c:67 + c])
                        y_, yk = tf()
                        ts('dve', y_[:, :N], d_[:, :N], dvq[:, l, 64 + c:65 + c], dvq[:, l, 66 + c:67 + c], ALU.mult, ALU.add,
                           r=[dk2, 'dvq'], w=[yk])
                        stt(y_[:, :N], sg[:, :N], 1.0, y_[:, :N], ALU.add, ALU.mult, r=[sgk, yk], w=[yk])
                        yield
                        bz, bzk = proj(ZCOFF + c * 128)
                        yield
                        tz, tzk = tf()
                        act(tz[:, :N], bz[:, :N], AF.Tanh, r=[bzk], w=[tzk], scale=0.5)
                        stt(tz[:, :N], tz[:, :N], 1.0, bz[:, :N], ALU.add, ALU.mult, r=[tzk, bzk], w=[tzk])
                        tten('pool', mixTs[:, c, :], y_[:, :N], tz[:, :N], ALU.mult, r=[yk, tzk], w=['mixTsc'])
                    ckpt('s5')
                    for c in range(2):
                        bp, bpk = proj(PVOFF + c * 128)
                        act(pvTs[:, c, :, 15:19], bp[:, :N].rearrange("p (b t) -> p b t", b=NSB), AF.Copy, r=[bpk], w=['pvTs'])
                        out_evs.append(S.dma('sp', pools_new[l, :, c, :, :], pvTs[:, c, :, 15:19], r=['pvTs']))
                        yield
                    for c in range(2):
                        x = pvTs[:, c, :, :]
                        sw, swk = tf()
                        swv = sw[:, 0:NSB * 19].rearrange("p (b w) -> p b w", b=NSB)
                        t2, t2k = tf()
                        t2v = t2[:, 0:NSB * 19].rearrange("p (b w) -> p b w", b=NSB)
                        tten('pool', t2v[:, :, 1:19], x[:, :, 1:19], x[:, :, 0:18], ALU.add, r=['pvTs'], w=[t2k])
                        if c == 0:
                            tten('pool', swv[0:64, :, 15:19], x[0:64, :, 15:19], x[0:64, :, 14:18], ALU.add, r=['pvTs'], w=[swk + 'a'])
                            tten('pool', swv[64:128, :, 15:19], t2v[64:128, :, 15:19], t2v[64:128, :, 13:17], ALU.add, r=[t2k], w=[swk + 'b'])
                        else:
                            t4, t4k = tf()
                            t4v = t4[:, 0:NSB * 19].rearrange("p (b w) -> p b w", b=NSB)
                            tten('pool', t4v[:, :, 3:19], t2v[:, :, 3:19], t2v[:, :, 1:17], ALU.add, r=[t2k], w=[t4k])
                            tten('pool', swv[0:64, :, 15:19], t4v[0:64, :, 15:19], t4v[0:64, :, 11:15], ALU.add, r=[t4k], w=[swk + 'a'])
                            t8, t8k = tf()
                            t8v = t8[:, 0:NSB * 19].rearrange("p (b w) -> p b w", b=NSB)
                            tten('pool', t8v[64:128, :, 7:19], t4v[64:128, :, 7:19], t4v[64:128, :, 3:15], ALU.add, r=[t4k], w=[t8k])
                            tten('pool', swv[64:128, :, 15:19], t8v[64:128, :, 15:19], t8v[64:128, :, 7:11], ALU.add, r=[t8k], w=[swk + 'b'])
                        pl, plk = tb()
                        yield
                        stt(pl[:, 0:N].rearrange("p (b t) -> p b t", b=NSB), swv[:, :, 15:19], misc[:, 1 + c:2 + c], x[:, :, 15:19],
                            ALU.mult, ALU.subtract, r=[swk + 'a', swk + 'b', 'misc', 'pvTs'], w=[plk])
                        by, byk = bank()
                        mm(by[:, :N], poolbd[:, l, c, :], pl[:, :N], True, True, r=['poolbd', plk], w=[byk])
                        bz, bzk = proj(ZPOFF + c * 128)
                        yield
                        tz, tzk = tf()
                        act(tz[:, :N], bz[:, :N], AF.Tanh, r=[bzk], w=[tzk], scale=0.5)
                        stt(tz[:, :N], tz[:, :N], 1.0, bz[:, :N], ALU.add, ALU.mult, r=[tzk, bzk], w=[tzk])
                        stt(mixTs[:, 2 + c, :], by[:, :N], dvh[:, l, 68 + c:69 + c], tz[:, :N], ALU.mult, ALU.mult,
                            r=[byk, 'dvh', tzk], w=['mixTsp'])
                    if 6 in hooks:
                        hooks[6]()
                    yield

                run_interleaved([('X', chainX()), ('Y', chainY())])
                ckpt('s6')
                gs, gsk = tt()
                for hf in range(2):
                    bk, bkey = bank()
                    for c4 in range(4):
                        ch = hf * 4 + c4
                        L_, lk_ = tf()
                        cpy('dve', L_[:, 0:64].rearrange("p (b t) -> p b t", b=NSB),
                            modT[:, l, 16 + ch, 1:17].unsqueeze(2).to_broadcast([128, NSB, 4]), r=['modT'], w=[lk_])
                        mm(bk[0:64, c4 * 128:(c4 + 1) * 128], L_[:, 0:64], ident32[:], True, True, r=[lk_, 'ident32'], w=[bkey])
                    act(gs[0:64, hf * 512:(hf + 1) * 512], bk[0:64, :], AF.Copy, r=[bkey], w=[gsk + 'g%d' % hf])
                T, tk_ = tt()
                for hf in range(2):
                    bo, bok = bank()
                    for kc in range(8):
                        mm(bo[0:64, :], mixTs[:, kc, :], wobf[:, kc, hf * 512:(hf + 1) * 512], kc == 0, kc == 7, r=['mixTsa', 'mixTsc', 'mixTsp', 'wout'], w=[bok])
                    tten('dve', T[0:64, hf * 512:(hf + 1) * 512], bo[0:64, :], gs[0:64, hf * 512:(hf + 1) * 512], ALU.mult,
                         r=[bok, gsk + 'g%d' % hf], w=[tk_ + 'h%d' % hf])
                tten('pool', T[0:64, :], T[0:64, :], xsm[0:64, :], ALU.add, r=[tk_ + 'h0', tk_ + 'h1', 'xsm'], w=[tk_])
                rstd, nmr, lk = ln_stats(T, 64, EPS2, [tk_])
                act(T[0:64, :], T[0:64, :], AF.Identity, r=[tk_] + lk, w=[tk_], scale=rstd, bias=nmr)
                tten('pool', T[0:64, :], T[0:64, :], lng[0:64, 0, :], ALU.mult, r=[tk_, 'lng'], w=[tk_])
                tten('pool', xsm[0:64, :], T[0:64, :], lng[0:64, 1, :], ALU.add, r=[tk_, 'lng'], w=['xsm'])

            xblk = [(xo[i], ['xo%dL' % i, 'xo%dR' % i]) for i in range(4)]
            sched_pl = [(p, l) for p in passes for l in range(nl)]
            carry_s1 = None
            diag_prebuilt = False
            pending_tails = None
            load_w_in(sched_pl[0][1], 1)
            load_w_in(sched_pl[0][1], 0)
            load_w_out(sched_pl[0][1])
            for si, (p, l) in enumerate(sched_pl):
                nxt = sched_pl[si + 1][1] if si + 1 < len(sched_pl) else None
                hooks = {}
                if nxt is not None:
                    hooks = {3: (lambda n=nxt: load_w_in(n, 1)), 6: (lambda n=nxt: load_w_in(n, 0))}
                if l == 0:
                    if p == 'H':
                        for i in range(4):
                            S.dma('sp', xo[i][:], xin_d[i * 128:(i + 1) * 128, :], w=['xo%dL' % i, 'xo%dR' % i])
                        S.dma('sp', xsm[0:64, :], xs_d, w=['xsm'])
                        S.dma('sp', [ropes[:, 0, :], ropes[:, 1, :]], [ropeCs_d, ropeSs_d], w=['ropes'])
                    else:
                        r0 = 512 + p * 512
                        for i in range(4):
                            S.dma('sp', xo[i][:], xin_d[r0 + i * 128: r0 + (i + 1) * 128, :], w=['xo%dL' % i, 'xo%dR' % i])
                ckpt('wload')
                layer_setup(l, p == 'H', build_diag=not diag_prebuilt, load_lng_now=(pending_tails is None))
                lng_loaded = pending_tails is None
                diag_prebuilt = False
                ckpt('setup')
                if p == 'H':
                    blist = list(range(l, 4))
                    groups = []
                    while blist:
                        take = 2 if len(blist) % 2 == 0 else 1
                        groups.append(blist[:take]); blist = blist[take:]
                    glist = [dict(blocks=[xblk[i] for i in gb], tcol=gb[0] * 128, first=False, last=False, hooks=None,
                                  partial=(len(gb) == 1 and gb[0] == l)) for gb in groups]
                else:
                    glist = [dict(blocks=[xblk[2 * gi], xblk[2 * gi + 1]], tcol=512 + p * 512 + gi * 256,
                                  first=(p == 0 and gi == 0), last=(p == 3 and gi == 1), hooks=hooks if gi == 1 else None)
                             for gi in range(2)]
                for gi, g in enumerate(glist):
                    nx = None
                    if gi + 1 < len(glist):
                        nx = (l, glist[gi + 1]['blocks'])
                    elif p != 'H' and l + 1 < nl:
                        nx = (l + 1, [xblk[0], xblk[1]])
                    nxt_gen = make_s1(*nx) if nx is not None else None
                    dn_ = None
                    if p != 'H' and gi == len(glist) - 1 and nxt is not None:
                        dn_ = nxt
                        diag_prebuilt = True
                    def _after_xy(l_=l):
                        load_lng(l_)
                    need_lng = not lng_loaded
                    lng_loaded = True
                    defer_ = (p != 'H') and not (l == nl - 1 and gi == len(glist) - 1)
                    pending_tails = prompt_group(l, g['blocks'], g['tcol'], g['first'], g['last'], hooks=g['hooks'],
                                                 s1_done=carry_s1, next_s1=nxt_gen, diag_next=dn_,
                                                 pending=pending_tails, defer=defer_,
                                                 after_xy=_after_xy if need_lng else None, partial=g.get('partial', False))
                    carry_s1 = True if nx is not None else None
                if p == 'H':
                    ts('pool', cst[:, l, :, :], cst[:, l, :, :], misc[:, 0:1], 1.0, ALU.mult, ALU.mult, r=['cst%d' % l, 'misc'], w=['cst%d' % l])
                    ts('pool', pst[:, l, :, :], pst[:, l, :, :], misc[:, 0:1], 1.0, ALU.mult, ALU.mult, r=['pst%d' % l, 'misc'], w=['pst%d' % l])
                    sample_group(l, hooks)
                if nxt is not None:
                    load_w_out(nxt)
                if l == nl - 1:
                    if p == 'H':
                        out_evs.append(S.dma('sp', ys_d, xsm[0:64, :], r=['xsm']))
                    else:
                        for i in range(4):
                            out_evs.append(S.dma('sp', yp_d[p * 512 + i * 128: p * 512 + (i + 1) * 128, :], xo[i][:], r=['xo%dL' % i, 'xo%dR' % i]))
        except _Stop:
            pass
        S.finish(out_evs)
        stats = dict(ops=S.nops, waits=S.nwaits, same=S.nsame, skipped=S.nskip)
    return nc, stats


def _rope_tables(pos):
    half = 32
    inv_freq = (np.float32(10000.0) ** (-np.arange(half, dtype=np.float32) * np.float32(2.0 / 64))).astype(np.float32)
    ang = pos.astype(np.float32)[None, :] * inv_freq[:, None]
    c = np.cos(ang).astype(np.float32)
    s = np.sin(ang).astype(np.float32)
    return np.tile(c, (4, 1)), np.tile(s, (4, 1))


def make_in_maps(inp):
    f = np.float32
    perm = np.array([(hkv * 4 + g) * 64 + d for g in range(4) for hkv in range(2) for d in range(64)])
    w_in = np.ascontiguousarray(inp['w_in'], dtype=f).copy()
    w_in[:, :, 1280:1792] = inp['w_in'][:, :, 1280 + perm]
    w_in[:, :, 2048:2560] = inp['w_in'][:, :, 2048 + perm]
    w_out = np.ascontiguousarray(inp['w_out'], dtype=f).copy()
    w_out[:, 512:1024, :] = inp['w_out'][:, 512 + perm, :]
    w_mod = np.ascontiguousarray(inp['w_mod'], dtype=f)
    bmodT = np.ascontiguousarray(inp['b_mod'].reshape(NL, 24, 128).transpose(2, 0, 1), dtype=f)
    pvec = np.zeros((128, NL, 70), f)
    cw = inp['conv_w'].reshape(NL, 31, 2, 128)
    pvec[:, :, 0:62] = cw.transpose(3, 0, 2, 1).reshape(128, NL, 62)
    for i, nm in enumerate(['conv_b', 'cnorm_g', 'cnorm_b', 'pool_scale']):
        pvec[:, :, 62 + 2 * i: 64 + 2 * i] = inp[nm].reshape(NL, 2, 128).transpose(2, 0, 1)
    poolbd = np.zeros((128, NL, 2, 128), f)
    for c in range(2):
        for gl in range(2):
            poolbd[gl * 64:(gl + 1) * 64, :, c, gl * 64:(gl + 1) * 64] = inp['pool_w'][:, 2 * c + gl].transpose(1, 0, 2)
    lngb = np.ascontiguousarray(np.stack([inp['ln_g'], inp['ln_b']], axis=1), dtype=f)
    sinks = np.ascontiguousarray(inp['sinks'].reshape(1, NL * 8), dtype=f)
    ident = np.eye(128, dtype=f)
    prot = np.zeros((128, 128), f)
    for m in range(128):
        if m % 64 < 32:
            prot[m + 32, m] = -1.0
        else:
            prot[m - 32, m] = 1.0
    kk = np.arange(128)[:, None]
    qq = np.arange(128)[None, :]
    m_own = np.tile((qq >= kk).astype(f), (1, 4))
    m_prev = np.tile((kk > qq).astype(f), (1, 4))
    col_t = np.tile(np.arange(4), 128)[None, :]
    col_b = np.tile(np.repeat(np.arange(16), 4), 8)[None, :]
    smask = np.zeros((128, 2, 512), f)
    smask[:, 0, :] = (np.arange(128)[:, None] > col_t).astype(f)
    rb = np.repeat(np.arange(16), 4)[:, None]
    rj = np.tile(np.arange(4), 16)[:, None]
    smask[0:64, 1, :] = ((rb == col_b) & (rj <= col_t)).astype(f)
    ropeCs, ropeSs = _rope_tables(np.tile(PAST_LEN + np.arange(4), 16))
    wch = np.array([[2, 4], [8, 16]])
    in_maps = []
    for c in range(NCORE):
        bi, hf = c // 2, c % 2
        xin = np.zeros((2560, D), f)
        if hf == 1:
            xin[:] = inp['x_prompt'][bi, 2048 - 512:4096]
        else:
            xin[512:] = inp['x_prompt'][bi, 0:2048]
        pos = hf * 2048 - 512 + np.arange(2560)
        ropeC, ropeS = _rope_tables(pos)
        masks = (np.stack([m_own, m_prev, m_prev * f(hf)], axis=1) - f(1.0)) * f(240000.0)
        misc = np.zeros((128, 36), f)
        misc[:, 0] = hf
        for ch in range(2):
            for ph in range(2):
                w = wch[ch, ph]
                misc[ph * 64:(ph + 1) * 64, 1 + ch] = 1.0 / w
                misc[ph * 64:(ph + 1) * 64, 4 + ch * 16: 4 + (ch + 1) * 16] = (
                    1.0 if hf == 1 else (w / np.minimum(w, np.arange(16) + 1.0))[None, :])
        sb_ = slice(c * NSB, (c + 1) * NSB)
        crows = np.concatenate([inp['c_prompt'][bi:bi + 1], inp['c_sample'][sb_]], axis=0)
        cT = np.ascontiguousarray(crows.reshape(17, 8, 128).transpose(2, 1, 0), dtype=f)
        cc = inp['cache_conv'][:, sb_]
        cpl = inp['cache_pool'][:, sb_]
        ck = inp['cache_k'][:, sb_].reshape(NL, NSB, 128, 128)
        cv = inp['cache_v'][:, sb_].reshape(NL, NSB, 128, 128)
        m = {
            'xin': xin, 'xsin': np.ascontiguousarray(inp['x_sample'][sb_].reshape(NS, D), dtype=f),
            'w_in': w_in, 'w_out': w_out, 'w_mod': w_mod, 'cT': cT, 'bmodT': bmodT, 'pvec': pvec, 'poolbd': poolbd,
            'lngb': lngb, 'sinks': sinks, 'ident': ident, 'prot': prot, 'masks': np.ascontiguousarray(masks),
            'smask': smask, 'ropeC': ropeC, 'ropeS': ropeS, 'ropeCs': ropeCs, 'ropeSs': ropeSs, 'misc': misc,
            'cconvT': np.ascontiguousarray(cc.reshape(NL, NSB, 30, 2, 128).transpose(0, 4, 3, 1, 2), dtype=f),
            'cpoolT': np.ascontiguousarray(cpl.reshape(NL, NSB, 15, 2, 128).transpose(0, 4, 3, 1, 2), dtype=f),
            'ckT': np.ascontiguousarray(ck.transpose(0, 3, 1, 2), dtype=f),
            'cv': np.ascontiguousarray(cv.transpose(0, 2, 1, 3), dtype=f),
            'cconv_n': np.ascontiguousarray(cc, dtype=f), 'cpool_n': np.ascontiguousarray(cpl, dtype=f),
            'ck_n': np.ascontiguousarray(ck, dtype=f), 'cv_n': np.ascontiguousarray(cv, dtype=f),
        }
        in_maps.append(m)
    return in_maps


def assemble(res):
    f = np.float32
    B, SEQ = 4, 4096
    y_p = np.zeros((B, SEQ, D), f)
    y_s = np.zeros((128, 4, D), f)
    conv_p = np.zeros((NL, B, 30, 256), f)
    pool_p = np.zeros((NL, B, 15, 256), f)
    k_p = np.zeros((NL, B, 128, 2, 64), f)
    v_p = np.zeros((NL, B, 128, 2, 64), f)
    conv_s = np.zeros((NL, 128, 30, 256), f)
    pool_s = np.zeros((NL, 128, 15, 256), f)
    k_s = np.zeros((NL, 128, 128, 2, 64), f)
    v_s = np.zeros((NL, 128, 128, 2, 64), f)
    for c in range(NCORE):
        r = res[c]
        bi, hf = c // 2, c % 2
        y_p[bi, hf * 2048:(hf + 1) * 2048] = r['y_p']
        sb_ = slice(c * NSB, (c + 1) * NSB)
        y_s[sb_] = r['y_s'].reshape(NSB, 4, D)
        if hf == 1:
            conv_p[:, bi] = r['convp'].transpose(0, 3, 2, 1).reshape(NL, 30, 256)
            pool_p[:, bi] = r['poolp'].transpose(0, 3, 2, 1).reshape(NL, 15, 256)
            k_p[:, bi] = r['kp'].transpose(0, 2, 1).reshape(NL, 128, 2, 64)
            v_p[:, bi] = r['vp'].reshape(NL, 128, 2, 64)
        conv_s[:, sb_, 0:26] = r['convs_old']
        conv_s[:, sb_, 26:30] = r['convs_new'].transpose(0, 3, 4, 2, 1).reshape(NL, NSB, 4, 256)
        pool_s[:, sb_, 0:11] = r['pools_old']
        pool_s[:, sb_, 11:15] = r['pools_new'].transpose(0, 3, 4, 2, 1).reshape(NL, NSB, 4, 256)
        k_s[:, sb_, 0:124] = r['ks_old'].reshape(NL, NSB, 124, 2, 64)
        k_s[:, sb_, 124:128] = r['ks_new'].transpose(0, 2, 1).reshape(NL, NSB, 4, 2, 64)
        v_s[:, sb_, 0:124] = r['vs_old'].reshape(NL, NSB, 124, 2, 64)
        v_s[:, sb_, 124:128] = r['vs_new'].reshape(NL, NSB, 4, 2, 64)
    return (y_p, y_s, conv_p, pool_p, k_p, v_p, conv_s, pool_s, k_s, v_s)


def kernel(**inputs):
    inp = {k: np.asarray(v) for k, v in inputs.items()}
    in_maps = make_in_maps(inp)
    nc, _ = build_program()
    res = run_bass_kernel_spmd(nc, in_maps, core_ids=list(range(NCORE)))
    return assemble(res.results)
```
